# Optimizing a Trainium2 kernel written in Bass

```python
import jax, jax.numpy as jnp
from jax import lax
import numpy as np

D_MODEL = 1024
BATCH = 4
SEQ = 4096
DEPTH = 2
DEC_BATCH = 32
DEC_SEQ = 8
PAST_LEN = 8192
PAGE_SIZE = 128

D_FF = 2816
PLE_DIM = 256
RMS_EPS = 1e-6
RWKV_HEAD_DIM = 64
RWKV_HEADS = D_MODEL // RWKV_HEAD_DIM
DECAY_LORA = 64
ICLR_LORA = 64
GATE_LORA = 160
LNX_EPS = 64e-5
ATTN_HEAD_DIM = 64
ATTN_HEADS = D_MODEL // ATTN_HEAD_DIM
KV_HEADS = 4
Q_PER_KV = ATTN_HEADS // KV_HEADS
DILATION_GROUPS = ((128, 1), (512, 4), (2048, 16))
N_GROUPS = len(DILATION_GROUPS)
MAX_WINDOW = max(w for w, _ in DILATION_GROUPS)
Q_BLOCK = 128
REL_BUCKETS = 32
REL_MAX_DIST = 2048
NEG_INF = -1e30

kernel_name = 'yoco_rwkv7_dilated_window_step'


def rms_norm(x, g):
    xf = x.astype(jnp.float32)
    y = xf * lax.rsqrt(jnp.mean(xf * xf, axis=-1, keepdims=True) + RMS_EPS)
    return (y * g.astype(jnp.float32)).astype(x.dtype)


def swiglu(x, wi, wo):
    gate, up = jnp.split(x @ wi, 2, axis=-1)
    return (jax.nn.silu(gate) * up) @ wo


def t5_buckets(dist):
    d = np.asarray(dist, dtype=np.int64)
    max_exact = REL_BUCKETS // 2
    large = max_exact + (np.log(np.maximum(d, 1) / max_exact) / np.log(REL_MAX_DIST / max_exact)
                         * (REL_BUCKETS - max_exact)).astype(np.int32)
    large = np.minimum(large, REL_BUCKETS - 1)
    return np.where(d < max_exact, d, large).astype(np.int32)


def rwkv7_time_mix(xn, shift_prev, s0, mix, w_rkv, w_o, w0, w1, w2, a0, a1, a2, g1, g2,
                   k_k, k_a, r_k, lnx_w, lnx_b):
    f32 = jnp.float32
    B, T, D = xn.shape
    H, N = RWKV_HEADS, RWKV_HEAD_DIM
    x_prev = jnp.concatenate([shift_prev[:, None].astype(xn.dtype), xn[:, :-1]], axis=1)
    xx = x_prev - xn
    xr, xw, xk, xv, xa, xg = (xn + xx * mix[j] for j in range(6))
    r = (xr @ w_rkv[0]).astype(f32).reshape(B, T, H, N)
    k = (xk @ w_rkv[1]).astype(f32)
    v = (xv @ w_rkv[2]).astype(f32).reshape(B, T, H, N)
    w_log = -jax.nn.softplus(-(w0 + jnp.tanh(xw @ w1) @ w2).astype(f32)) - 0.5
    decay = jnp.exp(-jnp.exp(w_log)).reshape(B, T, H, N)
    a = jax.nn.sigmoid((a0 + (xa @ a1) @ a2).astype(f32))
    g = jax.nn.sigmoid(xg @ g1) @ g2
    kk = (k * k_k.astype(f32)).reshape(B, T, H, N)
    kk = kk / jnp.maximum(jnp.sqrt(jnp.sum(kk * kk, -1, keepdims=True)), 1e-12)
    k = (k * (1.0 + (a - 1.0) * k_a.astype(f32))).reshape(B, T, H, N)
    b = kk * a.reshape(B, T, H, N)

    def step(S, inp):
        r_t, w_t, k_t, v_t, kk_t, b_t = inp
        sa = jnp.einsum('bhij,bhj->bhi', S, -kk_t)
        S = S * w_t[:, :, None, :] + sa[..., None] * b_t[:, :, None, :] + v_t[..., None] * k_t[:, :, None, :]
        return S, jnp.einsum('bhij,bhj->bhi', S, r_t)

    xs = tuple(jnp.moveaxis(t, 1, 0) for t in (r, decay, k, v, kk, b))
    s_T, ys = lax.scan(step, s0.astype(f32), xs)
    y = jnp.moveaxis(ys, 0, 1)
    mu = jnp.mean(y, -1, keepdims=True)
    var = jnp.mean(jnp.square(y - mu), -1, keepdims=True)
    y = ((y - mu) * lax.rsqrt(var + LNX_EPS)).reshape(B, T, D) * lnx_w.astype(f32) + lnx_b.astype(f32)
    y = y + (jnp.sum(r * k * r_k.astype(f32), -1, keepdims=True) * v).reshape(B, T, D)
    return (y.astype(xn.dtype) * g) @ w_o, s_T, xn[:, -1]


def dilated_window_attention(q, k_ext, v_ext, valid_ext, rel_bias):
    f32 = jnp.float32
    B, T = q.shape[:2]
    qb = Q_BLOCK if T % Q_BLOCK == 0 else T
    n_blk = T // qb
    idxs, biases = [], []
    for g, (win, dil) in enumerate(DILATION_GROUPS):
        offs = dil * np.arange(win // dil + 1)
        idxs.append(MAX_WINDOW + np.arange(qb)[:, None] - offs[None, :])
        bias = rel_bias[t5_buckets(offs)][:, g * ATTN_HEADS:(g + 1) * ATTN_HEADS].astype(f32)
        biases.append(bias.T.reshape(KV_HEADS, Q_PER_KV, -1))

    def block(bi):
        start = bi * qb
        q_blk = lax.dynamic_slice_in_dim(q, start, qb, axis=1)
        k_blk = lax.dynamic_slice_in_dim(k_ext, start, MAX_WINDOW + qb, axis=1)
        v_blk = lax.dynamic_slice_in_dim(v_ext, start, MAX_WINDOW + qb, axis=1)
        m_blk = lax.dynamic_slice_in_dim(valid_ext, start, MAX_WINDOW + qb, axis=0)
        outs, lses = [], []
        for g in range(N_GROUPS):
            idx = idxs[g]
            kg, vg = k_blk[:, idx], v_blk[:, idx]
            logits = jnp.einsum('bqhrd,bqjhd->bqhrj', q_blk[:, :, g], kg, preferred_element_type=f32) + biases[g]
            logits = jnp.where(m_blk[idx][None, :, None, None, :], logits, NEG_INF)
            mx = jnp.max(logits, -1, keepdims=True)
            pr = jnp.exp(logits - mx)
            den = jnp.sum(pr, -1, keepdims=True)
            outs.append(jnp.einsum('bqhrj,bqjhd->bqhrd', pr, vg, preferred_element_type=f32) / den)
            lses.append(mx + jnp.log(den))
        wts = jax.nn.softmax(jnp.stack(lses, 0), axis=0)
        return sum(wts[g] * outs[g] for g in range(N_GROUPS))

    out = lax.map(block, jnp.arange(n_blk))
    return jnp.moveaxis(out, 0, 1).reshape(B, T, ATTN_HEADS * ATTN_HEAD_DIM)


def trunk(x, p, wkv0, shift0, kv_past, w):
    B, T, _ = x.shape
    n_a = DEPTH // 2
    h = x
    wkv_out, shift_out = [], []
    kv_rows = kv_ext = valid_ext = None
    for i in range(DEPTH):
        if i == n_a:
            kv_rows = (rms_norm(h, w['kv_norm']) @ w['w_kv']).reshape(B, T, 2, KV_HEADS, ATTN_HEAD_DIM)
            pad = MAX_WINDOW - kv_past.shape[1]
            kv_ext = jnp.concatenate([jnp.zeros((B, pad, 2, KV_HEADS, ATTN_HEAD_DIM), h.dtype),
                                      kv_past.astype(h.dtype), kv_rows], axis=1)
            valid_ext = jnp.asarray(np.arange(MAX_WINDOW + T) >= pad)
        h = h + 0.5 * swiglu(rms_norm(h, w['norm_w'][i, 0]), w['ffn1_wi'][i], w['ffn1_wo'][i])
        hn = rms_norm(h, w['norm_w'][i, 1])
        if i < n_a:
            mix, s_new, sh_new = rwkv7_time_mix(
                hn, shift0[i], wkv0[i], w['rwkv_mix'][i], w['rwkv_wrkv'][i], w['rwkv_wo'][i],
                w['rwkv_w0'][i], w['rwkv_w1'][i], w['rwkv_w2'][i], w['rwkv_a0'][i], w['rwkv_a1'][i],
                w['rwkv_a2'][i], w['rwkv_g1'][i], w['rwkv_g2'][i], w['rwkv_kk'][i], w['rwkv_ka'][i],
                w['rwkv_rk'][i], w['rwkv_lnx_w'][i], w['rwkv_lnx_b'][i])
            wkv_out.append(s_new.astype(x.dtype))
            shift_out.append(sh_new)
        else:
            j = i - n_a
            q = (hn @ w['attn_wq'][j]).reshape(B, T, N_GROUPS, KV_HEADS, Q_PER_KV, ATTN_HEAD_DIM) * (ATTN_HEAD_DIM ** -0.5)
            att = dilated_window_attention(q, kv_ext[:, :, 0], kv_ext[:, :, 1], valid_ext, w['rel_bias'])
            mix = att.astype(h.dtype) @ w['attn_wo'][j]
        h = h + mix
        h = h + 0.5 * swiglu(rms_norm(h, w['norm_w'][i, 2]), w['ffn2_wi'][i], w['ffn2_wo'][i])
        h = h + jax.nn.sigmoid(rms_norm(h, w['norm_w'][i, 3]) @ w['pe_gate'][i]) * (p[i] @ w['pe_proj'][i])
    return rms_norm(h, w['final_norm']), jnp.stack(wkv_out), jnp.stack(shift_out), kv_rows


def setup_inputs(seed: int = 0) -> dict:
    key = jax.random.key(seed)
    ks = iter(jax.random.split(key, 40))
    def nrm(shape, scale=1.0):
        return scale * jax.random.normal(next(ks), shape, jnp.float32)
    def unif(shape, lo, hi):
        return jax.random.uniform(next(ks), shape, jnp.float32, minval=lo, maxval=hi)
    n_a = DEPTH // 2
    n_b = DEPTH - n_a
    H, N = RWKV_HEADS, RWKV_HEAD_DIM
    kv_len = min(MAX_WINDOW, PAST_LEN)
    D = D_MODEL
    return {
        'x_prompt': nrm((BATCH, SEQ, D)),
        'x_sample': nrm((DEC_BATCH, DEC_SEQ, D)),
        'state_wkv': nrm((n_a, DEC_BATCH, H, N, N), 0.3),
        'state_shift': nrm((n_a, DEC_BATCH, D)),
        'cache_kv': nrm((DEC_BATCH, kv_len, 2, KV_HEADS, ATTN_HEAD_DIM)),
        'p_prompt': nrm((DEPTH, BATCH, SEQ, PLE_DIM)),
        'p_sample': nrm((DEPTH, DEC_BATCH, DEC_SEQ, PLE_DIM)),
        'norm_w': 1.0 + nrm((DEPTH, 4, D), 0.1),
        'ffn1_wi': nrm((DEPTH, D, 2 * D_FF), D ** -0.5),
        'ffn1_wo': nrm((DEPTH, D_FF, D), D_FF ** -0.5),
        'ffn2_wi': nrm((DEPTH, D, 2 * D_FF), D ** -0.5),
        'ffn2_wo': nrm((DEPTH, D_FF, D), D_FF ** -0.5),
        'pe_proj': nrm((DEPTH, PLE_DIM, D), PLE_DIM ** -0.5),
        'pe_gate': nrm((DEPTH, D, D), D ** -0.5),
        'rwkv_mix': unif((n_a, 6, D), 0.0, 1.0),
        'rwkv_wrkv': nrm((n_a, 3, D, D), D ** -0.5),
        'rwkv_wo': nrm((n_a, D, D), D ** -0.5),
        'rwkv_w0': unif((n_a, D), -4.0, 0.0),
        'rwkv_w1': nrm((n_a, D, DECAY_LORA), D ** -0.5),
        'rwkv_w2': nrm((n_a, DECAY_LORA, D), 0.1 * DECAY_LORA ** -0.5),
        'rwkv_a0': nrm((n_a, D), 0.1),
        'rwkv_a1': nrm((n_a, D, ICLR_LORA), D ** -0.5),
        'rwkv_a2': nrm((n_a, ICLR_LORA, D), 0.3 * ICLR_LORA ** -0.5),
        'rwkv_g1': nrm((n_a, D, GATE_LORA), D ** -0.5),
        'rwkv_g2': nrm((n_a, GATE_LORA, D), GATE_LORA ** -0.5),
        'rwkv_kk': 0.85 + nrm((n_a, D), 0.05),
        'rwkv_ka': 1.0 + nrm((n_a, D), 0.05),
        'rwkv_rk': nrm((n_a, H, N), 0.1),
        'rwkv_lnx_w': 1.0 + nrm((n_a, D), 0.1),
        'rwkv_lnx_b': nrm((n_a, D), 0.01),
        'attn_wq': nrm((n_b, D, N_GROUPS * ATTN_HEADS * ATTN_HEAD_DIM), D ** -0.5),
        'attn_wo': nrm((n_b, ATTN_HEADS * ATTN_HEAD_DIM, D), (ATTN_HEADS * ATTN_HEAD_DIM) ** -0.5),
        'kv_norm': 1.0 + nrm((D,), 0.1),
        'w_kv': nrm((D, 2 * KV_HEADS * ATTN_HEAD_DIM), D ** -0.5),
        'rel_bias': nrm((REL_BUCKETS, N_GROUPS * ATTN_HEADS), 0.5),
        'final_norm': 1.0 + nrm((D,), 0.1),
    }


def reference(x_prompt, x_sample, state_wkv, state_shift, cache_kv, p_prompt, p_sample,
              norm_w, ffn1_wi, ffn1_wo, ffn2_wi, ffn2_wo, pe_proj, pe_gate,
              rwkv_mix, rwkv_wrkv, rwkv_wo, rwkv_w0, rwkv_w1, rwkv_w2, rwkv_a0, rwkv_a1, rwkv_a2,
              rwkv_g1, rwkv_g2, rwkv_kk, rwkv_ka, rwkv_rk, rwkv_lnx_w, rwkv_lnx_b,
              attn_wq, attn_wo, kv_norm, w_kv, rel_bias, final_norm):
    w = dict(norm_w=norm_w, ffn1_wi=ffn1_wi, ffn1_wo=ffn1_wo, ffn2_wi=ffn2_wi, ffn2_wo=ffn2_wo,
             pe_proj=pe_proj, pe_gate=pe_gate, rwkv_mix=rwkv_mix, rwkv_wrkv=rwkv_wrkv, rwkv_wo=rwkv_wo,
             rwkv_w0=rwkv_w0, rwkv_w1=rwkv_w1, rwkv_w2=rwkv_w2, rwkv_a0=rwkv_a0, rwkv_a1=rwkv_a1,
             rwkv_a2=rwkv_a2, rwkv_g1=rwkv_g1, rwkv_g2=rwkv_g2, rwkv_kk=rwkv_kk, rwkv_ka=rwkv_ka,
             rwkv_rk=rwkv_rk, rwkv_lnx_w=rwkv_lnx_w, rwkv_lnx_b=rwkv_lnx_b, attn_wq=attn_wq,
             attn_wo=attn_wo, kv_norm=kv_norm, w_kv=w_kv, rel_bias=rel_bias, final_norm=final_norm)
    n_a = DEPTH // 2
    B, T, _ = x_prompt.shape
    wkv0 = jnp.zeros((n_a, B, RWKV_HEADS, RWKV_HEAD_DIM, RWKV_HEAD_DIM), jnp.float32)
    shift0 = jnp.zeros((n_a, B, D_MODEL), x_prompt.dtype)
    kv0 = jnp.zeros((B, 0, 2, KV_HEADS, ATTN_HEAD_DIM), x_prompt.dtype)
    y_prompt, wkv_prompt, shift_prompt, kv_rows_p = trunk(x_prompt, p_prompt, wkv0, shift0, kv0, w)
    kv_prompt = kv_rows_p[:, T - min(MAX_WINDOW, T):]
    y_sample, wkv_sample, shift_sample, kv_sample = trunk(x_sample, p_sample, state_wkv, state_shift, cache_kv, w)
    return (y_prompt, y_sample, wkv_prompt, shift_prompt, kv_prompt, wkv_sample, shift_sample, kv_sample)
```

```python
from contextlib import ExitStack
import numpy as np
import ml_dtypes
import concourse.bass as bass
import concourse.mybir as mybir
from concourse.bass_utils import run_bass_kernel_spmd

F32 = mybir.dt.float32
BF16 = mybir.dt.bfloat16
ALU = mybir.AluOpType
AF = mybir.ActivationFunctionType
AX = mybir.AxisListType

D = 1024
DFF = 2816
NFC = DFF // 128
H = 16
NSB = 8
MAXW = 2048
NEG = -1e30
GROUPS = ((128, 1), (512, 4), (2048, 16))


class Res:
    __slots__ = ("name", "w", "r", "psum")

    def __init__(self, name, psum=False):
        self.name = name
        self.w = {}
        self.r = {}
        self.psum = psum


class T:
    def __init__(self, name, ap):
        self.name = name
        self.ap = ap
        self.res = Res(name)
        self._parts = {}

    def part(self, key):
        if key not in self._parts:
            self._parts[key] = Res(f"{self.name}.{key}", self.res.psum)
        return self._parts[key]

    def __getitem__(self, k):
        return self.ap[k]


def _res(x):
    return x.res if isinstance(x, T) else x


class Sched:
    COMPUTE = ("pe", "dve", "act", "pool")
    NDMASEM = 8

    def __init__(self, nc):
        self.nc = nc
        self.gstack = ExitStack()
        self.stack = self.gstack
        self.streams = {k: [] for k in ("pe", "dve", "act", "pool", "sp")}
        self.sems = {}
        self.cnt = {}
        for k in self.COMPUTE:
            self.sems[k] = self.gstack.enter_context(nc.semaphore(f"s_{k}"))
            self.cnt[k] = 0
        self.dq = {}
        for q in ("sp", "pool"):
            sl = []
            for i in range(self.NDMASEM):
                key = f"d_{q}{i}"
                self.sems[key] = self.gstack.enter_context(nc.semaphore(key))
                self.cnt[key] = 0
                sl.append(key)
            self.dq[q] = [sl, 0]
        self.known = {k: {} for k in self.streams}
        self.n_ops = 0
        self.uid = 0

    def sb(self, name, shape, dtype):
        self.uid += 1
        h = self.stack.enter_context(self.nc.sbuf_tensor(f"{name}_{self.uid}", list(shape), dtype))
        return T(name, h[:])

    def ps(self, name, shape, dtype=F32):
        self.uid += 1
        h = self.stack.enter_context(self.nc.psum_tensor(f"{name}_{self.uid}", list(shape), dtype))
        t = T(name, h[:])
        t.res.psum = True
        return t

    def dram(self, name, shape, dtype, kind="Internal"):
        h = self.nc.dram_tensor(name, list(shape), dtype, kind=kind)
        return T(name, h.ap())

    def _collect(self, ekey, reads, writes, is_dma=False):
        deps = {}

        def add(d, same_ok):
            for s, v in d.items():
                if s == ekey and not (same_ok or is_dma):
                    continue
                if deps.get(s, 0) < v:
                    deps[s] = v

        for r in reads:
            add(_res(r).w, same_ok=(ekey != "pe"))
            if _res(r).psum:
                add(_res(r).r, same_ok=False)
        for w in writes:
            add(_res(w).w, same_ok=False)
            add(_res(w).r, same_ok=False)
        kn = self.known[ekey]
        out = []
        for s, v in deps.items():
            if kn.get(s, 0) < v:
                kn[s] = v
                out.append((s, v))
        return out

    def op(self, ekey, fn, reads=(), writes=()):
        waits = self._collect(ekey, reads, writes)
        self.cnt[ekey] += 1
        v = self.cnt[ekey]
        self.streams[ekey].append((waits, fn, (ekey, 1)))
        for r in reads:
            _res(r).r[ekey] = v
        for w in writes:
            _res(w).w[ekey] = v
        self.n_ops += 1

    def dma(self, q, out_ap, in_ap, reads=(), writes=(), **kw):
        sl, i = self.dq[q]
        key = sl[i % self.NDMASEM]
        self.dq[q][1] = i + 1
        waits = self._collect(q, reads, writes, is_dma=True)
        prev = self.cnt[key]
        kn = self.known[q]
        if prev > 0 and kn.get(key, 0) < prev:
            kn[key] = prev
            waits.append((key, prev))
        val = prev + 16
        self.cnt[key] = val

        def fn(e, out_ap=out_ap, in_ap=in_ap, kw=kw):
            return e.dma_start(out=out_ap, in_=in_ap, **kw)

        self.streams[q].append((waits, fn, (key, 16)))
        for r in reads:
            _res(r).r[key] = val
        for w in writes:
            _res(w).w[key] = val
        self.n_ops += 1

    def barrier(self):
        for ekey in self.streams:
            kn = self.known[ekey]
            waits = []
            for s, v in self.cnt.items():
                if s == ekey or v == 0:
                    continue
                if kn.get(s, 0) < v:
                    kn[s] = v
                    waits.append((s, v))
            self.streams[ekey].append((waits, None, None))

    def emit(self):
        nc = self.nc
        sems = self.sems
        streams = self.streams
        with nc.Block() as block:
            def mk(ekey):
                def body(e):
                    for waits, fn, inc in streams[ekey]:
                        for s, v in waits:
                            e.wait_ge(sems[s], v)
                        if fn is not None:
                            fn(e).then_inc(sems[inc[0]], inc[1])
                return body
            block.tensor(mk("pe"))
            block.vector(mk("dve"))
            block.scalar(mk("act"))
            block.gpsimd(mk("pool"))
            block.sync(mk("sp"))
        self.streams = {k: [] for k in streams}


class Builder:
    def __init__(self, NPT, dbg=False):
        self.NPT = NPT
        self.NP = NPT * 128
        self.TT = NPT + NSB
        self.NTOK = self.TT * 128
        self.NH = NPT // 2
        self.NS1 = NSB // 2
        self.MT = self.NH + self.NS1
        self.MTOK = self.MT * 128
        self.BLK = min(256, self.NH * 128)
        self.dbg = dbg
        self.nc = bass.Bass("TRN2", target_bir_lowering=False)
        self.S = Sched(self.nc)
        self.outs = []

    def MM(self, out, lhsT, rhs, start, stop, R, W):
        self.S.op("pe", lambda e: e.matmul(out, lhsT, rhs, start=start, stop=stop), R, W)

    def TR(self, out, in_, ident, R, W):
        self.S.op("pe", lambda e: e.transpose(out, in_, ident), R, W)

    def ACT(self, out, in_, func, R, W, **kw):
        self.S.op("act", lambda e: e.activation(out, in_, func, **kw), R, W)

    def TT_(self, eng, out, a, b, op, R, W):
        self.S.op(eng, lambda e: e.tensor_tensor(out, a, b, op), R, W)

    def TS(self, eng, out, a, s1, s2, op0, op1, R, W):
        if op1 is None:
            self.S.op(eng, lambda e: e.tensor_scalar(out, a, s1, None, op0), R, W)
        else:
            self.S.op(eng, lambda e: e.tensor_scalar(out, a, s1, s2, op0, op1), R, W)

    def STT(self, out, a, s, b, op0, op1, R, W):
        self.S.op("dve", lambda e: e.scalar_tensor_tensor(out, a, s, b, op0, op1), R, W)

    def CP(self, eng, out, in_, R, W):
        if eng == "act":
            self.S.op("act", lambda e: e.activation(out, in_, AF.Copy), R, W)
        else:
            self.S.op(eng, lambda e: e.tensor_copy(out, in_), R, W)

    def RED(self, out, in_, op, R, W):
        self.S.op("dve", lambda e: e.tensor_reduce(out, in_, AX.X, op), R, W)

    def RCP(self, out, in_, R, W):
        self.S.op("dve", lambda e: e.reciprocal(out, in_), R, W)

    def MSET(self, eng, ap, val, W):
        self.S.op(eng, lambda e: e.memset(ap, val), (), W)

    def LD(self, out, in_, R, W, q="sp"):
        self.S.dma(q, out, in_, R, W)

    def phase(self, fn, *a):
        S = self.S
        import os
        self._phi = getattr(self, "_phi", 0) + 1
        lim = int(os.environ.get("K_MAXPH", "999"))
        if self._phi > lim:
            return
        print("PHASE", self._phi, getattr(fn, "__name__", "?"), a, flush=True)
        with ExitStack() as st:
            S.stack = st
            fn(*a)
            S.barrier()
            S.emit()
        S.stack = S.gstack

    def bcast_row(self, name, src_ap_1d, n):
        t = self.S.sb(name, [128, n], F32)
        self.LD(t.ap, src_ap_1d.partition_broadcast(128), [self.din_res], [t])
        return t

    def norm_tile(self, h, nw, out, scr, st):
        self.MSET("dve", st[:, 0:1], 0.0, [st])
        self.ACT(scr.ap, h.ap, AF.Square, [h, st], [scr, st], accum_out=st[:, 0:1])
        self.TS("dve", st[:, 1:2], st[:, 0:1], 1.0 / D, 1e-6, ALU.mult, ALU.add, [st], [st])
        self.ACT(st[:, 1:2], st[:, 1:2], AF.Sqrt, [st], [st])
        self.RCP(st[:, 2:3], st[:, 1:2], [st], [st])
        self.STT(out.ap, h.ap, st[:, 2:3], nw.ap, ALU.mult, ALU.mult, [h, st, nw], [out])

    def transpose_cols(self, src, ncol, dst_ap3, dst_res, psT, eng="dve"):
        kc = ncol // 128
        for k in range(kc):
            self.TR(psT[:, k * 128:(k + 1) * 128], src[:, k * 128:(k + 1) * 128], self.identb.ap, [src, self.identb], [psT])
        self.CP(eng, dst_ap3, psT[:, 0:ncol].rearrange("p (k t) -> p k t", k=kc), [psT], [dst_res])

    def load_w(self, name, w2d, rows, cols, c0=0, c1=None):
        c1 = cols if c1 is None else c1
        kc = rows // 128
        t = self.S.sb(name, [128, kc, c1 - c0], BF16)
        self.LD(t.ap, w2d[:, c0:c1].rearrange("(k p) c -> p k c", p=128), [self.din_res], [t], q="pool")
        return t

    def build(self):
        S, nc = self.S, self.nc
        NTOK, TT, NP, NPT = self.NTOK, self.TT, self.NP, self.NPT
        di = {}
        self.din_res = Res("inputs")

        def inp(name, shape):
            di[name] = S.dram(name, shape, F32, kind="ExternalInput")
            return di[name]

        inp("xin", [NTOK, D]); inp("pin", [2, NTOK, 256]); inp("wkv0", [NSB, H, 64, 64]); inp("shift0", [NSB, D])
        inp("cache", [NSB, MAXW, 512]); inp("sel", [128, 2]); inp("pin1", [self.MTOK, 256])
        inp("norm_w", [2, 4, D]); inp("ffn1_wi", [2, D, 2 * DFF]); inp("ffn1_wo", [2, DFF, D])
        inp("ffn2_wi", [2, D, 2 * DFF]); inp("ffn2_wo", [2, DFF, D]); inp("pe_proj", [2, 256, D]); inp("pe_gate", [2, D, D])
        inp("rwkv_mix", [6, D]); inp("rwkv_wrkv", [3, D, D]); inp("rwkv_wo", [D, D]); inp("rwkv_w0", [D])
        inp("rwkv_w1", [D, 64]); inp("rwkv_w2", [64, D]); inp("rwkv_a0", [D]); inp("rwkv_a1", [D, 64]); inp("rwkv_a2", [64, D])
        inp("rwkv_g1", [D, 160]); inp("rwkv_g2", [160, D]); inp("rwkv_kk", [D]); inp("rwkv_ka", [D]); inp("rwkv_rk", [D])
        inp("rwkv_lnx_w", [D]); inp("rwkv_lnx_b", [D]); inp("attn_wq", [D, 3 * D]); inp("attn_wo", [D, D])
        inp("kv_norm", [D]); inp("w_kv", [D, 512]); inp("final_norm", [D])
        inp("c_ident", [128, 128]); inp("c_tri", [128, 256]); inp("c_m1", [128, 384]); inp("c_m2", [128, 256])
        inp("c_rowmask", [128, 2])
        self.cfgs = self.attn_cfgs()
        for name, (d, nq, nres, g) in self.cfgs.items():
            inp(f"bias_{name}", [4, nq + 128, 4 * nq])
        self.di = di
        def outp(name, shape):
            t = S.dram(name, shape, F32, kind="ExternalOutput")
            self.outs.append(t)
            return t
        self.yout = outp("yout", [self.MTOK, D]); self.kvout = outp("kvout", [NTOK, 512])
        self.wkvp = outp("wkvp", [H, 64, 64]); self.wkvs = outp("wkvs", [NSB, H, 64, 64])
        self.shiftall = outp("shiftall", [NTOK, D])
        self.Hd = S.dram("Hd", [NTOK, D], F32)
        self.Hm = S.dram("Hm", [self.MTOK, D], F32)
        WM = MAXW + self.NH * 128
        self.KTm = S.dram("KTm", [65, 4, WM], BF16); self.VDm = S.dram("VDm", [WM, 256], BF16)
        self.KTsm = S.dram("KTsm", [self.NS1, 65, 4, MAXW + 8], BF16); self.VDsm = S.dram("VDsm", [self.NS1, MAXW + 8, 256], BF16)
        self.HN = S.dram("HN", [NTOK + 1, D], F32)
        self.Rd = S.dram("Rd", [NTOK, D], F32); self.Kd = S.dram("Kd", [NTOK, D], F32); self.Vd = S.dram("Vd", [NTOK, D], F32)
        self.KKd = S.dram("KKd", [NTOK, D], F32); self.Bd = S.dram("Bd", [NTOK, D], F32); self.LWd = S.dram("LWd", [NTOK, D], F32)
        self.Gd = S.dram("Gd", [NTOK, D], F32); self.BONd = S.dram("BONd", [NTOK, H], F32); self.Yd = S.dram("Yd", [NTOK, D], F32)
        self.KTp = S.dram("KTp", [65, 4, MAXW + NP], BF16); self.VDp = S.dram("VDp", [MAXW + NP, 256], BF16)
        self.KTs = S.dram("KTs", [NSB, 65, 4, MAXW + 8], BF16); self.VDs = S.dram("VDs", [NSB, MAXW + 8, 256], BF16)
        if self.dbg:
            self.dbgH = [outp(f"dbgH{i}", [NTOK if i < 4 else self.MTOK, D]) for i in range(8)]
            self.dbgY = outp("dbgY", [NTOK, D])
        self.identb = S.sb("identb", [128, 128], BF16)
        self.identf = S.sb("identf", [128, 128], F32)
        S.dma("pool", self.identb.ap, di["c_ident"].ap, [self.din_res], [self.identb])
        S.dma("sp", self.identf.ap, di["c_ident"].ap, [self.din_res], [self.identf])
        self.zero = S.sb("zero", [128, D], F32)
        self.MSET("pool", self.zero.ap, 0.0, [self.zero])
        self.ones_b = S.sb("ones_b", [128, 64], BF16)
        self.MSET("pool", self.ones_b.ap, 1.0, [self.ones_b])
        self.ones_f = S.sb("ones_f", [128, 1], F32)
        self.MSET("pool", self.ones_f.ap, 1.0, [self.ones_f])
        self.rowmask = S.sb("rowmask", [128, 2], F32)
        S.dma("sp", self.rowmask.ap, di["c_rowmask"].ap, [self.din_res], [self.rowmask])

        self.ndbg = 0
        self.cH, self.cTT, self.cNPT, self.cKT, self.cVD = self.Hd, self.TT, self.NPT, self.KTp, self.VDp
        self.cPin = lambda layer: di["pin"].ap[layer]
        self.phase(self.ph_init)
        for layer in range(2):
            if layer == 1:
                self.phase(self.ph_kv)
                self.phase(self.ph_select)
                self.phase(self.ph_select_s)
                self.cH, self.cTT, self.cNPT, self.cKT, self.cVD = self.Hm, self.MT, self.NH, self.KTm, self.VDm
                self.cPin = lambda layer: di["pin1"].ap
            self.phase(self.ph_ffn, layer, 0)
            self.snap()
            if layer == 0:
                self.phase(self.ph_rwkv_proj)
                self.phase(self.ph_rwkv_scan)
                self.phase(self.ph_rwkv_post)
            else:
                self.phase(self.ph_attn)
            self.snap()
            self.phase(self.ph_ffn, layer, 2)
            self.snap()
            self.phase(self.ph_pe, layer)
            self.snap()
        return nc

    def snap(self):
        if not self.dbg:
            return
        i = self.ndbg
        self.ndbg += 1

        def f():
            self.S.dma("sp", self.dbgH[i].ap, self.cH.ap, [self.cH], [self.dbgH[i]])
        self.phase(f)

    def ph_init(self):
        S = self.S
        S.dma("sp", self.Hd.ap, self.di["xin"].ap, [self.din_res], [self.Hd])
        S.dma("sp", self.HN[0:1, :], self.zero[0:1, :], [self.zero], [self.HN])
        zb = S.sb("zb", [128, 2048], BF16)
        self.MSET("dve", zb.ap, 0.0, [zb])
        nb = S.sb("nb", [128, 4 * MAXW], BF16)
        self.MSET("dve", nb.ap, NEG, [nb])
        for h in range(4):
            S.dma("sp", self.KTp[0:64, h, 0:MAXW], zb[0:64, :], [zb], [self.KTp])
        S.dma("sp", self.KTp[64:65, :, 0:MAXW], nb[0:1, :].rearrange("p (h n) -> p h n", h=4), [nb], [self.KTp])
        for h in range(4):
            for c0 in range(0, self.NP, 2048):
                c1 = min(self.NP, c0 + 2048)
                S.dma("sp", self.KTp[64:65, h, MAXW + c0:MAXW + c1], zb[0:1, 0:c1 - c0], [zb], [self.KTp])
        for r0 in range(0, MAXW, 128):
            S.dma("sp", self.VDp[r0:r0 + 128, :], zb[:, 0:256], [zb], [self.VDp])
        for b in range(NSB):
            for h in range(4):
                S.dma("sp", self.KTs[b, 64:65, h, 0:2048], zb[0:1, 0:2048], [zb], [self.KTs])
                S.dma("sp", self.KTs[b, 64:65, h, 2048:2056], zb[0:1, 0:8], [zb], [self.KTs])

    def ph_ffn(self, layer, which):
        S = self.S
        wi = self.di["ffn1_wi" if which == 0 else "ffn2_wi"].ap[layer]
        wo = self.di["ffn1_wo" if which == 0 else "ffn2_wo"].ap[layer]
        nw = self.bcast_row("nw", self.di["norm_w"].ap[layer, which], D)
        GT = 8
        groups = [list(range(s, min(self.cTT, s + GT))) for s in range(0, self.cTT, GT)]
        maxg = max(len(g) for g in groups)
        hg = [S.sb(f"hg{i}", [128, D], F32) for i in range(maxg)]
        xT = S.sb("xT", [128, 8, maxg * 128], BF16)
        hid = S.sb("hid", [128, NFC, maxg * 128], BF16)
        wob = S.sb("wob", [128, NFC, D], BF16)
        BW = 512
        blocks = [(c0, min(DFF, c0 + BW)) for c0 in range(0, DFF, BW)]
        wib = [S.sb(f"wib{i}", [128, 8, 2 * BW], BF16) for i in range(2)]
        xn = [S.sb(f"xn{i}", [128, D], BF16) for i in range(2)]
        scr = S.sb("scr", [128, D], F32)
        st = [S.sb(f"st{i}", [128, 4], F32) for i in range(2)]
        sg = [S.sb(f"sg{i}", [128, 512], BF16) for i in range(2)]
        psT = [S.ps(f"psT{i}", [128, D], BF16) for i in range(2)]
        pg = [S.ps(f"pg{i}", [128, 512]) for i in range(2)]
        pu = [S.ps(f"pu{i}", [128, 512]) for i in range(2)]
        po = S.ps("po", [128, D])
        cnt = 0
        nload = [0]
        fuse_norm = (layer == 0 and which == 0)
        if fuse_norm:
            nw1 = self.bcast_row("nw1", self.di["norm_w"].ap[0, 1], D)
            hno = [S.sb(f"hno{i}", [128, D], F32) for i in range(2)]

        def load_block(bi):
            c0, c1 = blocks[bi]
            w = wib[nload[0] % 2]
            nload[0] += 1
            bw = c1 - c0
            S.dma("pool", w[:, :, 0:bw], wi[:, c0:c1].rearrange("(k p) c -> p k c", p=128), [self.din_res], [w])
            S.dma("pool", w[:, :, BW:BW + bw], wi[:, DFF + c0:DFF + c1].rearrange("(k p) c -> p k c", p=128), [self.din_res], [w])
            return w

        pending = load_block(0)
        for gi, g in enumerate(groups):
            ng = len(g)
            ntok = ng * 128
            for i, t in enumerate(g):
                self.LD(hg[i].ap, self.cH[t * 128:(t + 1) * 128, :], [self.cH], [hg[i]])
                self.norm_tile(hg[i], nw, xn[i % 2], scr, st[i % 2])
                self.transpose_cols(xn[i % 2], D, xT[:, :, i * 128:(i + 1) * 128], xT, psT[i % 2])
            for bi, (c0, c1) in enumerate(blocks):
                w = pending
                if bi + 1 < len(blocks):
                    pending = load_block(bi + 1)
                elif gi + 1 < len(groups):
                    pending = load_block(0)
                if bi == 1:
                    S.dma("pool", wob.ap, wo.rearrange("(k p) c -> p k c", p=128), [self.din_res], [wob])
                for cc in range((c1 - c0) // 128):
                    c = c0 // 128 + cc
                    for n0 in range(0, ntok, 512):
                        n1 = min(ntok, n0 + 512)
                        nn = n1 - n0
                        j = cnt % 2
                        cnt += 1
                        for k in range(8):
                            self.MM(pg[j][:, 0:nn], w[:, k, cc * 128:(cc + 1) * 128], xT[:, k, n0:n1], k == 0, k == 7, [w, xT], [pg[j]])
                        for k in range(8):
                            self.MM(pu[j][:, 0:nn], w[:, k, BW + cc * 128:BW + (cc + 1) * 128], xT[:, k, n0:n1], k == 0, k == 7, [w, xT], [pu[j]])
                        self.ACT(sg[j][:, 0:nn], pg[j][:, 0:nn], AF.Silu, [pg[j]], [sg[j]])
                        self.TT_("dve", hid[:, c, n0:n1], pu[j][:, 0:nn], sg[j][:, 0:nn], ALU.mult, [pu[j], sg[j]], [hid])
            for i, t in enumerate(g):
                for hh in range(2):
                    for c in range(NFC):
                        self.MM(po[:, hh * 512:(hh + 1) * 512], hid[:, c, i * 128:(i + 1) * 128], wob[:, c, hh * 512:(hh + 1) * 512],
                                c == 0, c == NFC - 1, [hid, wob], [po])
                self.STT(hg[i].ap, po.ap, 0.5, hg[i].ap, ALU.mult, ALU.add, [po, hg[i]], [hg[i]])
                self.LD(self.cH[t * 128:(t + 1) * 128, :], hg[i].ap, [hg[i]], [self.cH])
                if fuse_norm:
                    o_ = hno[i % 2]
                    self.norm_tile(hg[i], nw1, o_, scr, st[i % 2])
                    self.LD(self.HN[1 + t * 128:1 + (t + 1) * 128, :], o_.ap, [o_], [self.HN])
                    self.LD(self.shiftall[t * 128:(t + 1) * 128, :], o_.ap, [o_], [self.shiftall])

    def ph_pe(self, layer):
        S = self.S
        nw = self.bcast_row("nw", self.di["norm_w"].ap[layer, 3], D)
        wg = self.load_w("wg", self.di["pe_gate"].ap[layer], D, D)
        wp = self.load_w("wp", self.di["pe_proj"].ap[layer], 256, D)
        h = [S.sb(f"h{i}", [128, D], F32) for i in range(2)]
        pb = [S.sb(f"pb{i}", [128, 256], BF16) for i in range(2)]
        xn = [S.sb(f"xn{i}", [128, D], BF16) for i in range(2)]
        xT = [S.sb(f"xT{i}", [128, 8, 128], BF16) for i in range(2)]
        pT = [S.sb(f"pT{i}", [128, 2, 128], BF16) for i in range(2)]
        sgm = [S.sb(f"sgm{i}", [128, D], F32) for i in range(2)]
        scr = S.sb("scr", [128, D], F32)
        st = [S.sb(f"st{i}", [128, 4], F32) for i in range(2)]
        psT = [S.ps(f"psT{i}", [128, D], BF16) for i in range(2)]
        pg = S.ps("pg", [128, D])
        pe = S.ps("pe", [128, D])
        fuse_final = (layer == 1)
        if fuse_final:
            nwf = self.bcast_row("nwf", self.di["final_norm"].ap, D)
            fo = [S.sb(f"fo{i}", [128, D], F32) for i in range(2)]
        for t in range(self.cTT):
            j = t % 2
            rows = slice(t * 128, (t + 1) * 128)
            self.LD(h[j].ap, self.cH[rows, :], [self.cH], [h[j]])
            S.dma("pool", pb[j].ap, self.cPin(layer)[rows, :], [self.din_res], [pb[j]])
            self.norm_tile(h[j], nw, xn[j], scr, st[j])
            self.transpose_cols(xn[j], D, xT[j].ap, xT[j], psT[j])
            self.transpose_cols(pb[j], 256, pT[j].ap, pT[j], psT[j], eng="act")
            for hh in range(2):
                cs = slice(hh * 512, (hh + 1) * 512)
                for k in range(8):
                    self.MM(pg[:, cs], xT[j][:, k, :], wg[:, k, cs], k == 0, k == 7, [xT[j], wg], [pg])
                for k in range(2):
                    self.MM(pe[:, cs], pT[j][:, k, :], wp[:, k, cs], k == 0, k == 1, [pT[j], wp], [pe])
            self.ACT(sgm[j].ap, pg.ap, AF.Sigmoid, [pg], [sgm[j]])
            self.TT_("dve", sgm[j].ap, pe.ap, sgm[j].ap, ALU.mult, [pe, sgm[j]], [sgm[j]])
            self.TT_("pool", h[j].ap, h[j].ap, sgm[j].ap, ALU.add, [h[j], sgm[j]], [h[j]])
            if fuse_final:
                self.norm_tile(h[j], nwf, fo[j], scr, st[j])
                self.LD(self.yout[rows, :], fo[j].ap, [fo[j]], [self.yout])
            else:
                self.LD(self.cH[rows, :], h[j].ap, [h[j]], [self.cH])

    def ph_final(self):
        S = self.S
        nw = self.bcast_row("nw", self.di["final_norm"].ap, D)
        h = [S.sb(f"h{i}", [128, D], F32) for i in range(2)]
        o = [S.sb(f"o{i}", [128, D], F32) for i in range(2)]
        scr = S.sb("scr", [128, D], F32)
        st = [S.sb(f"st{i}", [128, 4], F32) for i in range(2)]
        for t in range(self.cTT):
            j = t % 2
            rows = slice(t * 128, (t + 1) * 128)
            self.LD(h[j].ap, self.cH[rows, :], [self.cH], [h[j]])
            self.norm_tile(h[j], nw, o[j], scr, st[j])
            self.LD(self.yout[rows, :], o[j].ap, [o[j]], [self.yout])

    def ph_rwkv_norm(self):
        S = self.S
        nw = self.bcast_row("nw", self.di["norm_w"].ap[0, 1], D)
        h = [S.sb(f"h{i}", [128, D], F32) for i in range(2)]
        o = [S.sb(f"o{i}", [128, D], F32) for i in range(2)]
        scr = S.sb("scr", [128, D], F32)
        st = [S.sb(f"st{i}", [128, 4], F32) for i in range(2)]
        for t in range(self.TT):
            j = t % 2
            rows = slice(t * 128, (t + 1) * 128)
            self.LD(h[j].ap, self.Hd[rows, :], [self.Hd], [h[j]])
            self.norm_tile(h[j], nw, o[j], scr, st[j])
            self.LD(self.HN[1 + t * 128:1 + (t + 1) * 128, :], o[j].ap, [o[j]], [self.HN])
            self.LD(self.shiftall[rows, :], o[j].ap, [o[j]], [self.shiftall])

    def ph_rwkv_proj(self):
        S, di = self.S, self.di
        mixn = S.sb("mixn", [48, 128], F32)
        self.LD(mixn.ap, di["rwkv_mix"].ap.rearrange("a (k p) -> (a k) p", p=128), [self.din_res], [mixn])
        mixT = S.sb("mixT", [128, 48], F32)
        w0b = self.bcast_row("w0b", di["rwkv_w0"].ap, D)
        a0b = self.bcast_row("a0b", di["rwkv_a0"].ap, D)
        kkb = self.bcast_row("kkb", di["rwkv_kk"].ap, D)
        kab = self.bcast_row("kab", di["rwkv_ka"].ap, D)
        rkb = self.bcast_row("rkb", di["rwkv_rk"].ap, D)
        wr = [self.load_w(f"wrkv{i}", di["rwkv_wrkv"].ap[i], D, D) for i in range(3)]
        w1 = self.load_w("w1", di["rwkv_w1"].ap, D, 64)
        a1 = self.load_w("a1", di["rwkv_a1"].ap, D, 64)
        g1 = self.load_w("g1", di["rwkv_g1"].ap, D, 160)
        w2 = S.sb("w2", [64, D], BF16); S.dma("pool", w2.ap, di["rwkv_w2"].ap, [self.din_res], [w2])
        a2 = S.sb("a2", [64, D], BF16); S.dma("pool", a2.ap, di["rwkv_a2"].ap, [self.din_res], [a2])
        g2a = S.sb("g2a", [128, D], BF16); S.dma("pool", g2a.ap, di["rwkv_g2"].ap[0:128, :], [self.din_res], [g2a])
        g2b = S.sb("g2b", [32, D], BF16); S.dma("pool", g2b.ap, di["rwkv_g2"].ap[128:160, :], [self.din_res], [g2b])
        psm = S.ps("psm", [128, 512])
        self.S.op("pe", lambda e: e.transpose(psm[:, 0:48], mixn.ap, self.identf[0:48, 0:48]), [mixn, self.identf], [psm])
        self.CP("dve", mixT.ap, psm[:, 0:48], [psm], [mixT])
        def scaled(name, w, j, cols):
            t = S.sb(name, [128, 8, cols], BF16)
            for k in range(8):
                self.TS("dve" if k % 2 else "pool", t[:, k, :], w[:, k, :], mixT[:, j * 8 + k:j * 8 + k + 1], None, ALU.mult, None, [w, mixT], [t])
            return t
        wrs = [scaled("wrs0", wr[0], 0, D), scaled("wrs1", wr[1], 2, D), scaled("wrs2", wr[2], 3, D)]
        w1s = scaled("w1s", w1, 1, 64)
        a1s = scaled("a1s", a1, 4, 64)
        g1s = scaled("g1s", g1, 5, 160)
        hn = [S.sb(f"hn{i}", [128, D], F32) for i in range(2)]
        hp = [S.sb(f"hp{i}", [128, D], F32) for i in range(2)]
        xx = S.sb("xx", [128, D], F32)
        tmp = S.sb("tmp", [128, D], F32)
        xm = [S.sb(f"xm{i}", [128, D], BF16) for i in range(2)]
        xTh = [S.sb(f"xTh{i}", [128, 8, 128], BF16) for i in range(2)]
        xTx = [S.sb(f"xTx{i}", [128, 8, 128], BF16) for i in range(2)]
        lo = S.sb("lo", [128, 128], BF16)
        lo2 = S.sb("lo2", [32, 128], BF16)
        o_r = S.sb("o_r", [128, D], F32); o_k = S.sb("o_k", [128, D], F32); o_v = S.sb("o_v", [128, D], F32)
        o_a = S.sb("o_a", [128, D], F32); o_kk = S.sb("o_kk", [128, D], F32); o_b = S.sb("o_b", [128, D], F32)
        o_w = S.sb("o_w", [128, D], F32); o_g = S.sb("o_g", [128, D], F32)
        sm = S.sb("sm", [128, 3, H], F32)
        psT = [S.ps(f"psT{i}", [128, D], BF16) for i in range(2)]
        pp = [S.ps(f"pp{i}", [128, D]) for i in range(2)]
        pl = S.ps("pl", [128, 512])
        v3 = lambda t_: t_.ap.rearrange("p (h n) -> p h n", h=H)
        for t in range(self.TT):
            j = t % 2
            rows = slice(t * 128, (t + 1) * 128)
            self.LD(hn[j].ap, self.HN[1 + t * 128:1 + (t + 1) * 128, :], [self.HN], [hn[j]])
            self.LD(hp[j].ap, self.HN[t * 128:(t + 1) * 128, :], [self.HN], [hp[j]])
            if t >= self.NPT:
                self.LD(hp[j][0:1, :], di["shift0"].ap[t - self.NPT:t - self.NPT + 1, :], [self.din_res], [hp[j]])
            self.CP("pool", xm[0].ap, hn[j].ap, [hn[j]], [xm[0]])
            self.TT_("dve", xm[1].ap, hp[j].ap, hn[j].ap, ALU.subtract, [hp[j], hn[j]], [xm[1]])
            xh, xd = xTh[j], xTx[j]
            self.transpose_cols(xm[0], D, xh.ap, xh, psT[0], eng="act")
            self.transpose_cols(xm[1], D, xd.ap, xd, psT[1], eng="act")
            def proj(w, ws, pt):
                for hh in range(2):
                    cs = slice(hh * 512, (hh + 1) * 512)
                    for k in range(8):
                        self.MM(pt[:, cs], xh[:, k, :], w[:, k, cs], k == 0, False, [xh, w], [pt])
                    for k in range(8):
                        self.MM(pt[:, cs], xd[:, k, :], ws[:, k, cs], False, k == 7, [xd, ws], [pt])
            def lora1(out, w, ws, c0, c1):
                for k in range(8):
                    self.MM(out, w[:, k, c0:c1], xh[:, k, :], k == 0, False, [w, xh], [pl])
                for k in range(8):
                    self.MM(out, ws[:, k, c0:c1], xd[:, k, :], False, k == 7, [ws, xd], [pl])
            proj(wr[0], wrs[0], pp[0]); self.CP("act", o_r.ap, pp[0].ap, [pp[0]], [o_r])
            proj(wr[1], wrs[1], pp[1]); self.CP("act", o_k.ap, pp[1].ap, [pp[1]], [o_k])
            proj(wr[2], wrs[2], pp[0]); self.CP("act", o_v.ap, pp[0].ap, [pp[0]], [o_v])
            lora1(pl[0:64, 0:128], w1, w1s, 0, 64)
            self.ACT(lo[0:64, :], pl[0:64, 0:128], AF.Tanh, [pl], [lo])
            for hh in range(2):
                cs = slice(hh * 512, (hh + 1) * 512)
                self.MM(pp[1][:, cs], lo[0:64, :], w2[:, cs], True, True, [lo, w2], [pp[1]])
            self.TT_("dve", o_w.ap, pp[1].ap, w0b.ap, ALU.add, [pp[1], w0b], [o_w])
            self.ACT(o_w.ap, o_w.ap, AF.Sigmoid, [o_w], [o_w])
            self.TS("dve", o_w.ap, o_w.ap, -float(np.exp(-0.5)), None, ALU.mult, None, [o_w], [o_w])
            if t >= self.NPT:
                self.TS("dve", o_w.ap, o_w.ap, self.rowmask[:, 1:2], None, ALU.mult, None, [o_w, self.rowmask], [o_w])
            lora1(pl[0:64, 0:128], a1, a1s, 0, 64)
            self.CP("act", lo[0:64, :], pl[0:64, 0:128], [pl], [lo])
            for hh in range(2):
                cs = slice(hh * 512, (hh + 1) * 512)
                self.MM(pp[0][:, cs], lo[0:64, :], a2[:, cs], True, True, [lo, a2], [pp[0]])
            self.TT_("dve", o_a.ap, pp[0].ap, a0b.ap, ALU.add, [pp[0], a0b], [o_a])
            self.ACT(o_a.ap, o_a.ap, AF.Sigmoid, [o_a], [o_a])
            lora1(pl[:, 0:128], g1, g1s, 0, 128)
            lora1(pl[0:32, 128:256], g1, g1s, 128, 160)
            self.ACT(lo.ap, pl[:, 0:128], AF.Sigmoid, [pl], [lo])
            self.ACT(lo2.ap, pl[0:32, 128:256], AF.Sigmoid, [pl], [lo2])
            for hh in range(2):
                cs = slice(hh * 512, (hh + 1) * 512)
                self.MM(pp[1][:, cs], lo.ap, g2a[:, cs], True, False, [lo, g2a], [pp[1]])
                self.MM(pp[1][:, cs], lo2.ap, g2b[:, cs], False, True, [lo2, g2b], [pp[1]])
            self.CP("act", o_g.ap, pp[1].ap, [pp[1]], [o_g])
            self.TT_("dve", o_kk.ap, o_k.ap, kkb.ap, ALU.mult, [o_k, kkb], [o_kk])
            self.TT_("pool", tmp.ap, o_kk.ap, o_kk.ap, ALU.mult, [o_kk], [tmp])
            self.RED(sm[:, 0, :], v3(tmp), ALU.add, [tmp], [sm])
            self.ACT(sm[:, 0, :], sm[:, 0, :], AF.Sqrt, [sm], [sm])
            self.TS("dve", sm[:, 0, :], sm[:, 0, :], 1e-12, None, ALU.max, None, [sm], [sm])
            self.RCP(sm[:, 1, :], sm[:, 0, :], [sm], [sm])
            self.TT_("dve", v3(o_kk), v3(o_kk), sm[:, 1, :].unsqueeze(2).broadcast_to([128, H, 64]), ALU.mult, [o_kk, sm], [o_kk])
            self.TT_("pool", o_b.ap, o_kk.ap, o_a.ap, ALU.mult, [o_kk, o_a], [o_b])
            self.TS("dve", tmp.ap, o_a.ap, -1.0, None, ALU.add, None, [o_a], [tmp])
            self.TT_("dve", tmp.ap, tmp.ap, kab.ap, ALU.mult, [tmp, kab], [tmp])
            self.STT(o_k.ap, tmp.ap, 1.0, o_k.ap, ALU.add, ALU.mult, [tmp, o_k], [o_k])
            self.TT_("pool", tmp.ap, o_r.ap, o_k.ap, ALU.mult, [o_r, o_k], [tmp])
            self.TT_("dve", tmp.ap, tmp.ap, rkb.ap, ALU.mult, [tmp, rkb], [tmp])
            self.RED(sm[:, 2, :], v3(tmp), ALU.add, [tmp], [sm])
            if t >= self.NPT:
                self.TS("dve", o_b.ap, o_b.ap, self.rowmask[:, 1:2], None, ALU.mult, None, [o_b, self.rowmask], [o_b])
                self.TS("dve", o_k.ap, o_k.ap, self.rowmask[:, 1:2], None, ALU.mult, None, [o_k, self.rowmask], [o_k])
            for src, dst in ((o_r, self.Rd), (o_k, self.Kd), (o_v, self.Vd), (o_kk, self.KKd), (o_b, self.Bd), (o_w, self.LWd), (o_g, self.Gd)):
                self.LD(dst[rows, :], src.ap, [src], [dst])
            self.LD(self.BONd[rows, :], sm[:, 2, :], [sm], [self.BONd])

    def ph_rwkv_scan(self):
        S, di = self.S, self.di
        import os
        LV = float(os.environ.get("K_SCAN", "9"))
        tri = S.sb("tri", [128, 256], F32); self.LD(tri.ap, di["c_tri"].ap, [self.din_res], [tri])
        m1 = S.sb("m1", [128, 384], BF16); S.dma("pool", m1.ap, di["c_m1"].ap, [self.din_res], [m1])
        m2 = S.sb("m2", [128, 256], BF16); S.dma("pool", m2.ap, di["c_m2"].ap, [self.din_res], [m2])
        names = ("r", "k", "v", "kk", "b", "lw")
        srcs = (self.Rd, self.Kd, self.Vd, self.KKd, self.Bd, self.LWd)
        inb = {n: S.sb(f"in_{n}", [128, D], F32) for n in names}
        ecum = S.sb("ecum", [128, D], F32); encum = S.sb("encum", [128, D], F32)
        eex = S.sb("eex", [128, D], F32); erc = S.sb("erc", [128, D], F32)
        PB = [dict(tA=S.sb(f"tA{p}", [128, D], BF16), tR=S.sb(f"tR{p}", [128, D], BF16), tB=S.sb(f"tB{p}", [128, D], BF16),
                   tK=S.sb(f"tK{p}", [128, D], BF16), hB=S.sb(f"hB{p}", [128, D], BF16), hK=S.sb(f"hK{p}", [128, D], BF16),
                   vb=S.sb(f"vb{p}", [128, D], BF16), wc=S.sb(f"wc{p}", [64, H], F32)) for p in range(2)]
        yt = S.sb("yt", [128, D], F32)
        Mf = [S.sb(f"Mf{h}", [64, 64], F32) for h in range(H)]
        Mb = [S.sb(f"Mb{h}", [64, 64], BF16) for h in range(H)]
        G = 6
        NB = G
        TTh = [S.sb(f"TTh{i}", [64, 512], BF16) for i in range(NB)]
        SC1 = [S.sb(f"SC1{i}", [128, 384], BF16) for i in range(NB)]
        SC2 = [S.sb(f"SC2{i}", [128, 256], BF16) for i in range(NB)]
        XX = [[S.sb(f"XX{i}_{p}", [128, 256], BF16) for p in range(2)] for i in range(NB)]
        PP = [[S.sb(f"PP{i}_{p}", [128, 128], BF16) for p in range(2)] for i in range(NB)]
        Zb = [S.sb(f"Zb{i}", [128, 64], BF16) for i in range(NB)]
        AhT = [S.sb(f"AhT{i}", [64, 128], BF16) for i in range(NB)]
        Ub = [S.sb(f"Ub{i}", [128, 64], BF16) for i in range(NB)]
        st0 = S.sb("st0", [64, 64], F32)
        pcum = S.ps("pcum", [128, D]); prc = pcum
        bank = [S.ps(f"bk{i}", [128, 512]) for i in range(G)]
        pw = bank
        def prep(t, p):
            tA, tR, tB, tK, hB, hK, vb, wc = (PB[p][k] for k in ('tA', 'tR', 'tB', 'tK', 'hB', 'hK', 'vb', 'wc'))
            if LV < 2:
                return
            rows = slice(t * 128, (t + 1) * 128)
            for n, s_ in zip(names, srcs):
                self.LD(inb[n].ap, s_[rows, :], [s_], [inb[n]])
            lw = inb["lw"]
            if LV < 2.2:
                return
            for hh in range(2):
                cs = slice(hh * 512, (hh + 1) * 512)
                self.MM(pcum[:, cs], tri[:, 0:128], lw[:, cs], True, True, [tri, lw], [pcum])
            if LV < 2.12:
                return
            self.ACT(ecum.ap, pcum.ap, AF.Exp, [pcum], [ecum])
            if LV < 2.13:
                return
            self.ACT(encum.ap, pcum.ap, AF.Exp, [pcum], [encum], scale=-1.0)
            if LV < 2.14:
                return
            self.TT_("dve", eex.ap, pcum.ap, lw.ap, ALU.subtract, [pcum, lw], [eex])
            self.ACT(eex.ap, eex.ap, AF.Exp, [eex], [eex])
            if LV < 2.15:
                return
            for hh in range(2):
                cs = slice(hh * 512, (hh + 1) * 512)
                self.MM(prc[:, cs], tri[:, 128:256], lw[:, cs], True, True, [tri, lw], [prc])
            self.ACT(erc.ap, prc.ap, AF.Exp, [prc], [erc])
            if LV < 2.3:
                return
            self.STT(tA.ap, inb["kk"].ap, -1.0, eex.ap, ALU.mult, ALU.mult, [inb["kk"], eex], [tA])
            self.TT_("pool", tR.ap, inb["r"].ap, ecum.ap, ALU.mult, [inb["r"], ecum], [tR])
            self.TT_("dve", tB.ap, inb["b"].ap, encum.ap, ALU.mult, [inb["b"], encum], [tB])
            self.TT_("pool", tK.ap, inb["k"].ap, encum.ap, ALU.mult, [inb["k"], encum], [tK])
            self.TT_("dve", hB.ap, inb["b"].ap, erc.ap, ALU.mult, [inb["b"], erc], [hB])
            self.TT_("pool", hK.ap, inb["k"].ap, erc.ap, ALU.mult, [inb["k"], erc], [hK])
            self.CP("pool", vb.ap, inb["v"].ap, [inb["v"]], [vb])
            if LV < 2.4:
                return
            pz = pw[0]
            for h in range(H):
                self.MM(pz[0:64, h:h + 1], lw[:, h * 64:(h + 1) * 64], self.ones_f.ap, True, True, [lw, self.ones_f], [pz])
            self.ACT(wc.ap, pz[0:64, 0:H], AF.Exp, [pz], [wc])

        def heads_group(t, p, g0):
            tA, tR, tB, tK, hB, hK, vb, wc = (PB[p][k] for k in ('tA', 'tR', 'tB', 'tK', 'hB', 'hK', 'vb', 'wc'))
            heads = list(range(g0, min(H, g0 + G)))
            for i, h in enumerate(heads):
                hs = slice(h * 64, (h + 1) * 64)
                bk, tt = bank[i], TTh[i]
                pv = bk[0:64, 0:256].bitcast(BF16)
                for q, src in enumerate((tA, tR, tB, tK)):
                    self.TR(pv[:, q * 128:(q + 1) * 128], src[:, hs], self.identb.ap, [src, self.identb], [bk])
                self.CP("act", tt.ap, pv, [bk], [tt])
            for i, h in enumerate(heads):
                bk, tt, s1 = bank[i], TTh[i], SC1[i]
                self.MM(bk[:, 0:128], tt[:, 0:128], tt[:, 256:384], True, True, [tt], [bk])
                self.MM(bk[:, 128:384], tt[:, 256:384], tt[:, 0:256], True, True, [tt], [bk])
                self.TT_("dve", s1.ap, bk[:, 0:384], m1.ap, ALU.mult, [bk, m1], [s1])
            for i, h in enumerate(heads):
                bk, tt, s2 = bank[i], TTh[i], SC2[i]
                self.MM(bk[:, 0:256], tt[:, 384:512], tt[:, 0:256], True, True, [tt], [bk])
                self.TT_("dve", s2.ap, bk[:, 0:256], m2.ap, ALU.mult, [bk, m2], [s2])
            stt = {}
            for i, h in enumerate(heads):
                stt[i] = [SC1[i][:, 0:256], SC1[i], self.identb.ap, self.identb]
            for lv in range(7):
                for i, h in enumerate(heads):
                    bk = bank[i]
                    xcur, xres, pcur, pres = stt[i]
                    X, XT_ = xcur[:, 0:128], xcur[:, 128:256]
                    self.MM(bk[:, 0:128], X, pcur, True, True, [xres, pres], [bk])
                    if lv < 6:
                        self.MM(bk[:, 128:256], XT_, X, True, True, [xres], [bk])
                        self.MM(bk[:, 256:384], X, XT_, True, True, [xres], [bk])
                    pn = PP[i][lv % 2]
                    self.TT_("dve", pn.ap, bk[:, 0:128], pcur, ALU.add, [bk, pres], [pn])
                    stt[i][2], stt[i][3] = pn.ap, pn
                    if lv < 6:
                        xn_ = XX[i][lv % 2]
                        self.CP("act", xn_.ap, bk[:, 128:384], [bk], [xn_])
                        stt[i][0], stt[i][1] = xn_.ap, xn_
            for i, h in enumerate(heads):
                hs = slice(h * 64, (h + 1) * 64)
                bk = bank[i]
                P, pres = stt[i][2], stt[i][3]
                self.MM(bk[:, 0:64], SC2[i][:, 0:128], vb[:, hs], True, True, [SC2[i], vb], [bk])
                self.MM(bk[0:64, 64:192], tA[:, hs], P, True, True, [tA, pres], [bk])
                self.CP("act", Zb[i].ap, bk[:, 0:64], [bk], [Zb[i]])
                self.CP("dve", AhT[i].ap, bk[0:64, 64:192], [bk], [AhT[i]])
            for i, h in enumerate(heads):
                bk = bank[i]
                P, pres = stt[i][2], stt[i][3]
                self.MM(bk[:, 256:320], P, Zb[i].ap, True, False, [pres, Zb[i]], [bk])
                self.MM(bk[:, 256:320], AhT[i].ap, Mb[h].ap, False, True, [AhT[i], Mb[h]], [bk])
                self.CP("dve", Ub[i].ap, bk[:, 256:320], [bk], [Ub[i]])
            for i, h in enumerate(heads):
                hs = slice(h * 64, (h + 1) * 64)
                bk, tt = bank[i], TTh[i]
                self.MM(bk[:, 320:384], SC2[i][:, 128:256], vb[:, hs], True, False, [SC2[i], vb], [bk])
                self.MM(bk[:, 320:384], tt[:, 128:256], Mb[h].ap, False, False, [tt, Mb[h]], [bk])
                self.MM(bk[:, 320:384], SC1[i][:, 256:384], Ub[i].ap, False, True, [SC1[i], Ub[i]], [bk])
                self.CP("act", yt[:, hs], bk[:, 320:384], [bk], [yt])
            for i, h in enumerate(heads):
                hs = slice(h * 64, (h + 1) * 64)
                bk = bank[i]
                self.MM(bk[0:64, 384:448], hK[:, hs], vb[:, hs], True, False, [hK, vb], [bk])
                self.MM(bk[0:64, 384:448], hB[:, hs], Ub[i].ap, False, True, [hB, Ub[i]], [bk])
                self.STT(Mf[h].ap, Mf[h].ap, wc[:, h:h + 1], bk[0:64, 384:448], ALU.mult, ALU.add, [Mf[h], wc, bk], [Mf[h]])
                self.CP("pool", Mb[h].ap, Mf[h].ap, [Mf[h]], [Mb[h]])

        seqs = [(list(range(self.NPT)), None, self.wkvp.ap)]
        for b in range(NSB):
            seqs.append(([self.NPT + b], b, self.wkvs.ap[b]))
        hcnt = 0
        for tiles, b0, dst in seqs:
            for h in range(H):
                if b0 is None:
                    self.MSET("pool", Mf[h].ap, 0.0, [Mf[h]])
                else:
                    self.LD(st0.ap, di["wkv0"].ap[b0, h], [self.din_res], [st0])
                    pz = pw[h % 4]
                    self.S.op("pe", lambda e, o=pz[0:64, 0:64], i_=st0.ap, idn=self.identf[0:64, 0:64]: e.transpose(o, i_, idn), [st0, self.identf], [pz])
                    self.CP("dve", Mf[h].ap, pz[0:64, 0:64], [pz], [Mf[h]])
                self.CP("pool", Mb[h].ap, Mf[h].ap, [Mf[h]], [Mb[h]])
            par = 0
            prep(tiles[0], par)
            for idx, t in enumerate(tiles):
                rows = slice(t * 128, (t + 1) * 128)
                gl = list(range(0, H, G))
                for gi, g0 in enumerate(gl):
                    if gi == len(gl) - 1 and idx + 1 < len(tiles):
                        prep(tiles[idx + 1], 1 - par)
                    heads_group(t, par, g0)
                par = 1 - par
                self.LD(self.Yd[rows, :], yt.ap, [yt], [self.Yd])
            for h in range(H):
                pz = pw[h % 4]
                self.S.op("pe", lambda e, o=pz[0:64, 0:64], i_=Mf[h].ap, idn=self.identf[0:64, 0:64]: e.transpose(o, i_, idn), [Mf[h], self.identf], [pz])
                self.CP("dve", st0.ap, pz[0:64, 0:64], [pz], [st0])
                self.LD(dst[h], st0.ap, [st0], [self.wkvp if b0 is None else self.wkvs])

    def ph_rwkv_post(self):
        S, di = self.S, self.di
        lwb = self.bcast_row("lwb", di["rwkv_lnx_w"].ap, D)
        lbb = self.bcast_row("lbb", di["rwkv_lnx_b"].ap, D)
        wo = self.load_w("wo", di["rwkv_wo"].ap, D, D)
        y = [S.sb(f"y{i}", [128, D], F32) for i in range(2)]
        g = [S.sb(f"g{i}", [128, D], F32) for i in range(2)]
        v = [S.sb(f"v{i}", [128, D], F32) for i in range(2)]
        h = [S.sb(f"h{i}", [128, D], F32) for i in range(2)]
        bon = [S.sb(f"bon{i}", [128, H], F32) for i in range(2)]
        tmpl = [S.sb(f"tmp{i}", [128, D], F32) for i in range(2)]
        sml = [S.sb(f"sm{i}", [128, 4, H], F32) for i in range(2)]
        ob = [S.sb(f"ob{i}", [128, D], BF16) for i in range(2)]
        xT = [S.sb(f"xT{i}", [128, 8, 128], BF16) for i in range(2)]
        psT = [S.ps(f"psT{i}", [128, D], BF16) for i in range(2)]
        pol = [S.ps(f"po{i}", [128, D]) for i in range(2)]
        v3 = lambda a: a.rearrange("p (h n) -> p h n", h=H)
        bc = lambda a: a.unsqueeze(2).broadcast_to([128, H, 64])

        def tile_ops(t):
            j = t % 2
            rows = slice(t * 128, (t + 1) * 128)
            yy, tmp, sm, po = y[j], tmpl[j], sml[j], pol[j]
            ops = []

            def loads():
                self.LD(y[j].ap, self.Yd[rows, :], [self.Yd], [y[j]])
                self.LD(g[j].ap, self.Gd[rows, :], [self.Gd], [g[j]])
                self.LD(v[j].ap, self.Vd[rows, :], [self.Vd], [v[j]])
                self.LD(h[j].ap, self.Hd[rows, :], [self.Hd], [h[j]])
                self.LD(bon[j].ap, self.BONd[rows, :], [self.BONd], [bon[j]])
                if self.dbg:
                    self.LD(self.dbgY[rows, :], y[j].ap, [y[j]], [self.dbgY])
            ops.append(loads)
            ops.append(lambda: self.RED(sm[:, 0, :], v3(yy.ap), ALU.add, [yy], [sm]))
            ops.append(lambda: self.TS("dve", sm[:, 0, :], sm[:, 0, :], 1.0 / 64, None, ALU.mult, None, [sm], [sm]))
            ops.append(lambda: self.TT_("dve", v3(yy.ap), v3(yy.ap), bc(sm[:, 0, :]), ALU.subtract, [yy, sm], [yy]))
            ops.append(lambda: self.TT_("pool", tmp.ap, yy.ap, yy.ap, ALU.mult, [yy], [tmp]))
            ops.append(lambda: self.RED(sm[:, 1, :], v3(tmp.ap), ALU.add, [tmp], [sm]))
            ops.append(lambda: self.TS("dve", sm[:, 1, :], sm[:, 1, :], 1.0 / 64, 64e-5, ALU.mult, ALU.add, [sm], [sm]))
            ops.append(lambda: self.ACT(sm[:, 1, :], sm[:, 1, :], AF.Sqrt, [sm], [sm]))
            ops.append(lambda: self.RCP(sm[:, 2, :], sm[:, 1, :], [sm], [sm]))
            ops.append(lambda: self.TT_("pool", v3(tmp.ap), v3(v[j].ap), bc(bon[j].ap), ALU.mult, [v[j], bon[j]], [tmp]))
            ops.append(lambda: self.TT_("dve", v3(yy.ap), v3(yy.ap), bc(sm[:, 2, :]), ALU.mult, [yy, sm], [yy]))
            ops.append(lambda: self.TT_("pool", yy.ap, yy.ap, lwb.ap, ALU.mult, [yy, lwb], [yy]))
            ops.append(lambda: self.TT_("dve", yy.ap, yy.ap, lbb.ap, ALU.add, [yy, lbb], [yy]))
            ops.append(lambda: self.TT_("pool", yy.ap, yy.ap, tmp.ap, ALU.add, [yy, tmp], [yy]))
            ops.append(lambda: self.TT_("dve", ob[j].ap, yy.ap, g[j].ap, ALU.mult, [yy, g[j]], [ob[j]]))
            ops.append(lambda: self.transpose_cols(ob[j], D, xT[j].ap, xT[j], psT[j], eng="act"))

            def mm():
                for hh in range(2):
                    cs = slice(hh * 512, (hh + 1) * 512)
                    for k in range(8):
                        self.MM(po[:, cs], xT[j][:, k, :], wo[:, k, cs], k == 0, k == 7, [xT[j], wo], [po])
            ops.append(mm)
            ops.append(lambda: self.TT_("dve", h[j].ap, po.ap, h[j].ap, ALU.add, [po, h[j]], [h[j]]))
            ops.append(lambda: self.LD(self.Hd[rows, :], h[j].ap, [h[j]], [self.Hd]))
            return ops

        for t0_ in range(0, self.TT, 2):
            lists = [tile_ops(t) for t in range(t0_, min(self.TT, t0_ + 2))]
            for k in range(len(lists[0])):
                for l in lists:
                    l[k]()

    def ph_kv(self):
        S, di = self.S, self.di
        nw = self.bcast_row("nw", di["kv_norm"].ap, D)
        wkv = self.load_w("wkv", di["w_kv"].ap, D, 512)
        h = [S.sb(f"h{i}", [128, D], F32) for i in range(2)]
        xn = [S.sb(f"xn{i}", [128, D], BF16) for i in range(2)]
        xT = [S.sb(f"xT{i}", [128, 8, 128], BF16) for i in range(2)]
        kv = [S.sb(f"kv{i}", [128, 512], F32) for i in range(2)]
        kvb = [S.sb(f"kvb{i}", [128, 512], BF16) for i in range(2)]
        ktb = [S.sb(f"ktb{i}", [64, 4, 128], BF16) for i in range(2)]
        scr = S.sb("scr", [128, D], F32)
        st = [S.sb(f"st{i}", [128, 4], F32) for i in range(2)]
        cb = [S.sb(f"cb{i}", [128, 256], BF16) for i in range(2)]
        psT = [S.ps(f"psT{i}", [128, D], BF16) for i in range(2)]
        pk = S.ps("pk", [128, 512])
        pkt = [S.ps(f"pkt{i}", [64, 512], BF16) for i in range(2)]
        for t in range(self.TT):
            j = t % 2
            rows = slice(t * 128, (t + 1) * 128)
            self.LD(h[j].ap, self.Hd[rows, :], [self.Hd], [h[j]])
            self.norm_tile(h[j], nw, xn[j], scr, st[j])
            self.transpose_cols(xn[j], D, xT[j].ap, xT[j], psT[j])
            for k in range(8):
                self.MM(pk.ap, xT[j][:, k, :], wkv[:, k, :], k == 0, k == 7, [xT[j], wkv], [pk])
            self.CP("act", kv[j].ap, pk.ap, [pk], [kv[j]])
            self.CP("dve", kvb[j].ap, pk.ap, [pk], [kvb[j]])
            self.LD(self.kvout[rows, :], kv[j].ap, [kv[j]], [self.kvout])
            for hd in range(4):
                self.TR(pkt[j][:, hd * 128:(hd + 1) * 128], kvb[j][:, hd * 64:(hd + 1) * 64], self.identb.ap, [kvb[j], self.identb], [pkt[j]])
            self.CP("act", ktb[j].ap, pkt[j].ap.rearrange("p (h t) -> p h t", h=4), [pkt[j]], [ktb[j]])
            if t < self.NPT:
                self.LD(self.KTp[0:64, :, MAXW + t * 128:MAXW + (t + 1) * 128], ktb[j].ap, [ktb[j]], [self.KTp])
                self.LD(self.VDp[MAXW + t * 128:MAXW + (t + 1) * 128, :], kvb[j][:, 256:512], [kvb[j]], [self.VDp])
            else:
                b = t - self.NPT
                self.LD(self.KTs[b, 0:64, :, MAXW:MAXW + 8], ktb[j][:, :, 0:8], [ktb[j]], [self.KTs])
                self.LD(self.VDs[b, MAXW:MAXW + 8, :], kvb[j][0:8, 256:512], [kvb[j]], [self.VDs])
        cnt = 0
        for b in range(NSB):
            S.dma("pool", self.VDs[b, 0:MAXW, :], di["cache"].ap[b, :, 256:512], [self.din_res], [self.VDs])
            for r0 in range(0, MAXW, 128):
                j = cnt % 2
                cnt += 1
                S.dma("pool", cb[j].ap, di["cache"].ap[b, r0:r0 + 128, 0:256], [self.din_res], [cb[j]])
                for hd in range(4):
                    self.TR(pkt[j][:, hd * 128:(hd + 1) * 128], cb[j][:, hd * 64:(hd + 1) * 64], self.identb.ap, [cb[j], self.identb], [pkt[j]])
                self.CP("act" if j else "dve", ktb[j].ap, pkt[j].ap.rearrange("p (h t) -> p h t", h=4), [pkt[j]], [ktb[j]])
                self.LD(self.KTs[b, 0:64, :, r0:r0 + 128], ktb[j].ap, [ktb[j]], [self.KTs])

    def ph_select(self):
        S, di = self.S, self.di
        NH = self.NH
        sel = S.sb("sel", [128, 2], F32)
        self.LD(sel.ap, di["sel"].ap, [self.din_res], [sel])
        NB_ = 4 if NH % 4 == 0 else 1
        a = [S.sb(f"a{i}", [128, NB_, D], F32) for i in range(2)]
        b = [S.sb(f"b{i}", [128, NB_, D], F32) for i in range(2)]
        for ii, i in enumerate(range(0, NH, NB_)):
            j = ii % 2
            ra = self.Hd[i * 128:(i + NB_) * 128, :].rearrange("(n p) d -> p n d", p=128)
            rb = self.Hd[(NH + i) * 128:(NH + i + NB_) * 128, :].rearrange("(n p) d -> p n d", p=128)
            self.LD(a[j].ap, ra, [self.Hd], [a[j]])
            self.LD(b[j].ap, rb, [self.Hd], [b[j]])
            self.TS("pool", a[j].ap, a[j].ap, sel[:, 0:1], None, ALU.mult, None, [a[j], sel], [a[j]])
            self.STT(a[j].ap, b[j].ap, sel[:, 1:2], a[j].ap, ALU.mult, ALU.add, [b[j], sel, a[j]], [a[j]])
            self.LD(self.Hm[i * 128:(i + NB_) * 128, :].rearrange("(n p) d -> p n d", p=128), a[j].ap, [a[j]], [self.Hm])
        W = MAXW + NH * 128
        off = NH * 128
        ka = [S.sb(f"ka{i}", [65, W], BF16) for i in range(2)]
        kb = [S.sb(f"kb{i}", [65, W], BF16) for i in range(2)]
        for hd in range(4):
            j = hd % 2
            self.LD(ka[j].ap, self.KTp[:, hd, 0:W], [self.KTp], [ka[j]])
            self.LD(kb[j].ap, self.KTp[:, hd, off:off + W], [self.KTp], [kb[j]])
            self.TS("pool", ka[j].ap, ka[j].ap, sel[0:65, 0:1], None, ALU.mult, None, [ka[j], sel], [ka[j]])
            self.STT(ka[j].ap, kb[j].ap, sel[0:65, 1:2], ka[j].ap, ALU.mult, ALU.add, [kb[j], sel, ka[j]], [ka[j]])
            self.LD(self.KTm[:, hd, :], ka[j].ap, [ka[j]], [self.KTm])
        nvt = W // 128
        VB = 8 if nvt % 8 == 0 else 1
        va = [S.sb(f"va{i}", [128, VB, 256], BF16) for i in range(2)]
        vb_ = [S.sb(f"vb{i}", [128, VB, 256], BF16) for i in range(2)]
        for ii, i in enumerate(range(0, nvt, VB)):
            j = ii % 2
            self.LD(va[j].ap, self.VDp[i * 128:(i + VB) * 128, :].rearrange("(n p) d -> p n d", p=128), [self.VDp], [va[j]])
            self.LD(vb_[j].ap, self.VDp[off + i * 128:off + (i + VB) * 128, :].rearrange("(n p) d -> p n d", p=128), [self.VDp], [vb_[j]])
            self.TS("pool", va[j].ap, va[j].ap, sel[:, 0:1], None, ALU.mult, None, [va[j], sel], [va[j]])
            self.STT(va[j].ap, vb_[j].ap, sel[:, 1:2], va[j].ap, ALU.mult, ALU.add, [vb_[j], sel, va[j]], [va[j]])
            self.LD(self.VDm[i * 128:(i + VB) * 128, :].rearrange("(n p) d -> p n d", p=128), va[j].ap, [va[j]], [self.VDm])

    def ph_select_s(self):
        S, di = self.S, self.di
        NH = self.NH
        sel = S.sb("sel", [128, 2], F32)
        self.LD(sel.ap, di["sel"].ap, [self.din_res], [sel])
        a = [S.sb(f"a{i}", [128, D], F32) for i in range(2)]
        b = [S.sb(f"b{i}", [128, D], F32) for i in range(2)]
        NS1 = self.NS1
        for s_ in range(NS1):
            j = s_ % 2
            r0, r1 = (self.NPT + s_) * 128, (self.NPT + NS1 + s_) * 128
            self.LD(a[j].ap, self.Hd[r0:r0 + 128, :], [self.Hd], [a[j]])
            self.LD(b[j].ap, self.Hd[r1:r1 + 128, :], [self.Hd], [b[j]])
            self.TS("pool", a[j].ap, a[j].ap, sel[:, 0:1], None, ALU.mult, None, [a[j], sel], [a[j]])
            self.STT(a[j].ap, b[j].ap, sel[:, 1:2], a[j].ap, ALU.mult, ALU.add, [b[j], sel, a[j]], [a[j]])
            self.LD(self.Hm[(NH + s_) * 128:(NH + s_ + 1) * 128, :], a[j].ap, [a[j]], [self.Hm])
        WS = MAXW + 8
        ksa = [S.sb(f"ksa{i}", [65, 4, WS], BF16) for i in range(2)]
        ksb = [S.sb(f"ksb{i}", [65, 4, WS], BF16) for i in range(2)]
        vsa = [S.sb(f"vsa{i}", [128, 17, 256], BF16) for i in range(2)]
        vsb = [S.sb(f"vsb{i}", [128, 17, 256], BF16) for i in range(2)]
        for s_ in range(NS1):
            j = s_ % 2
            self.LD(ksa[j].ap, self.KTs[s_], [self.KTs], [ksa[j]])
            self.LD(ksb[j].ap, self.KTs[NS1 + s_], [self.KTs], [ksb[j]])
            self.TS("pool", ksa[j].ap, ksa[j].ap, sel[0:65, 0:1], None, ALU.mult, None, [ksa[j], sel], [ksa[j]])
            self.STT(ksa[j].ap, ksb[j].ap, sel[0:65, 1:2], ksa[j].ap, ALU.mult, ALU.add, [ksb[j], sel, ksa[j]], [ksa[j]])
            self.LD(self.KTsm[s_], ksa[j].ap, [ksa[j]], [self.KTsm])
            for (src, dstt) in ((self.VDs[s_], vsa[j]), (self.VDs[NS1 + s_], vsb[j])):
                self.LD(dstt[:, 0:16, :], src[0:MAXW, :].rearrange("(n p) d -> p n d", p=128), [self.VDs], [dstt])
                self.LD(dstt[0:8, 16, :], src[MAXW:MAXW + 8, :], [self.VDs], [dstt])
            for reg in (lambda t_: t_[:, 0:16, :], lambda t_: t_[0:8, 16, :]):
                pa_ = 128 if reg(vsa[j]).shape[0] == 128 else 8
                self.TS("pool", reg(vsa[j]), reg(vsa[j]), sel[0:pa_, 0:1], None, ALU.mult, None, [vsa[j], sel], [vsa[j]])
                self.STT(reg(vsa[j]), reg(vsb[j]), sel[0:pa_, 1:2], reg(vsa[j]), ALU.mult, ALU.add, [vsb[j], sel, vsa[j]], [vsa[j]])
            self.LD(self.VDsm[s_, 0:MAXW, :].rearrange("(n p) d -> p n d", p=128), vsa[j][:, 0:16, :], [vsa[j]], [self.VDsm])
            self.LD(self.VDsm[s_, MAXW:MAXW + 8, :], vsa[j][0:8, 16, :], [vsa[j]], [self.VDsm])

    def attn_cfgs(self):
        B = self.BLK
        c = {"p0": (1, 128, 1, 0), "p1": (4, 32, 4, 1), "p2": (16, B // 16, 16, 2),
             "s0": (1, 8, 1, 0), "s1": (4, 2, 4, 1), "s2": (16, 1, 8, 2)}
        return c

    def attn_group(self, name, QT, qcol0, KT, kpos0, Vsrc, vrow0, ACC, acol0, first, bufs):
        d, nq, nres, g = self.cfgs[name]
        bias = self.biasT[name]
        nk = nq + 128
        ktl = [(0, 128), (128, nk)]
        vt, pt_, tmpf, pS, pO = bufs
        vres = self.vres
        merged = 8 * nq <= 512
        W4 = 4 * nq
        units = []
        for rho in range(nres):
            shared = {}
            for kvh in range(4):
                st = {}

                def A(rho=rho, kvh=kvh, st=st, shared=shared):
                    if kvh == 0:
                        vts = []
                        for ki, (j0, j1) in enumerate(ktl):
                            vtile = vt[self.vcnt % len(vt)]
                            self.vcnt += 1
                            r0 = vrow0 + rho + d * (j0 - 128)
                            src = Vsrc[r0:r0 + d * (j1 - j0 - 1) + 1:d, :] if d > 1 else Vsrc[r0:r0 + (j1 - j0), :]
                            self.LD(vtile[0:j1 - j0, :], src, [vres], [vtile])
                            vts.append(vtile)
                        shared["vts"] = vts
                    q0 = qcol0 + rho
                    qs = QT[:, g, kvh * 4:(kvh + 1) * 4, q0:q0 + d * (nq - 1) + 1:d] if d > 1 else QT[:, g, kvh * 4:(kvh + 1) * 4, q0:q0 + nq]
                    pts = []
                    if merged:
                        c = self.acnt % len(pS)
                        self.acnt += 1
                        ps_, tf = pS[c], tmpf[c % len(tmpf)]
                        p_ = pt_[self.pcnt % len(pt_)]
                        self.pcnt += 1
                    for ki, (j0, j1) in enumerate(ktl):
                        nkk = j1 - j0
                        if not merged:
                            c = self.acnt % len(pS)
                            self.acnt += 1
                            ps_, tf = pS[c], tmpf[c % len(tmpf)]
                            p_ = pt_[self.pcnt % len(pt_)]
                            self.pcnt += 1
                        co = ki * W4 if merged else 0
                        k0 = kpos0 + rho + d * (j0 - 128)
                        kslice = KT[:, kvh, k0:k0 + d * (nkk - 1) + 1:d] if d > 1 else KT[:, kvh, k0:k0 + nkk]
                        out = ps_[0:nkk, co:co + W4].rearrange("p (h q) -> p h q", h=4)
                        self.MM(out, kslice, qs, True, True, [KT, QT], [ps_])
                        if not merged:
                            self.TT_("dve", tf[0:nkk, 0:W4], ps_[0:nkk, 0:W4], bias[0:nkk, kvh, ki, :], ALU.add, [ps_, bias], [tf])
                            self.ACT(p_[0:nkk, 0:W4], tf[0:nkk, 0:W4], AF.Exp, [tf], [p_])
                        pts.append((p_, nkk, co))
                    if merged:
                        self.TT_("dve", tf[:, 0:2 * W4], ps_[:, 0:2 * W4], bias[:, kvh, :, :].rearrange("p a c -> p (a c)"), ALU.add, [ps_, bias], [tf])
                        self.ACT(p_[:, 0:2 * W4], tf[:, 0:2 * W4], AF.Exp, [tf], [p_])
                    st["pts"] = pts

                def B(rho=rho, kvh=kvh, st=st, shared=shared):
                    pts, vts = st["pts"], shared["vts"]
                    if merged:
                        hf = self.ocnt % 2
                        self.ocnt += 1
                        po_ = pO[0][0:64, hf * 512:(hf + 1) * 512]
                        pres = [pO[0].part(hf)]
                    else:
                        po_ = pO[0][0:64, :]
                        pres = [pO[0].part(0), pO[0].part(1)]
                    for ki, (p_, nkk, co) in enumerate(pts):
                        self.MM(po_[:, 0:W4], vts[ki][0:nkk, kvh * 64:(kvh + 1) * 64], p_[0:nkk, co:co + W4], ki == 0, ki == 1, [vts[ki], p_], pres)
                    for ki, (p_, nkk, co) in enumerate(pts):
                        self.MM(po_[:, W4:2 * W4], self.ones_b[0:nkk, :], p_[0:nkk, co:co + W4], ki == 0, ki == 1, [self.ones_b, p_], pres)
                    a0 = acol0 + rho
                    dst = ACC[:, :, kvh * 4:(kvh + 1) * 4, a0:a0 + d * (nq - 1) + 1:d] if d > 1 else ACC[:, :, kvh * 4:(kvh + 1) * 4, a0:a0 + nq]
                    srcp = po_[:, 0:2 * W4].rearrange("p (a h q) -> p a h q", a=2, h=4)
                    if first:
                        self.CP("dve", dst, srcp, pres, [ACC])
                    else:
                        self.TT_("dve", dst, srcp, dst, ALU.add, pres + [ACC], [ACC])
                units.append((A, B))
        return units

    def run_units(self, units, L=3):
        n = len(units)
        for u in range(min(L, n)):
            units[u][0]()
        for u in range(n):
            if u + L < n:
                units[u + L][0]()
            units[u][1]()

    def ph_attn(self):
        S, di = self.S, self.di
        BLK = self.BLK
        nw = self.bcast_row("nw", di["norm_w"].ap[1, 1], D)
        wo = S.sb("wo", [64, H, D], BF16)
        S.dma("pool", wo.ap, di["attn_wo"].ap.rearrange("(h p) c -> p h c", p=64), [self.din_res], [wo])
        self.biasT = {}
        for name, (d, nq, nres, g) in self.cfgs.items():
            bt = S.sb(f"bias_{name}", [128, 4, 2, 4 * nq], F32)
            self.MSET("pool", bt.ap, 0.0, [bt])
            src = di[f"bias_{name}"].ap
            self.LD(bt[:, :, 0, :], src[:, 0:128, :].rearrange("k j c -> j k c"), [self.din_res], [bt])
            self.LD(bt[0:nq, :, 1, :], src[:, 128:128 + nq, :].rearrange("k j c -> j k c"), [self.din_res], [bt])
            self.biasT[name] = bt
        KT = S.sb("KT", [65, 4, MAXW + BLK], BF16)
        QT = S.sb("QT", [65, 3, H, BLK], BF16)
        self.MSET("pool", QT[64:65, :, :, :], 1.0, [QT])
        ACC = S.sb("ACC", [64, 2, H, BLK], F32)
        fin = S.sb("fin", [64, H, BLK], BF16)
        h = [S.sb(f"h{i}", [128, D], F32) for i in range(2)]
        xn = [S.sb(f"xn{i}", [128, D], BF16) for i in range(2)]
        xT = S.sb("xT", [128, 8, BLK], BF16)
        scr = S.sb("scr", [128, D], F32)
        st = [S.sb(f"st{i}", [128, 4], F32) for i in range(2)]
        vt = [S.sb(f"vt{i}", [128, 256], BF16) for i in range(8)]
        pt_ = [S.sb(f"pt{i}", [128, 512], BF16) for i in range(8)]
        tmpf = [S.sb(f"tf{i}", [128, 512], F32) for i in range(4)]
        psT1 = S.ps("psT", [128, D], BF16)
        psT = [psT1, psT1]
        pq1 = S.ps("pq", [64, 512])
        pq = [pq1, pq1]
        pS = [S.ps(f"pS{i}", [128, 512]) for i in range(4)]
        pO = [S.ps("pO0", [64, 1024])]
        bufs = (vt, pt_, tmpf, pS, pO)
        self.vcnt = self.acnt = self.pcnt = self.ocnt = 0
        wq = di["attn_wq"].ap
        wqb = [S.sb(f"wqb{i}", [128, 8, 512], BF16) for i in range(2)]

        def qproj(tiles, ncols):
            for i, t in enumerate(tiles):
                j = i % 2
                self.LD(h[j].ap, self.cH[t * 128:(t + 1) * 128, :], [self.cH], [h[j]])
                self.norm_tile(h[j], nw, xn[j], scr, st[j])
                self.transpose_cols(xn[j], D, xT[:, :, i * 128:(i + 1) * 128], xT, psT[j])
            qc = 0
            for cb_ in range(6):
                w = wqb[cb_ % 2]
                S.dma("pool", w.ap, wq[:, cb_ * 512:(cb_ + 1) * 512].rearrange("(k p) c -> p k c", p=128), [self.din_res], [w])
                for hh in range(8):
                    gh = cb_ * 8 + hh
                    g, hd = gh // 16, gh % 16
                    pz = pq[qc % 2]
                    qc += 1
                    for k in range(8):
                        self.MM(pz[:, 0:ncols], w[:, k, hh * 64:(hh + 1) * 64], xT[:, k, 0:ncols], k == 0, k == 7, [w, xT], [pz])
                    self.ACT(QT[0:64, g, hd, 0:ncols], pz[:, 0:ncols], AF.Copy, [pz], [QT], scale=0.125)

        def finish(tiles, ncols_valid):
            self.RCP(ACC[:, 1, :, 0:ncols_valid], ACC[:, 1, :, 0:ncols_valid], [ACC], [ACC])
            self.TT_("dve", fin[:, :, 0:ncols_valid], ACC[:, 0, :, 0:ncols_valid], ACC[:, 1, :, 0:ncols_valid], ALU.mult, [ACC], [fin])
            for i, t in enumerate(tiles):
                j = i % 2
                nv = min(128, ncols_valid - i * 128)
                self.LD(h[j].ap, self.cH[t * 128:(t + 1) * 128, :], [self.cH], [h[j]])
                for hh in range(2):
                    cs = slice(hh * 512, (hh + 1) * 512)
                    for hd in range(H):
                        self.MM(pS[hh][0:nv, :], fin[:, hd, i * 128:i * 128 + nv], wo[:, hd, cs], hd == 0, hd == H - 1, [fin, wo], [pS[hh]])
                    self.TT_("dve", h[j][0:nv, cs], pS[hh][0:nv, :], h[j][0:nv, cs], ALU.add, [pS[hh], h[j]], [h[j]])
                self.LD(self.cH[t * 128:(t + 1) * 128, :], h[j].ap, [h[j]], [self.cH])

        self.vres = self.cVD
        nblk = (self.cNPT * 128) // BLK
        tpb = BLK // 128
        for bi in range(nblk):
            tiles = list(range(bi * tpb, (bi + 1) * tpb))
            qproj(tiles, BLK)
            base = bi * BLK
            for hd in range(4):
                self.LD(KT[:, hd, :], self.cKT[:, hd, base:base + MAXW + BLK], [self.cKT], [KT])
            units = []
            for si in range(tpb):
                units += self.attn_group("p0", QT, si * 128, KT, MAXW + si * 128, self.cVD.ap, MAXW + base + si * 128, ACC, si * 128, True, bufs)
            for si in range(tpb):
                units += self.attn_group("p1", QT, si * 128, KT, MAXW + si * 128, self.cVD.ap, MAXW + base + si * 128, ACC, si * 128, False, bufs)
            units += self.attn_group("p2", QT, 0, KT, MAXW, self.cVD.ap, MAXW + base, ACC, 0, False, bufs)
            self.run_units(units)
            finish(tiles, BLK)
        for b in range(self.NS1):
            t = self.cNPT + b
            kts = KT
            for hd in range(4):
                self.LD(kts[:, hd, 0:MAXW + 8], self.KTsm[b, :, hd, :], [self.KTsm], [kts])
            self.vres = self.VDsm
            qproj([t], 128)
            units = []
            for gi, nm in enumerate(("s0", "s1", "s2")):
                units += self.attn_group(nm, QT, 0, kts, MAXW, self.VDsm.ap[b], MAXW, ACC, 0, gi == 0, bufs)
            self.run_units(units)
            finish([t], 8)


def t5_buckets(dist):
    d = np.asarray(dist, dtype=np.int64)
    max_exact = 16
    large = max_exact + (np.log(np.maximum(d, 1) / max_exact) / np.log(2048 / max_exact) * (32 - max_exact)).astype(np.int32)
    large = np.minimum(large, 31)
    return np.where(d < max_exact, d, large).astype(np.int32)


def make_bias_tables(rel_bias, cfgs):
    out = {}
    for name, (d, nq, nres, g) in cfgs.items():
        nk = nq + 128
        j = np.arange(nk)[:, None]
        i = np.arange(nq)[None, :]
        m = i - j + 128
        valid = (m >= 0) & (m <= 128)
        bk = t5_buckets(d * np.clip(m, 0, 128))
        tab = np.empty((4, nk, 4, nq), np.float32)
        for kvh in range(4):
            for hq in range(4):
                col = g * 16 + kvh * 4 + hq
                tab[kvh, :, hq, :] = np.where(valid, rel_bias[bk, col], np.float32(NEG))
        out[name] = np.ascontiguousarray(tab.reshape(4, nk, 4 * nq))
    return out


def make_consts():
    s = np.arange(128)[:, None]
    t = np.arange(128)[None, :]
    c = {}
    c["c_ident"] = np.eye(128, dtype=np.float32)
    c["c_tri"] = np.concatenate([(s <= t), (s > t)], 1).astype(np.float32)
    c["c_m1"] = np.concatenate([(t < s), (s < t), (s <= t)], 1).astype(np.float32)
    c["c_m2"] = np.concatenate([(s < t), (s <= t)], 1).astype(np.float32)
    rm = np.zeros((128, 2), np.float32)
    rm[:, 0] = 1.0
    rm[:8, 1] = 1.0
    c["c_rowmask"] = rm
    return c


_CACHE = {}


def get_builder(NPT, dbg=False):
    key = (NPT, dbg)
    if key not in _CACHE:
        b = Builder(NPT, dbg)
        b.build()
        _CACHE[key] = b
    return _CACHE[key]


def core_inputs(b, c, half, x_prompt_seq, x_sample, state_wkv, state_shift, cache_kv, p_prompt_seq, p_sample, weights, consts, bias_tabs):
    NTOK, NP = b.NTOK, b.NP
    xin = np.zeros((NTOK, D), np.float32)
    xin[:NP] = x_prompt_seq
    pin = np.zeros((2, NTOK, 256), np.float32)
    pin[:, :NP] = p_prompt_seq
    for s in range(NSB):
        r0 = NP + s * 128
        xin[r0:r0 + 8] = x_sample[s]
        pin[:, r0:r0 + 8] = p_sample[:, s]
    NHT = b.NH * 128
    pin1 = np.zeros((b.MTOK, 256), np.float32)
    pin1[:NHT] = p_prompt_seq[1, half * NHT:(half + 1) * NHT]
    for s in range(b.NS1):
        pin1[NHT + s * 128:NHT + s * 128 + 8] = p_sample[1, half * b.NS1 + s]
    sel = np.zeros((128, 2), np.float32)
    sel[:, half] = 1.0
    m = {"sel": sel, "pin1": pin1, "xin": xin, "pin": pin, "wkv0": np.ascontiguousarray(state_wkv), "shift0": np.ascontiguousarray(state_shift),
         "cache": np.ascontiguousarray(cache_kv.reshape(NSB, MAXW, 512))}
    m.update(weights)
    m.update(consts)
    for name, tab in bias_tabs.items():
        m[f"bias_{name}"] = tab
    return m


def kernel(x_prompt, x_sample, state_wkv, state_shift, cache_kv, p_prompt, p_sample,
           norm_w, ffn1_wi, ffn1_wo, ffn2_wi, ffn2_wo, pe_proj, pe_gate,
           rwkv_mix, rwkv_wrkv, rwkv_wo, rwkv_w0, rwkv_w1, rwkv_w2, rwkv_a0, rwkv_a1, rwkv_a2,
           rwkv_g1, rwkv_g2, rwkv_kk, rwkv_ka, rwkv_rk, rwkv_lnx_w, rwkv_lnx_b,
           attn_wq, attn_wo, kv_norm, w_kv, rel_bias, final_norm, _dbg=False):
    f = lambda a: np.ascontiguousarray(np.asarray(a, dtype=np.float32))
    x_prompt, x_sample, p_prompt, p_sample = f(x_prompt), f(x_sample), f(p_prompt), f(p_sample)
    state_wkv, state_shift, cache_kv = f(state_wkv), f(state_shift), f(cache_kv)
    B, T, _ = x_prompt.shape
    NPT = T // 128
    b = get_builder(NPT, _dbg)
    weights = {"norm_w": f(norm_w), "ffn1_wi": f(ffn1_wi), "ffn1_wo": f(ffn1_wo), "ffn2_wi": f(ffn2_wi), "ffn2_wo": f(ffn2_wo),
               "pe_proj": f(pe_proj), "pe_gate": f(pe_gate), "rwkv_mix": f(rwkv_mix)[0], "rwkv_wrkv": f(rwkv_wrkv)[0],
               "rwkv_wo": f(rwkv_wo)[0], "rwkv_w0": f(rwkv_w0)[0], "rwkv_w1": f(rwkv_w1)[0], "rwkv_w2": f(rwkv_w2)[0],
               "rwkv_a0": f(rwkv_a0)[0], "rwkv_a1": f(rwkv_a1)[0], "rwkv_a2": f(rwkv_a2)[0], "rwkv_g1": f(rwkv_g1)[0],
               "rwkv_g2": f(rwkv_g2)[0], "rwkv_kk": f(rwkv_kk)[0], "rwkv_ka": f(rwkv_ka)[0], "rwkv_rk": f(rwkv_rk)[0].reshape(-1),
               "rwkv_lnx_w": f(rwkv_lnx_w)[0], "rwkv_lnx_b": f(rwkv_lnx_b)[0], "attn_wq": f(attn_wq)[0], "attn_wo": f(attn_wo)[0],
               "kv_norm": f(kv_norm), "w_kv": f(w_kv), "final_norm": f(final_norm)}
    consts = make_consts()
    bias_tabs = make_bias_tables(f(rel_bias), b.cfgs)
    nsb_total = x_sample.shape[0]
    ngrp = nsb_total // NSB
    in_maps = []
    for c in range(8):
        pb = c % B
        sg = c % ngrp
        sl = slice(sg * NSB, (sg + 1) * NSB)
        in_maps.append(core_inputs(b, c, c // B, x_prompt[pb], x_sample[sl], state_wkv[0, sl], state_shift[0, sl], cache_kv[sl],
                                   p_prompt[:, pb], p_sample[:, sl], weights, consts, bias_tabs))
    res = run_bass_kernel_spmd(b.nc, in_maps, core_ids=list(range(8)))
    R = res.results
    NP = b.NP
    NHT = b.NH * 128
    y_prompt = np.stack([np.concatenate([R[c]["yout"][:NHT], R[c + B]["yout"][:NHT]], 0) for c in range(B)])
    kvw = min(MAXW, T)
    kv_prompt = np.stack([R[c]["kvout"][NP - kvw:NP].reshape(kvw, 2, 4, 64) for c in range(B)])
    wkv_prompt = np.stack([R[c]["wkvp"] for c in range(B)])[None]
    shift_prompt = np.stack([R[c]["shiftall"][NP - 1] for c in range(B)])[None]
    ys, kvs, wks, shs = [], [], [], []
    for sg in range(ngrp):
        r = R[sg]
        for s in range(NSB):
            r0 = NP + s * 128
            rr = R[sg + (s // b.NS1) * B]
            ys.append(rr["yout"][NHT + (s % b.NS1) * 128:NHT + (s % b.NS1) * 128 + 8])
            kvs.append(r["kvout"][r0:r0 + 8].reshape(8, 2, 4, 64))
            shs.append(r["shiftall"][r0 + 7])
        wks.append(r["wkvs"])
    y_sample = np.stack(ys)
    kv_sample = np.stack(kvs)
    wkv_sample = np.concatenate(wks, 0)[None]
    shift_sample = np.stack(shs)[None]
    outs = (y_prompt, y_sample, wkv_prompt, shift_prompt, kv_prompt, wkv_sample, shift_sample, kv_sample)
    outs = tuple(np.ascontiguousarray(o, dtype=np.float32) for o in outs)
    if _dbg:
        return outs, R
    return outs
```

```python
from contextlib import ExitStack
import numpy as np
import ml_dtypes
import concourse.bass as bass
import concourse.mybir as mybir
from concourse.bass_utils import run_bass_kernel_spmd

F32 = mybir.dt.float32
BF16 = mybir.dt.bfloat16
ALU = mybir.AluOpType
AF = mybir.ActivationFunctionType
AX = mybir.AxisListType

D = 1024
DFF = 2816
NFC = DFF // 128
H = 16
NSB = 8
MAXW = 2048
NEG = -1e30
GROUPS = ((128, 1), (512, 4), (2048, 16))


class Res:
    __slots__ = ("name", "w", "r", "psum")

    def __init__(self, name, psum=False):
        self.name = name
        self.w = {}
        self.r = {}
        self.psum = psum


class T:
    def __init__(self, name, ap):
        self.name = name
        self.ap = ap
        self.res = Res(name)
        self._parts = {}

    def part(self, key):
        if key not in self._parts:
            self._parts[key] = Res(f"{self.name}.{key}", self.res.psum)
        return self._parts[key]

    def __getitem__(self, k):
        return self.ap[k]


def _res(x):
    return x.res if isinstance(x, T) else x


class Sched:
    COMPUTE = ("pe", "dve", "act", "pool")
    NDMASEM = 8

    def __init__(self, nc):
        self.nc = nc
        self.gstack = ExitStack()
        self.stack = self.gstack
        self.streams = {k: [] for k in ("pe", "dve", "act", "pool", "sp")}
        self.sems = {}
        self.cnt = {}
        for k in self.COMPUTE:
            self.sems[k] = self.gstack.enter_context(nc.semaphore(f"s_{k}"))
            self.cnt[k] = 0
        self.dq = {}
        for q in ("sp", "pool"):
            sl = []
            for i in range(self.NDMASEM):
                key = f"d_{q}{i}"
                self.sems[key] = self.gstack.enter_context(nc.semaphore(key))
                self.cnt[key] = 0
                sl.append(key)
            self.dq[q] = [sl, 0]
        self.known = {k: {} for k in self.streams}
        self.n_ops = 0
        self.uid = 0

    def sb(self, name, shape, dtype):
        self.uid += 1
        h = self.stack.enter_context(self.nc.sbuf_tensor(f"{name}_{self.uid}", list(shape), dtype))
        return T(name, h[:])

    def ps(self, name, shape, dtype=F32):
        self.uid += 1
        h = self.stack.enter_context(self.nc.psum_tensor(f"{name}_{self.uid}", list(shape), dtype))
        t = T(name, h[:])
        t.res.psum = True
        return t

    def dram(self, name, shape, dtype, kind="Internal"):
        h = self.nc.dram_tensor(name, list(shape), dtype, kind=kind)
        return T(name, h.ap())

    def _collect(self, ekey, reads, writes, is_dma=False):
        deps = {}

        def add(d, same_ok):
            for s, v in d.items():
                if s == ekey and not (same_ok or is_dma):
                    continue
                if deps.get(s, 0) < v:
                    deps[s] = v

        for r in reads:
            add(_res(r).w, same_ok=(ekey != "pe"))
            if _res(r).psum:
                add(_res(r).r, same_ok=False)
        for w in writes:
            add(_res(w).w, same_ok=False)
            add(_res(w).r, same_ok=False)
        kn = self.known[ekey]
        out = []
        for s, v in deps.items():
            if kn.get(s, 0) < v:
                kn[s] = v
                out.append((s, v))
        return out

    def op(self, ekey, fn, reads=(), writes=()):
        waits = self._collect(ekey, reads, writes)
        self.cnt[ekey] += 1
        v = self.cnt[ekey]
        self.streams[ekey].append((waits, fn, (ekey, 1)))
        for r in reads:
            _res(r).r[ekey] = v
        for w in writes:
            _res(w).w[ekey] = v
        self.n_ops += 1

    def dma(self, q, out_ap, in_ap, reads=(), writes=(), **kw):
        sl, i = self.dq[q]
        key = sl[i % self.NDMASEM]
        self.dq[q][1] = i + 1
        waits = self._collect(q, reads, writes, is_dma=True)
        prev = self.cnt[key]
        kn = self.known[q]
        if prev > 0 and kn.get(key, 0) < prev:
            kn[key] = prev
            waits.append((key, prev))
        val = prev + 16
        self.cnt[key] = val

        def fn(e, out_ap=out_ap, in_ap=in_ap, kw=kw):
            return e.dma_start(out=out_ap, in_=in_ap, **kw)

        self.streams[q].append((waits, fn, (key, 16)))
        for r in reads:
            _res(r).r[key] = val
        for w in writes:
            _res(w).w[key] = val
        self.n_ops += 1

    def barrier(self):
        for ekey in self.streams:
            kn = self.known[ekey]
            waits = []
            for s, v in self.cnt.items():
                if s == ekey or v == 0:
                    continue
                if kn.get(s, 0) < v:
                    kn[s] = v
                    waits.append((s, v))
            self.streams[ekey].append((waits, None, None))

    def emit(self):
        nc = self.nc
        sems = self.sems
        streams = self.streams
        with nc.Block() as block:
            def mk(ekey):
                def body(e):
                    for waits, fn, inc in streams[ekey]:
                        for s, v in waits:
                            e.wait_ge(sems[s], v)
                        if fn is not None:
                            fn(e).then_inc(sems[inc[0]], inc[1])
                return body
            block.tensor(mk("pe"))
            block.vector(mk("dve"))
            block.scalar(mk("act"))
            block.gpsimd(mk("pool"))
            block.sync(mk("sp"))
        self.streams = {k: [] for k in streams}


class Builder:
    def __init__(self, NPT, dbg=False):
        self.NPT = NPT
        self.NP = NPT * 128
        self.TT = NPT + NSB
        self.NTOK = self.TT * 128
        self.NH = NPT // 2
        self.NS1 = NSB // 2
        self.MT = self.NH + self.NS1
        self.MTOK = self.MT * 128
        self.BLK = min(256, self.NH * 128)
        self.dbg = dbg
        self.nc = bass.Bass("TRN2", target_bir_lowering=False)
        self.S = Sched(self.nc)
        self.outs = []

    def MM(self, out, lhsT, rhs, start, stop, R, W):
        self.S.op("pe", lambda e: e.matmul(out, lhsT, rhs, start=start, stop=stop), R, W)

    def TR(self, out, in_, ident, R, W):
        self.S.op("pe", lambda e: e.transpose(out, in_, ident), R, W)

    def ACT(self, out, in_, func, R, W, **kw):
        self.S.op("act", lambda e: e.activation(out, in_, func, **kw), R, W)

    def TT_(self, eng, out, a, b, op, R, W):
        self.S.op(eng, lambda e: e.tensor_tensor(out, a, b, op), R, W)

    def TS(self, eng, out, a, s1, s2, op0, op1, R, W):
        if op1 is None:
            self.S.op(eng, lambda e: e.tensor_scalar(out, a, s1, None, op0), R, W)
        else:
            self.S.op(eng, lambda e: e.tensor_scalar(out, a, s1, s2, op0, op1), R, W)

    def STT(self, out, a, s, b, op0, op1, R, W):
        self.S.op("dve", lambda e: e.scalar_tensor_tensor(out, a, s, b, op0, op1), R, W)

    def CP(self, eng, out, in_, R, W):
        if eng == "act":
            self.S.op("act", lambda e: e.activation(out, in_, AF.Copy), R, W)
        else:
            self.S.op(eng, lambda e: e.tensor_copy(out, in_), R, W)

    def RED(self, out, in_, op, R, W):
        self.S.op("dve", lambda e: e.tensor_reduce(out, in_, AX.X, op), R, W)

    def RCP(self, out, in_, R, W):
        self.S.op("dve", lambda e: e.reciprocal(out, in_), R, W)

    def MSET(self, eng, ap, val, W):
        self.S.op(eng, lambda e: e.memset(ap, val), (), W)

    def LD(self, out, in_, R, W, q="sp"):
        self.S.dma(q, out, in_, R, W)

    def phase(self, fn, *a):
        S = self.S
        import os
        self._phi = getattr(self, "_phi", 0) + 1
        lim = int(os.environ.get("K_MAXPH", "999"))
        if self._phi > lim:
            return
        print("PHASE", self._phi, getattr(fn, "__name__", "?"), a, flush=True)
        with ExitStack() as st:
            S.stack = st
            fn(*a)
            S.barrier()
            S.emit()
        S.stack = S.gstack

    def bcast_row(self, name, src_ap_1d, n):
        t = self.S.sb(name, [128, n], F32)
        self.LD(t.ap, src_ap_1d.partition_broadcast(128), [self.din_res], [t])
        return t

    def norm_tile(self, h, nw, out, scr, st):
        self.MSET("dve", st[:, 0:1], 0.0, [st])
        self.ACT(scr.ap, h.ap, AF.Square, [h, st], [scr, st], accum_out=st[:, 0:1])
        self.TS("dve", st[:, 1:2], st[:, 0:1], 1.0 / D, 1e-6, ALU.mult, ALU.add, [st], [st])
        self.ACT(st[:, 1:2], st[:, 1:2], AF.Sqrt, [st], [st])
        self.RCP(st[:, 2:3], st[:, 1:2], [st], [st])
        self.STT(out.ap, h.ap, st[:, 2:3], nw.ap, ALU.mult, ALU.mult, [h, st, nw], [out])

    def transpose_cols(self, src, ncol, dst_ap3, dst_res, psT, eng="dve"):
        kc = ncol // 128
        for k in range(kc):
            self.TR(psT[:, k * 128:(k + 1) * 128], src[:, k * 128:(k + 1) * 128], self.identb.ap, [src, self.identb], [psT])
        self.CP(eng, dst_ap3, psT[:, 0:ncol].rearrange("p (k t) -> p k t", k=kc), [psT], [dst_res])

    def load_w(self, name, w2d, rows, cols, c0=0, c1=None):
        c1 = cols if c1 is None else c1
        kc = rows // 128
        t = self.S.sb(name, [128, kc, c1 - c0], BF16)
        self.LD(t.ap, w2d[:, c0:c1].rearrange("(k p) c -> p k c", p=128), [self.din_res], [t], q="pool")
        return t

    def build(self):
        S, nc = self.S, self.nc
        NTOK, TT, NP, NPT = self.NTOK, self.TT, self.NP, self.NPT
        di = {}
        self.din_res = Res("inputs")

        def inp(name, shape):
            di[name] = S.dram(name, shape, F32, kind="ExternalInput")
            return di[name]

        inp("xin", [NTOK, D]); inp("pin", [2, NTOK, 256]); inp("wkv0", [NSB, H, 64, 64]); inp("shift0", [NSB, D])
        inp("cache", [NSB // 2, MAXW, 512]); inp("sel", [128, 2]); inp("pin1", [self.MTOK, 256])
        inp("norm_w", [2, 4, D]); inp("ffn1_wi", [2, D, 2 * DFF]); inp("ffn1_wo", [2, DFF, D])
        inp("ffn2_wi", [2, D, 2 * DFF]); inp("ffn2_wo", [2, DFF, D]); inp("pe_proj", [2, 256, D]); inp("pe_gate", [2, D, D])
        inp("rwkv_mix", [6, D]); inp("rwkv_wrkv", [3, D, D]); inp("rwkv_wo", [D, D]); inp("rwkv_w0", [D])
        inp("rwkv_w1", [D, 64]); inp("rwkv_w2", [64, D]); inp("rwkv_a0", [D]); inp("rwkv_a1", [D, 64]); inp("rwkv_a2", [64, D])
        inp("rwkv_g1", [D, 160]); inp("rwkv_g2", [160, D]); inp("rwkv_kk", [D]); inp("rwkv_ka", [D]); inp("rwkv_rk", [D])
        inp("rwkv_lnx_w", [D]); inp("rwkv_lnx_b", [D]); inp("attn_wq", [D, 3 * D]); inp("attn_wo", [D, D])
        inp("kv_norm", [D]); inp("w_kv", [D, 512]); inp("final_norm", [D])
        inp("c_ident", [128, 128]); inp("c_tri", [128, 256]); inp("c_m1", [128, 384]); inp("c_m2", [128, 256])
        inp("c_rowmask", [128, 2])
        self.cfgs = self.attn_cfgs()
        for name, (d, nq, nres, g) in self.cfgs.items():
            inp(f"bias_{name}", [4, nq + 128, 4 * nq])
        self.di = di
        def outp(name, shape):
            t = S.dram(name, shape, F32, kind="ExternalOutput")
            self.outs.append(t)
            return t
        self.yout = outp("yout", [self.MTOK, D]); self.kvout = outp("kvout", [NTOK, 512])
        self.wkvp = outp("wkvp", [H, 64, 64]); self.wkvs = outp("wkvs", [NSB, H, 64, 64])
        self.shiftall = outp("shiftall", [NTOK, D])
        self.Hd = S.dram("Hd", [NTOK, D], F32)
        self.Hm = S.dram("Hm", [self.MTOK, D], F32)
        WM = MAXW + self.NH * 128
        self.KTm = S.dram("KTm", [65, 4, WM], BF16); self.VDm = S.dram("VDm", [WM, 256], BF16)
        self.KTsm = S.dram("KTsm", [self.NS1, 65, 4, MAXW + 8], BF16); self.VDsm = S.dram("VDsm", [self.NS1, MAXW + 8, 256], BF16)
        self.HN = S.dram("HN", [NTOK + 1, D], F32)
        self.Rd = S.dram("Rd", [NTOK, D], F32); self.Kd = S.dram("Kd", [NTOK, D], F32); self.Vd = S.dram("Vd", [NTOK, D], F32)
        self.KKd = S.dram("KKd", [NTOK, D], F32); self.Bd = S.dram("Bd", [NTOK, D], F32); self.LWd = S.dram("LWd", [NTOK, D], F32)
        self.Gd = S.dram("Gd", [NTOK, D], F32); self.BONd = S.dram("BONd", [NTOK, H], F32); self.Yd = S.dram("Yd", [NTOK, D], F32)
        self.KTp = S.dram("KTp", [65, 4, MAXW + NP], BF16); self.VDp = S.dram("VDp", [MAXW + NP, 256], BF16)
        self.KTs = S.dram("KTs", [NSB, 65, 4, MAXW + 8], BF16); self.VDs = S.dram("VDs", [NSB, MAXW + 8, 256], BF16)
        if self.dbg:
            self.dbgH = [outp(f"dbgH{i}", [NTOK if i < 4 else self.MTOK, D]) for i in range(8)]
            self.dbgY = outp("dbgY", [NTOK, D])
        self.identb = S.sb("identb", [128, 128], BF16)
        self.identf = S.sb("identf", [128, 128], F32)
        S.dma("pool", self.identb.ap, di["c_ident"].ap, [self.din_res], [self.identb])
        S.dma("sp", self.identf.ap, di["c_ident"].ap, [self.din_res], [self.identf])
        self.zero = S.sb("zero", [128, D], F32)
        self.MSET("pool", self.zero.ap, 0.0, [self.zero])
        self.ones_b = S.sb("ones_b", [128, 64], BF16)
        self.MSET("pool", self.ones_b.ap, 1.0, [self.ones_b])
        self.ones_f = S.sb("ones_f", [128, 1], F32)
        self.MSET("pool", self.ones_f.ap, 1.0, [self.ones_f])
        self.rowmask = S.sb("rowmask", [128, 2], F32)
        S.dma("sp", self.rowmask.ap, di["c_rowmask"].ap, [self.din_res], [self.rowmask])

        self.ndbg = 0
        self.cH, self.cTT, self.cNPT, self.cKT, self.cVD = self.Hd, self.TT, self.NPT, self.KTp, self.VDp
        self.cPin = lambda layer: di["pin"].ap[layer]
        self.phase(self.ph_init)
        for layer in range(2):
            if layer == 1:
                self.phase(self.ph_kv)
                self.phase(self.ph_select)
                self.phase(self.ph_select_s)
                self.cH, self.cTT, self.cNPT, self.cKT, self.cVD = self.Hm, self.MT, self.NH, self.KTm, self.VDm
                self.cPin = lambda layer: di["pin1"].ap
            self.phase(self.ph_ffn, layer, 0)
            self.snap()
            if layer == 0:
                self.phase(self.ph_rwkv_proj)
                self.phase(self.ph_rwkv_scan)
                self.phase(self.ph_rwkv_post)
            else:
                self.phase(self.ph_attn)
            self.snap()
            self.phase(self.ph_ffn, layer, 2)
            self.snap()
            self.phase(self.ph_pe, layer)
            self.snap()
        return nc

    def snap(self):
        if not self.dbg:
            return
        i = self.ndbg
        self.ndbg += 1

        def f():
            self.S.dma("sp", self.dbgH[i].ap, self.cH.ap, [self.cH], [self.dbgH[i]])
        self.phase(f)

    def ph_init(self):
        S = self.S
        S.dma("sp", self.Hd.ap, self.di["xin"].ap, [self.din_res], [self.Hd])
        S.dma("sp", self.HN[0:1, :], self.zero[0:1, :], [self.zero], [self.HN])
        zb = S.sb("zb", [128, 2048], BF16)
        self.MSET("dve", zb.ap, 0.0, [zb])
        nb = S.sb("nb", [128, 4 * MAXW], BF16)
        self.MSET("dve", nb.ap, NEG, [nb])
        for h in range(4):
            S.dma("sp", self.KTp[0:64, h, 0:MAXW], zb[0:64, :], [zb], [self.KTp])
        S.dma("sp", self.KTp[64:65, :, 0:MAXW], nb[0:1, :].rearrange("p (h n) -> p h n", h=4), [nb], [self.KTp])
        for h in range(4):
            for c0 in range(0, self.NP, 2048):
                c1 = min(self.NP, c0 + 2048)
                S.dma("sp", self.KTp[64:65, h, MAXW + c0:MAXW + c1], zb[0:1, 0:c1 - c0], [zb], [self.KTp])
        for r0 in range(0, MAXW, 128):
            S.dma("sp", self.VDp[r0:r0 + 128, :], zb[:, 0:256], [zb], [self.VDp])
        for b in range(self.NS1):
            for h in range(4):
                S.dma("sp", self.KTsm[b, 64:65, h, 0:2048], zb[0:1, 0:2048], [zb], [self.KTsm])
                S.dma("sp", self.KTsm[b, 64:65, h, 2048:2056], zb[0:1, 0:8], [zb], [self.KTsm])

    def ph_ffn(self, layer, which):
        S = self.S
        wi = self.di["ffn1_wi" if which == 0 else "ffn2_wi"].ap[layer]
        wo = self.di["ffn1_wo" if which == 0 else "ffn2_wo"].ap[layer]
        nw = self.bcast_row("nw", self.di["norm_w"].ap[layer, which], D)
        GT = 8
        groups = [list(range(s, min(self.cTT, s + GT))) for s in range(0, self.cTT, GT)]
        maxg = max(len(g) for g in groups)
        hg = [S.sb(f"hg{i}", [128, D], F32) for i in range(maxg)]
        xT = S.sb("xT", [128, 8, maxg * 128], BF16)
        hid = S.sb("hid", [128, NFC, maxg * 128], BF16)
        wob = S.sb("wob", [128, NFC, D], BF16)
        BW = 512
        blocks = [(c0, min(DFF, c0 + BW)) for c0 in range(0, DFF, BW)]
        wib = [S.sb(f"wib{i}", [128, 8, 2 * BW], BF16) for i in range(2)]
        xn = [S.sb(f"xn{i}", [128, D], BF16) for i in range(2)]
        scr = S.sb("scr", [128, D], F32)
        st = [S.sb(f"st{i}", [128, 4], F32) for i in range(2)]
        sg = [S.sb(f"sg{i}", [128, 512], BF16) for i in range(2)]
        psT = [S.ps(f"psT{i}", [128, D], BF16) for i in range(2)]
        pg = [S.ps(f"pg{i}", [128, 512]) for i in range(2)]
        pu = [S.ps(f"pu{i}", [128, 512]) for i in range(2)]
        po = S.ps("po", [128, D])
        cnt = 0
        nload = [0]
        fuse_norm = (layer == 0 and which == 0)
        if fuse_norm:
            nw1 = self.bcast_row("nw1", self.di["norm_w"].ap[0, 1], D)
            hno = [S.sb(f"hno{i}", [128, D], F32) for i in range(2)]

        def load_block(bi):
            c0, c1 = blocks[bi]
            w = wib[nload[0] % 2]
            nload[0] += 1
            bw = c1 - c0
            S.dma("pool", w[:, :, 0:bw], wi[:, c0:c1].rearrange("(k p) c -> p k c", p=128), [self.din_res], [w])
            S.dma("pool", w[:, :, BW:BW + bw], wi[:, DFF + c0:DFF + c1].rearrange("(k p) c -> p k c", p=128), [self.din_res], [w])
            return w

        pending = load_block(0)
        for gi, g in enumerate(groups):
            ng = len(g)
            ntok = ng * 128
            for i, t in enumerate(g):
                self.LD(hg[i].ap, self.cH[t * 128:(t + 1) * 128, :], [self.cH], [hg[i]])
                self.norm_tile(hg[i], nw, xn[i % 2], scr, st[i % 2])
                self.transpose_cols(xn[i % 2], D, xT[:, :, i * 128:(i + 1) * 128], xT, psT[i % 2])
            for bi, (c0, c1) in enumerate(blocks):
                w = pending
                if bi + 1 < len(blocks):
                    pending = load_block(bi + 1)
                elif gi + 1 < len(groups):
                    pending = load_block(0)
                if bi == 1:
                    S.dma("pool", wob.ap, wo.rearrange("(k p) c -> p k c", p=128), [self.din_res], [wob])
                for cc in range((c1 - c0) // 128):
                    c = c0 // 128 + cc
                    for n0 in range(0, ntok, 512):
                        n1 = min(ntok, n0 + 512)
                        nn = n1 - n0
                        j = cnt % 2
                        cnt += 1
                        for k in range(8):
                            self.MM(pg[j][:, 0:nn], w[:, k, cc * 128:(cc + 1) * 128], xT[:, k, n0:n1], k == 0, k == 7, [w, xT], [pg[j]])
                        for k in range(8):
                            self.MM(pu[j][:, 0:nn], w[:, k, BW + cc * 128:BW + (cc + 1) * 128], xT[:, k, n0:n1], k == 0, k == 7, [w, xT], [pu[j]])
                        self.ACT(sg[j][:, 0:nn], pg[j][:, 0:nn], AF.Silu, [pg[j]], [sg[j]])
                        self.TT_("dve", hid[:, c, n0:n1], pu[j][:, 0:nn], sg[j][:, 0:nn], ALU.mult, [pu[j], sg[j]], [hid])
            for i, t in enumerate(g):
                for hh in range(2):
                    for c in range(NFC):
                        self.MM(po[:, hh * 512:(hh + 1) * 512], hid[:, c, i * 128:(i + 1) * 128], wob[:, c, hh * 512:(hh + 1) * 512],
                                c == 0, c == NFC - 1, [hid, wob], [po])
                self.STT(hg[i].ap, po.ap, 0.5, hg[i].ap, ALU.mult, ALU.add, [po, hg[i]], [hg[i]])
                self.LD(self.cH[t * 128:(t + 1) * 128, :], hg[i].ap, [hg[i]], [self.cH])
                if fuse_norm:
                    o_ = hno[i % 2]
                    self.norm_tile(hg[i], nw1, o_, scr, st[i % 2])
                    self.LD(self.HN[1 + t * 128:1 + (t + 1) * 128, :], o_.ap, [o_], [self.HN])
                    self.LD(self.shiftall[t * 128:(t + 1) * 128, :], o_.ap, [o_], [self.shiftall])

    def ph_pe(self, layer):
        S = self.S
        nw = self.bcast_row("nw", self.di["norm_w"].ap[layer, 3], D)
        wg = self.load_w("wg", self.di["pe_gate"].ap[layer], D, D)
        wp = self.load_w("wp", self.di["pe_proj"].ap[layer], 256, D)
        h = [S.sb(f"h{i}", [128, D], F32) for i in range(2)]
        pb = [S.sb(f"pb{i}", [128, 256], BF16) for i in range(2)]
        xn = [S.sb(f"xn{i}", [128, D], BF16) for i in range(2)]
        xT = [S.sb(f"xT{i}", [128, 8, 128], BF16) for i in range(2)]
        pT = [S.sb(f"pT{i}", [128, 2, 128], BF16) for i in range(2)]
        sgm = [S.sb(f"sgm{i}", [128, D], F32) for i in range(2)]
        scr = S.sb("scr", [128, D], F32)
        st = [S.sb(f"st{i}", [128, 4], F32) for i in range(2)]
        psT = [S.ps(f"psT{i}", [128, D], BF16) for i in range(2)]
        pg = S.ps("pg", [128, D])
        pe = S.ps("pe", [128, D])
        fuse_final = (layer == 1)
        if fuse_final:
            nwf = self.bcast_row("nwf", self.di["final_norm"].ap, D)
            fo = [S.sb(f"fo{i}", [128, D], F32) for i in range(2)]
        for t in range(self.cTT):
            j = t % 2
            rows = slice(t * 128, (t + 1) * 128)
            self.LD(h[j].ap, self.cH[rows, :], [self.cH], [h[j]])
            S.dma("pool", pb[j].ap, self.cPin(layer)[rows, :], [self.din_res], [pb[j]])
            self.norm_tile(h[j], nw, xn[j], scr, st[j])
            self.transpose_cols(xn[j], D, xT[j].ap, xT[j], psT[j])
            self.transpose_cols(pb[j], 256, pT[j].ap, pT[j], psT[j], eng="act")
            for hh in range(2):
                cs = slice(hh * 512, (hh + 1) * 512)
                for k in range(8):
                    self.MM(pg[:, cs], xT[j][:, k, :], wg[:, k, cs], k == 0, k == 7, [xT[j], wg], [pg])
                for k in range(2):
                    self.MM(pe[:, cs], pT[j][:, k, :], wp[:, k, cs], k == 0, k == 1, [pT[j], wp], [pe])
            self.ACT(sgm[j].ap, pg.ap, AF.Sigmoid, [pg], [sgm[j]])
            self.TT_("dve", sgm[j].ap, pe.ap, sgm[j].ap, ALU.mult, [pe, sgm[j]], [sgm[j]])
            self.TT_("pool", h[j].ap, h[j].ap, sgm[j].ap, ALU.add, [h[j], sgm[j]], [h[j]])
            if fuse_final:
                self.norm_tile(h[j], nwf, fo[j], scr, st[j])
                self.LD(self.yout[rows, :], fo[j].ap, [fo[j]], [self.yout])
            else:
                self.LD(self.cH[rows, :], h[j].ap, [h[j]], [self.cH])

    def ph_final(self):
        S = self.S
        nw = self.bcast_row("nw", self.di["final_norm"].ap, D)
        h = [S.sb(f"h{i}", [128, D], F32) for i in range(2)]
        o = [S.sb(f"o{i}", [128, D], F32) for i in range(2)]
        scr = S.sb("scr", [128, D], F32)
        st = [S.sb(f"st{i}", [128, 4], F32) for i in range(2)]
        for t in range(self.cTT):
            j = t % 2
            rows = slice(t * 128, (t + 1) * 128)
            self.LD(h[j].ap, self.cH[rows, :], [self.cH], [h[j]])
            self.norm_tile(h[j], nw, o[j], scr, st[j])
            self.LD(self.yout[rows, :], o[j].ap, [o[j]], [self.yout])

    def ph_rwkv_norm(self):
        S = self.S
        nw = self.bcast_row("nw", self.di["norm_w"].ap[0, 1], D)
        h = [S.sb(f"h{i}", [128, D], F32) for i in range(2)]
        o = [S.sb(f"o{i}", [128, D], F32) for i in range(2)]
        scr = S.sb("scr", [128, D], F32)
        st = [S.sb(f"st{i}", [128, 4], F32) for i in range(2)]
        for t in range(self.TT):
            j = t % 2
            rows = slice(t * 128, (t + 1) * 128)
            self.LD(h[j].ap, self.Hd[rows, :], [self.Hd], [h[j]])
            self.norm_tile(h[j], nw, o[j], scr, st[j])
            self.LD(self.HN[1 + t * 128:1 + (t + 1) * 128, :], o[j].ap, [o[j]], [self.HN])
            self.LD(self.shiftall[rows, :], o[j].ap, [o[j]], [self.shiftall])

    def ph_rwkv_proj(self):
        S, di = self.S, self.di
        mixn = S.sb("mixn", [48, 128], F32)
        self.LD(mixn.ap, di["rwkv_mix"].ap.rearrange("a (k p) -> (a k) p", p=128), [self.din_res], [mixn])
        mixT = S.sb("mixT", [128, 48], F32)
        w0b = self.bcast_row("w0b", di["rwkv_w0"].ap, D)
        a0b = self.bcast_row("a0b", di["rwkv_a0"].ap, D)
        kkb = self.bcast_row("kkb", di["rwkv_kk"].ap, D)
        kab = self.bcast_row("kab", di["rwkv_ka"].ap, D)
        rkb = self.bcast_row("rkb", di["rwkv_rk"].ap, D)
        wr = [self.load_w(f"wrkv{i}", di["rwkv_wrkv"].ap[i], D, D) for i in range(3)]
        w1 = self.load_w("w1", di["rwkv_w1"].ap, D, 64)
        a1 = self.load_w("a1", di["rwkv_a1"].ap, D, 64)
        g1 = self.load_w("g1", di["rwkv_g1"].ap, D, 160)
        w2 = S.sb("w2", [64, D], BF16); S.dma("pool", w2.ap, di["rwkv_w2"].ap, [self.din_res], [w2])
        a2 = S.sb("a2", [64, D], BF16); S.dma("pool", a2.ap, di["rwkv_a2"].ap, [self.din_res], [a2])
        g2a = S.sb("g2a", [128, D], BF16); S.dma("pool", g2a.ap, di["rwkv_g2"].ap[0:128, :], [self.din_res], [g2a])
        g2b = S.sb("g2b", [32, D], BF16); S.dma("pool", g2b.ap, di["rwkv_g2"].ap[128:160, :], [self.din_res], [g2b])
        psm = S.ps("psm", [128, 512])
        self.S.op("pe", lambda e: e.transpose(psm[:, 0:48], mixn.ap, self.identf[0:48, 0:48]), [mixn, self.identf], [psm])
        self.CP("dve", mixT.ap, psm[:, 0:48], [psm], [mixT])
        def scaled(name, w, j, cols):
            t = S.sb(name, [128, 8, cols], BF16)
            for k in range(8):
                self.TS("dve" if k % 2 else "pool", t[:, k, :], w[:, k, :], mixT[:, j * 8 + k:j * 8 + k + 1], None, ALU.mult, None, [w, mixT], [t])
            return t
        wrs = [scaled("wrs0", wr[0], 0, D), scaled("wrs1", wr[1], 2, D), scaled("wrs2", wr[2], 3, D)]
        w1s = scaled("w1s", w1, 1, 64)
        a1s = scaled("a1s", a1, 4, 64)
        g1s = scaled("g1s", g1, 5, 160)
        hn = [S.sb(f"hn{i}", [128, D], F32) for i in range(2)]
        hp = [S.sb(f"hp{i}", [128, D], F32) for i in range(2)]
        xx = S.sb("xx", [128, D], F32)
        tmp = S.sb("tmp", [128, D], F32)
        xm = [S.sb(f"xm{i}", [128, D], BF16) for i in range(2)]
        xTh = [S.sb(f"xTh{i}", [128, 8, 128], BF16) for i in range(2)]
        xTx = [S.sb(f"xTx{i}", [128, 8, 128], BF16) for i in range(2)]
        lo = S.sb("lo", [128, 128], BF16)
        lo2 = S.sb("lo2", [32, 128], BF16)
        o_r = S.sb("o_r", [128, D], F32); o_k = S.sb("o_k", [128, D], F32); o_v = S.sb("o_v", [128, D], F32)
        o_a = S.sb("o_a", [128, D], F32); o_kk = S.sb("o_kk", [128, D], F32); o_b = S.sb("o_b", [128, D], F32)
        o_w = S.sb("o_w", [128, D], F32); o_g = S.sb("o_g", [128, D], F32)
        sm = S.sb("sm", [128, 3, H], F32)
        psT = [S.ps(f"psT{i}", [128, D], BF16) for i in range(2)]
        pp = [S.ps(f"pp{i}", [128, D]) for i in range(2)]
        pl = S.ps("pl", [128, 512])
        v3 = lambda t_: t_.ap.rearrange("p (h n) -> p h n", h=H)
        for t in range(self.TT):
            j = t % 2
            rows = slice(t * 128, (t + 1) * 128)
            self.LD(hn[j].ap, self.HN[1 + t * 128:1 + (t + 1) * 128, :], [self.HN], [hn[j]])
            self.LD(hp[j].ap, self.HN[t * 128:(t + 1) * 128, :], [self.HN], [hp[j]])
            if t >= self.NPT:
                self.LD(hp[j][0:1, :], di["shift0"].ap[t - self.NPT:t - self.NPT + 1, :], [self.din_res], [hp[j]])
            self.CP("pool", xm[0].ap, hn[j].ap, [hn[j]], [xm[0]])
            self.TT_("dve", xm[1].ap, hp[j].ap, hn[j].ap, ALU.subtract, [hp[j], hn[j]], [xm[1]])
            xh, xd = xTh[j], xTx[j]
            self.transpose_cols(xm[0], D, xh.ap, xh, psT[0], eng="act")
            self.transpose_cols(xm[1], D, xd.ap, xd, psT[1], eng="act")
            def proj(w, ws, pt):
                for hh in range(2):
                    cs = slice(hh * 512, (hh + 1) * 512)
                    for k in range(8):
                        self.MM(pt[:, cs], xh[:, k, :], w[:, k, cs], k == 0, False, [xh, w], [pt])
                    for k in range(8):
                        self.MM(pt[:, cs], xd[:, k, :], ws[:, k, cs], False, k == 7, [xd, ws], [pt])
            def lora1(out, w, ws, c0, c1):
                for k in range(8):
                    self.MM(out, w[:, k, c0:c1], xh[:, k, :], k == 0, False, [w, xh], [pl])
                for k in range(8):
                    self.MM(out, ws[:, k, c0:c1], xd[:, k, :], False, k == 7, [ws, xd], [pl])
            proj(wr[0], wrs[0], pp[0]); self.CP("act", o_r.ap, pp[0].ap, [pp[0]], [o_r])
            proj(wr[1], wrs[1], pp[1]); self.CP("act", o_k.ap, pp[1].ap, [pp[1]], [o_k])
            proj(wr[2], wrs[2], pp[0]); self.CP("act", o_v.ap, pp[0].ap, [pp[0]], [o_v])
            lora1(pl[0:64, 0:128], w1, w1s, 0, 64)
            self.ACT(lo[0:64, :], pl[0:64, 0:128], AF.Tanh, [pl], [lo])
            for hh in range(2):
                cs = slice(hh * 512, (hh + 1) * 512)
                self.MM(pp[1][:, cs], lo[0:64, :], w2[:, cs], True, True, [lo, w2], [pp[1]])
            self.TT_("dve", o_w.ap, pp[1].ap, w0b.ap, ALU.add, [pp[1], w0b], [o_w])
            self.ACT(o_w.ap, o_w.ap, AF.Sigmoid, [o_w], [o_w])
            self.TS("dve", o_w.ap, o_w.ap, -float(np.exp(-0.5)), None, ALU.mult, None, [o_w], [o_w])
            if t >= self.NPT:
                self.TS("dve", o_w.ap, o_w.ap, self.rowmask[:, 1:2], None, ALU.mult, None, [o_w, self.rowmask], [o_w])
            lora1(pl[0:64, 0:128], a1, a1s, 0, 64)
            self.CP("act", lo[0:64, :], pl[0:64, 0:128], [pl], [lo])
            for hh in range(2):
                cs = slice(hh * 512, (hh + 1) * 512)
                self.MM(pp[0][:, cs], lo[0:64, :], a2[:, cs], True, True, [lo, a2], [pp[0]])
            self.TT_("dve", o_a.ap, pp[0].ap, a0b.ap, ALU.add, [pp[0], a0b], [o_a])
            self.ACT(o_a.ap, o_a.ap, AF.Sigmoid, [o_a], [o_a])
            lora1(pl[:, 0:128], g1, g1s, 0, 128)
            lora1(pl[0:32, 128:256], g1, g1s, 128, 160)
            self.ACT(lo.ap, pl[:, 0:128], AF.Sigmoid, [pl], [lo])
            self.ACT(lo2.ap, pl[0:32, 128:256], AF.Sigmoid, [pl], [lo2])
            for hh in range(2):
                cs = slice(hh * 512, (hh + 1) * 512)
                self.MM(pp[1][:, cs], lo.ap, g2a[:, cs], True, False, [lo, g2a], [pp[1]])
                self.MM(pp[1][:, cs], lo2.ap, g2b[:, cs], False, True, [lo2, g2b], [pp[1]])
            self.CP("act", o_g.ap, pp[1].ap, [pp[1]], [o_g])
            self.TT_("dve", o_kk.ap, o_k.ap, kkb.ap, ALU.mult, [o_k, kkb], [o_kk])
            self.TT_("pool", tmp.ap, o_kk.ap, o_kk.ap, ALU.mult, [o_kk], [tmp])
            self.RED(sm[:, 0, :], v3(tmp), ALU.add, [tmp], [sm])
            self.ACT(sm[:, 0, :], sm[:, 0, :], AF.Sqrt, [sm], [sm])
            self.TS("dve", sm[:, 0, :], sm[:, 0, :], 1e-12, None, ALU.max, None, [sm], [sm])
            self.RCP(sm[:, 1, :], sm[:, 0, :], [sm], [sm])
            self.TT_("dve", v3(o_kk), v3(o_kk), sm[:, 1, :].unsqueeze(2).broadcast_to([128, H, 64]), ALU.mult, [o_kk, sm], [o_kk])
            self.TT_("pool", o_b.ap, o_kk.ap, o_a.ap, ALU.mult, [o_kk, o_a], [o_b])
            self.TS("dve", tmp.ap, o_a.ap, -1.0, None, ALU.add, None, [o_a], [tmp])
            self.TT_("dve", tmp.ap, tmp.ap, kab.ap, ALU.mult, [tmp, kab], [tmp])
            self.STT(o_k.ap, tmp.ap, 1.0, o_k.ap, ALU.add, ALU.mult, [tmp, o_k], [o_k])
            self.TT_("pool", tmp.ap, o_r.ap, o_k.ap, ALU.mult, [o_r, o_k], [tmp])
            self.TT_("dve", tmp.ap, tmp.ap, rkb.ap, ALU.mult, [tmp, rkb], [tmp])
            self.RED(sm[:, 2, :], v3(tmp), ALU.add, [tmp], [sm])
            if t >= self.NPT:
                self.TS("dve", o_b.ap, o_b.ap, self.rowmask[:, 1:2], None, ALU.mult, None, [o_b, self.rowmask], [o_b])
                self.TS("dve", o_k.ap, o_k.ap, self.rowmask[:, 1:2], None, ALU.mult, None, [o_k, self.rowmask], [o_k])
            for src, dst in ((o_r, self.Rd), (o_k, self.Kd), (o_v, self.Vd), (o_kk, self.KKd), (o_b, self.Bd), (o_w, self.LWd), (o_g, self.Gd)):
                self.LD(dst[rows, :], src.ap, [src], [dst])
            self.LD(self.BONd[rows, :], sm[:, 2, :], [sm], [self.BONd])

    def ph_rwkv_scan(self):
        S, di = self.S, self.di
        import os
        LV = float(os.environ.get("K_SCAN", "9"))
        tri = S.sb("tri", [128, 256], F32); self.LD(tri.ap, di["c_tri"].ap, [self.din_res], [tri])
        m1 = S.sb("m1", [128, 384], BF16); S.dma("pool", m1.ap, di["c_m1"].ap, [self.din_res], [m1])
        m2 = S.sb("m2", [128, 256], BF16); S.dma("pool", m2.ap, di["c_m2"].ap, [self.din_res], [m2])
        names = ("r", "k", "v", "kk", "b", "lw")
        srcs = (self.Rd, self.Kd, self.Vd, self.KKd, self.Bd, self.LWd)
        inb = {n: S.sb(f"in_{n}", [128, D], F32) for n in names}
        ecum = S.sb("ecum", [128, D], F32); encum = S.sb("encum", [128, D], F32)
        eex = S.sb("eex", [128, D], F32); erc = S.sb("erc", [128, D], F32)
        PB = [dict(tA=S.sb(f"tA{p}", [128, D], BF16), tR=S.sb(f"tR{p}", [128, D], BF16), tB=S.sb(f"tB{p}", [128, D], BF16),
                   tK=S.sb(f"tK{p}", [128, D], BF16), hB=S.sb(f"hB{p}", [128, D], BF16), hK=S.sb(f"hK{p}", [128, D], BF16),
                   vb=S.sb(f"vb{p}", [128, D], BF16), wc=S.sb(f"wc{p}", [64, H], F32)) for p in range(2)]
        yt = S.sb("yt", [128, D], F32)
        Mf = [S.sb(f"Mf{h}", [64, 64], F32) for h in range(H)]
        Mb = [S.sb(f"Mb{h}", [64, 64], BF16) for h in range(H)]
        G = 6
        NB = G
        TTh = [S.sb(f"TTh{i}", [64, 512], BF16) for i in range(NB)]
        SC1 = [S.sb(f"SC1{i}", [128, 384], BF16) for i in range(NB)]
        SC2 = [S.sb(f"SC2{i}", [128, 256], BF16) for i in range(NB)]
        XX = [[S.sb(f"XX{i}_{p}", [128, 256], BF16) for p in range(2)] for i in range(NB)]
        PP = [[S.sb(f"PP{i}_{p}", [128, 128], BF16) for p in range(2)] for i in range(NB)]
        Zb = [S.sb(f"Zb{i}", [128, 64], BF16) for i in range(NB)]
        AhT = [S.sb(f"AhT{i}", [64, 128], BF16) for i in range(NB)]
        Ub = [S.sb(f"Ub{i}", [128, 64], BF16) for i in range(NB)]
        st0 = S.sb("st0", [64, 64], F32)
        pcum = S.ps("pcum", [128, D]); prc = pcum
        bank = [S.ps(f"bk{i}", [128, 512]) for i in range(G)]
        pw = bank
        def prep(t, p):
            tA, tR, tB, tK, hB, hK, vb, wc = (PB[p][k] for k in ('tA', 'tR', 'tB', 'tK', 'hB', 'hK', 'vb', 'wc'))
            if LV < 2:
                return
            rows = slice(t * 128, (t + 1) * 128)
            for n, s_ in zip(names, srcs):
                self.LD(inb[n].ap, s_[rows, :], [s_], [inb[n]])
            lw = inb["lw"]
            if LV < 2.2:
                return
            for hh in range(2):
                cs = slice(hh * 512, (hh + 1) * 512)
                self.MM(pcum[:, cs], tri[:, 0:128], lw[:, cs], True, True, [tri, lw], [pcum])
            if LV < 2.12:
                return
            self.ACT(ecum.ap, pcum.ap, AF.Exp, [pcum], [ecum])
            if LV < 2.13:
                return
            self.ACT(encum.ap, pcum.ap, AF.Exp, [pcum], [encum], scale=-1.0)
            if LV < 2.14:
                return
            self.TT_("dve", eex.ap, pcum.ap, lw.ap, ALU.subtract, [pcum, lw], [eex])
            self.ACT(eex.ap, eex.ap, AF.Exp, [eex], [eex])
            if LV < 2.15:
                return
            for hh in range(2):
                cs = slice(hh * 512, (hh + 1) * 512)
                self.MM(prc[:, cs], tri[:, 128:256], lw[:, cs], True, True, [tri, lw], [prc])
            self.ACT(erc.ap, prc.ap, AF.Exp, [prc], [erc])
            if LV < 2.3:
                return
            self.STT(tA.ap, inb["kk"].ap, -1.0, eex.ap, ALU.mult, ALU.mult, [inb["kk"], eex], [tA])
            self.TT_("pool", tR.ap, inb["r"].ap, ecum.ap, ALU.mult, [inb["r"], ecum], [tR])
            self.TT_("dve", tB.ap, inb["b"].ap, encum.ap, ALU.mult, [inb["b"], encum], [tB])
            self.TT_("pool", tK.ap, inb["k"].ap, encum.ap, ALU.mult, [inb["k"], encum], [tK])
            self.TT_("dve", hB.ap, inb["b"].ap, erc.ap, ALU.mult, [inb["b"], erc], [hB])
            self.TT_("pool", hK.ap, inb["k"].ap, erc.ap, ALU.mult, [inb["k"], erc], [hK])
            self.CP("pool", vb.ap, inb["v"].ap, [inb["v"]], [vb])
            if LV < 2.4:
                return
            pz = pw[0]
            for h in range(H):
                self.MM(pz[0:64, h:h + 1], lw[:, h * 64:(h + 1) * 64], self.ones_f.ap, True, True, [lw, self.ones_f], [pz])
            self.ACT(wc.ap, pz[0:64, 0:H], AF.Exp, [pz], [wc])

        def heads_group(t, p, g0):
            tA, tR, tB, tK, hB, hK, vb, wc = (PB[p][k] for k in ('tA', 'tR', 'tB', 'tK', 'hB', 'hK', 'vb', 'wc'))
            heads = list(range(g0, min(H, g0 + G)))
            for i, h in enumerate(heads):
                hs = slice(h * 64, (h + 1) * 64)
                bk, tt = bank[i], TTh[i]
                pv = bk[0:64, 0:256].bitcast(BF16)
                for q, src in enumerate((tA, tR, tB, tK)):
                    self.TR(pv[:, q * 128:(q + 1) * 128], src[:, hs], self.identb.ap, [src, self.identb], [bk])
                self.CP("act", tt.ap, pv, [bk], [tt])
            for i, h in enumerate(heads):
                bk, tt, s1 = bank[i], TTh[i], SC1[i]
                self.MM(bk[:, 0:128], tt[:, 0:128], tt[:, 256:384], True, True, [tt], [bk])
                self.MM(bk[:, 128:384], tt[:, 256:384], tt[:, 0:256], True, True, [tt], [bk])
                self.TT_("dve", s1.ap, bk[:, 0:384], m1.ap, ALU.mult, [bk, m1], [s1])
            for i, h in enumerate(heads):
                bk, tt, s2 = bank[i], TTh[i], SC2[i]
                self.MM(bk[:, 0:256], tt[:, 384:512], tt[:, 0:256], True, True, [tt], [bk])
                self.TT_("dve", s2.ap, bk[:, 0:256], m2.ap, ALU.mult, [bk, m2], [s2])
            stt = {}
            for i, h in enumerate(heads):
                stt[i] = [SC1[i][:, 0:256], SC1[i], self.identb.ap, self.identb]
            for lv in range(7):
                for i, h in enumerate(heads):
                    bk = bank[i]
                    xcur, xres, pcur, pres = stt[i]
                    X, XT_ = xcur[:, 0:128], xcur[:, 128:256]
                    self.MM(bk[:, 0:128], X, pcur, True, True, [xres, pres], [bk])
                    if lv < 6:
                        self.MM(bk[:, 128:256], XT_, X, True, True, [xres], [bk])
                        self.MM(bk[:, 256:384], X, XT_, True, True, [xres], [bk])
                    pn = PP[i][lv % 2]
                    self.TT_("dve", pn.ap, bk[:, 0:128], pcur, ALU.add, [bk, pres], [pn])
                    stt[i][2], stt[i][3] = pn.ap, pn
                    if lv < 6:
                        xn_ = XX[i][lv % 2]
                        self.CP("act", xn_.ap, bk[:, 128:384], [bk], [xn_])
                        stt[i][0], stt[i][1] = xn_.ap, xn_
            for i, h in enumerate(heads):
                hs = slice(h * 64, (h + 1) * 64)
                bk = bank[i]
                P, pres = stt[i][2], stt[i][3]
                self.MM(bk[:, 0:64], SC2[i][:, 0:128], vb[:, hs], True, True, [SC2[i], vb], [bk])
                self.MM(bk[0:64, 64:192], tA[:, hs], P, True, True, [tA, pres], [bk])
                self.CP("act", Zb[i].ap, bk[:, 0:64], [bk], [Zb[i]])
                self.CP("dve", AhT[i].ap, bk[0:64, 64:192], [bk], [AhT[i]])
            for i, h in enumerate(heads):
                bk = bank[i]
                P, pres = stt[i][2], stt[i][3]
                self.MM(bk[:, 256:320], P, Zb[i].ap, True, False, [pres, Zb[i]], [bk])
                self.MM(bk[:, 256:320], AhT[i].ap, Mb[h].ap, False, True, [AhT[i], Mb[h]], [bk])
                self.CP("dve", Ub[i].ap, bk[:, 256:320], [bk], [Ub[i]])
            for i, h in enumerate(heads):
                hs = slice(h * 64, (h + 1) * 64)
                bk, tt = bank[i], TTh[i]
                self.MM(bk[:, 320:384], SC2[i][:, 128:256], vb[:, hs], True, False, [SC2[i], vb], [bk])
                self.MM(bk[:, 320:384], tt[:, 128:256], Mb[h].ap, False, False, [tt, Mb[h]], [bk])
                self.MM(bk[:, 320:384], SC1[i][:, 256:384], Ub[i].ap, False, True, [SC1[i], Ub[i]], [bk])
                self.CP("act", yt[:, hs], bk[:, 320:384], [bk], [yt])
            for i, h in enumerate(heads):
                hs = slice(h * 64, (h + 1) * 64)
                bk = bank[i]
                self.MM(bk[0:64, 384:448], hK[:, hs], vb[:, hs], True, False, [hK, vb], [bk])
                self.MM(bk[0:64, 384:448], hB[:, hs], Ub[i].ap, False, True, [hB, Ub[i]], [bk])
                self.STT(Mf[h].ap, Mf[h].ap, wc[:, h:h + 1], bk[0:64, 384:448], ALU.mult, ALU.add, [Mf[h], wc, bk], [Mf[h]])
                self.CP("pool", Mb[h].ap, Mf[h].ap, [Mf[h]], [Mb[h]])

        seqs = [(list(range(self.NPT)), None, self.wkvp.ap)]
        for b in range(NSB):
            seqs.append(([self.NPT + b], b, self.wkvs.ap[b]))
        hcnt = 0
        for tiles, b0, dst in seqs:
            for h in range(H):
                if b0 is None:
                    self.MSET("pool", Mf[h].ap, 0.0, [Mf[h]])
                else:
                    self.LD(st0.ap, di["wkv0"].ap[b0, h], [self.din_res], [st0])
                    pz = pw[h % 4]
                    self.S.op("pe", lambda e, o=pz[0:64, 0:64], i_=st0.ap, idn=self.identf[0:64, 0:64]: e.transpose(o, i_, idn), [st0, self.identf], [pz])
                    self.CP("dve", Mf[h].ap, pz[0:64, 0:64], [pz], [Mf[h]])
                self.CP("pool", Mb[h].ap, Mf[h].ap, [Mf[h]], [Mb[h]])
            par = 0
            prep(tiles[0], par)
            for idx, t in enumerate(tiles):
                rows = slice(t * 128, (t + 1) * 128)
                gl = list(range(0, H, G))
                for gi, g0 in enumerate(gl):
                    if gi == len(gl) - 1 and idx + 1 < len(tiles):
                        prep(tiles[idx + 1], 1 - par)
                    heads_group(t, par, g0)
                par = 1 - par
                self.LD(self.Yd[rows, :], yt.ap, [yt], [self.Yd])
            for h in range(H):
                pz = pw[h % 4]
                self.S.op("pe", lambda e, o=pz[0:64, 0:64], i_=Mf[h].ap, idn=self.identf[0:64, 0:64]: e.transpose(o, i_, idn), [Mf[h], self.identf], [pz])
                self.CP("dve", st0.ap, pz[0:64, 0:64], [pz], [st0])
                self.LD(dst[h], st0.ap, [st0], [self.wkvp if b0 is None else self.wkvs])

    def ph_rwkv_post(self):
        S, di = self.S, self.di
        lwb = self.bcast_row("lwb", di["rwkv_lnx_w"].ap, D)
        lbb = self.bcast_row("lbb", di["rwkv_lnx_b"].ap, D)
        wo = self.load_w("wo", di["rwkv_wo"].ap, D, D)
        y = [S.sb(f"y{i}", [128, D], F32) for i in range(2)]
        g = [S.sb(f"g{i}", [128, D], F32) for i in range(2)]
        v = [S.sb(f"v{i}", [128, D], F32) for i in range(2)]
        h = [S.sb(f"h{i}", [128, D], F32) for i in range(2)]
        bon = [S.sb(f"bon{i}", [128, H], F32) for i in range(2)]
        tmpl = [S.sb(f"tmp{i}", [128, D], F32) for i in range(2)]
        sml = [S.sb(f"sm{i}", [128, 4, H], F32) for i in range(2)]
        ob = [S.sb(f"ob{i}", [128, D], BF16) for i in range(2)]
        xT = [S.sb(f"xT{i}", [128, 8, 128], BF16) for i in range(2)]
        psT = [S.ps(f"psT{i}", [128, D], BF16) for i in range(2)]
        pol = [S.ps(f"po{i}", [128, D]) for i in range(2)]
        v3 = lambda a: a.rearrange("p (h n) -> p h n", h=H)
        bc = lambda a: a.unsqueeze(2).broadcast_to([128, H, 64])

        def tile_ops(t):
            j = t % 2
            rows = slice(t * 128, (t + 1) * 128)
            yy, tmp, sm, po = y[j], tmpl[j], sml[j], pol[j]
            ops = []

            def loads():
                self.LD(y[j].ap, self.Yd[rows, :], [self.Yd], [y[j]])
                self.LD(g[j].ap, self.Gd[rows, :], [self.Gd], [g[j]])
                self.LD(v[j].ap, self.Vd[rows, :], [self.Vd], [v[j]])
                self.LD(h[j].ap, self.Hd[rows, :], [self.Hd], [h[j]])
                self.LD(bon[j].ap, self.BONd[rows, :], [self.BONd], [bon[j]])
                if self.dbg:
                    self.LD(self.dbgY[rows, :], y[j].ap, [y[j]], [self.dbgY])
            ops.append(loads)
            ops.append(lambda: self.RED(sm[:, 0, :], v3(yy.ap), ALU.add, [yy], [sm]))
            ops.append(lambda: self.TS("dve", sm[:, 0, :], sm[:, 0, :], 1.0 / 64, None, ALU.mult, None, [sm], [sm]))
            ops.append(lambda: self.TT_("dve", v3(yy.ap), v3(yy.ap), bc(sm[:, 0, :]), ALU.subtract, [yy, sm], [yy]))
            ops.append(lambda: self.TT_("pool", tmp.ap, yy.ap, yy.ap, ALU.mult, [yy], [tmp]))
            ops.append(lambda: self.RED(sm[:, 1, :], v3(tmp.ap), ALU.add, [tmp], [sm]))
            ops.append(lambda: self.TS("dve", sm[:, 1, :], sm[:, 1, :], 1.0 / 64, 64e-5, ALU.mult, ALU.add, [sm], [sm]))
            ops.append(lambda: self.ACT(sm[:, 1, :], sm[:, 1, :], AF.Sqrt, [sm], [sm]))
            ops.append(lambda: self.RCP(sm[:, 2, :], sm[:, 1, :], [sm], [sm]))
            ops.append(lambda: self.TT_("pool", v3(tmp.ap), v3(v[j].ap), bc(bon[j].ap), ALU.mult, [v[j], bon[j]], [tmp]))
            ops.append(lambda: self.TT_("dve", v3(yy.ap), v3(yy.ap), bc(sm[:, 2, :]), ALU.mult, [yy, sm], [yy]))
            ops.append(lambda: self.TT_("pool", yy.ap, yy.ap, lwb.ap, ALU.mult, [yy, lwb], [yy]))
            ops.append(lambda: self.TT_("dve", yy.ap, yy.ap, lbb.ap, ALU.add, [yy, lbb], [yy]))
            ops.append(lambda: self.TT_("pool", yy.ap, yy.ap, tmp.ap, ALU.add, [yy, tmp], [yy]))
            ops.append(lambda: self.TT_("dve", ob[j].ap, yy.ap, g[j].ap, ALU.mult, [yy, g[j]], [ob[j]]))
            ops.append(lambda: self.transpose_cols(ob[j], D, xT[j].ap, xT[j], psT[j], eng="act"))

            def mm():
                for hh in range(2):
                    cs = slice(hh * 512, (hh + 1) * 512)
                    for k in range(8):
                        self.MM(po[:, cs], xT[j][:, k, :], wo[:, k, cs], k == 0, k == 7, [xT[j], wo], [po])
            ops.append(mm)
            ops.append(lambda: self.TT_("dve", h[j].ap, po.ap, h[j].ap, ALU.add, [po, h[j]], [h[j]]))
            ops.append(lambda: self.LD(self.Hd[rows, :], h[j].ap, [h[j]], [self.Hd]))
            return ops

        for t0_ in range(0, self.TT, 2):
            lists = [tile_ops(t) for t in range(t0_, min(self.TT, t0_ + 2))]
            for k in range(len(lists[0])):
                for l in lists:
                    l[k]()

    def ph_kv(self):
        S, di = self.S, self.di
        nw = self.bcast_row("nw", di["kv_norm"].ap, D)
        wkv = self.load_w("wkv", di["w_kv"].ap, D, 512)
        h = [S.sb(f"h{i}", [128, D], F32) for i in range(2)]
        xn = [S.sb(f"xn{i}", [128, D], BF16) for i in range(2)]
        xT = [S.sb(f"xT{i}", [128, 8, 128], BF16) for i in range(2)]
        kv = [S.sb(f"kv{i}", [128, 512], F32) for i in range(2)]
        kvb = [S.sb(f"kvb{i}", [128, 512], BF16) for i in range(2)]
        ktb = [S.sb(f"ktb{i}", [64, 4, 128], BF16) for i in range(2)]
        scr = S.sb("scr", [128, D], F32)
        st = [S.sb(f"st{i}", [128, 4], F32) for i in range(2)]
        cb = [S.sb(f"cb{i}", [128, 256], BF16) for i in range(2)]
        psT = [S.ps(f"psT{i}", [128, D], BF16) for i in range(2)]
        pk = S.ps("pk", [128, 512])
        pkt = [S.ps(f"pkt{i}", [64, 512], BF16) for i in range(2)]
        for t in range(self.TT):
            j = t % 2
            rows = slice(t * 128, (t + 1) * 128)
            self.LD(h[j].ap, self.Hd[rows, :], [self.Hd], [h[j]])
            self.norm_tile(h[j], nw, xn[j], scr, st[j])
            self.transpose_cols(xn[j], D, xT[j].ap, xT[j], psT[j])
            for k in range(8):
                self.MM(pk.ap, xT[j][:, k, :], wkv[:, k, :], k == 0, k == 7, [xT[j], wkv], [pk])
            self.CP("act", kv[j].ap, pk.ap, [pk], [kv[j]])
            self.CP("dve", kvb[j].ap, pk.ap, [pk], [kvb[j]])
            self.LD(self.kvout[rows, :], kv[j].ap, [kv[j]], [self.kvout])
            for hd in range(4):
                self.TR(pkt[j][:, hd * 128:(hd + 1) * 128], kvb[j][:, hd * 64:(hd + 1) * 64], self.identb.ap, [kvb[j], self.identb], [pkt[j]])
            self.CP("act", ktb[j].ap, pkt[j].ap.rearrange("p (h t) -> p h t", h=4), [pkt[j]], [ktb[j]])
            if t < self.NPT:
                self.LD(self.KTp[0:64, :, MAXW + t * 128:MAXW + (t + 1) * 128], ktb[j].ap, [ktb[j]], [self.KTp])
                self.LD(self.VDp[MAXW + t * 128:MAXW + (t + 1) * 128, :], kvb[j][:, 256:512], [kvb[j]], [self.VDp])
            else:
                b = t - self.NPT
                self.LD(self.KTs[b, 0:64, :, MAXW:MAXW + 8], ktb[j][:, :, 0:8], [ktb[j]], [self.KTs])
                self.LD(self.VDs[b, MAXW:MAXW + 8, :], kvb[j][0:8, 256:512], [kvb[j]], [self.VDs])
        cnt = 0
        for b in range(self.NS1):
            S.dma("pool", self.VDsm[b, 0:MAXW, :], di["cache"].ap[b, :, 256:512], [self.din_res], [self.VDsm])
            for r0 in range(0, MAXW, 128):
                j = cnt % 2
                cnt += 1
                S.dma("pool", cb[j].ap, di["cache"].ap[b, r0:r0 + 128, 0:256], [self.din_res], [cb[j]])
                for hd in range(4):
                    self.TR(pkt[j][:, hd * 128:(hd + 1) * 128], cb[j][:, hd * 64:(hd + 1) * 64], self.identb.ap, [cb[j], self.identb], [pkt[j]])
                self.CP("act" if j else "dve", ktb[j].ap, pkt[j].ap.rearrange("p (h t) -> p h t", h=4), [pkt[j]], [ktb[j]])
                self.LD(self.KTsm[b, 0:64, :, r0:r0 + 128], ktb[j].ap, [ktb[j]], [self.KTsm])

    def ph_select(self):
        S, di = self.S, self.di
        NH = self.NH
        sel = S.sb("sel", [128, 2], F32)
        self.LD(sel.ap, di["sel"].ap, [self.din_res], [sel])
        NB_ = 4 if NH % 4 == 0 else 1
        a = [S.sb(f"a{i}", [128, NB_, D], F32) for i in range(2)]
        b = [S.sb(f"b{i}", [128, NB_, D], F32) for i in range(2)]
        for ii, i in enumerate(range(0, NH, NB_)):
            j = ii % 2
            ra = self.Hd[i * 128:(i + NB_) * 128, :].rearrange("(n p) d -> p n d", p=128)
            rb = self.Hd[(NH + i) * 128:(NH + i + NB_) * 128, :].rearrange("(n p) d -> p n d", p=128)
            self.LD(a[j].ap, ra, [self.Hd], [a[j]])
            self.LD(b[j].ap, rb, [self.Hd], [b[j]])
            self.TS("pool", a[j].ap, a[j].ap, sel[:, 0:1], None, ALU.mult, None, [a[j], sel], [a[j]])
            self.STT(a[j].ap, b[j].ap, sel[:, 1:2], a[j].ap, ALU.mult, ALU.add, [b[j], sel, a[j]], [a[j]])
            self.LD(self.Hm[i * 128:(i + NB_) * 128, :].rearrange("(n p) d -> p n d", p=128), a[j].ap, [a[j]], [self.Hm])
        W = MAXW + NH * 128
        off = NH * 128
        ka = [S.sb(f"ka{i}", [65, W], BF16) for i in range(2)]
        kb = [S.sb(f"kb{i}", [65, W], BF16) for i in range(2)]
        for hd in range(4):
            j = hd % 2
            self.LD(ka[j].ap, self.KTp[:, hd, 0:W], [self.KTp], [ka[j]])
            self.LD(kb[j].ap, self.KTp[:, hd, off:off + W], [self.KTp], [kb[j]])
            self.TS("pool", ka[j].ap, ka[j].ap, sel[0:65, 0:1], None, ALU.mult, None, [ka[j], sel], [ka[j]])
            self.STT(ka[j].ap, kb[j].ap, sel[0:65, 1:2], ka[j].ap, ALU.mult, ALU.add, [kb[j], sel, ka[j]], [ka[j]])
            self.LD(self.KTm[:, hd, :], ka[j].ap, [ka[j]], [self.KTm])
        nvt = W // 128
        VB = 8 if nvt % 8 == 0 else 1
        va = [S.sb(f"va{i}", [128, VB, 256], BF16) for i in range(2)]
        vb_ = [S.sb(f"vb{i}", [128, VB, 256], BF16) for i in range(2)]
        for ii, i in enumerate(range(0, nvt, VB)):
            j = ii % 2
            self.LD(va[j].ap, self.VDp[i * 128:(i + VB) * 128, :].rearrange("(n p) d -> p n d", p=128), [self.VDp], [va[j]])
            self.LD(vb_[j].ap, self.VDp[off + i * 128:off + (i + VB) * 128, :].rearrange("(n p) d -> p n d", p=128), [self.VDp], [vb_[j]])
            self.TS("pool", va[j].ap, va[j].ap, sel[:, 0:1], None, ALU.mult, None, [va[j], sel], [va[j]])
            self.STT(va[j].ap, vb_[j].ap, sel[:, 1:2], va[j].ap, ALU.mult, ALU.add, [vb_[j], sel, va[j]], [va[j]])
            self.LD(self.VDm[i * 128:(i + VB) * 128, :].rearrange("(n p) d -> p n d", p=128), va[j].ap, [va[j]], [self.VDm])

    def ph_select_s(self):
        S, di = self.S, self.di
        NH = self.NH
        sel = S.sb("sel", [128, 2], F32)
        self.LD(sel.ap, di["sel"].ap, [self.din_res], [sel])
        a = [S.sb(f"a{i}", [128, D], F32) for i in range(2)]
        b = [S.sb(f"b{i}", [128, D], F32) for i in range(2)]
        NS1 = self.NS1
        for s_ in range(NS1):
            j = s_ % 2
            r0, r1 = (self.NPT + s_) * 128, (self.NPT + NS1 + s_) * 128
            self.LD(a[j].ap, self.Hd[r0:r0 + 128, :], [self.Hd], [a[j]])
            self.LD(b[j].ap, self.Hd[r1:r1 + 128, :], [self.Hd], [b[j]])
            self.TS("pool", a[j].ap, a[j].ap, sel[:, 0:1], None, ALU.mult, None, [a[j], sel], [a[j]])
            self.STT(a[j].ap, b[j].ap, sel[:, 1:2], a[j].ap, ALU.mult, ALU.add, [b[j], sel, a[j]], [a[j]])
            self.LD(self.Hm[(NH + s_) * 128:(NH + s_ + 1) * 128, :], a[j].ap, [a[j]], [self.Hm])
        ksa = [S.sb(f"ksa{i}", [64, 4, 8], BF16) for i in range(2)]
        ksb = [S.sb(f"ksb{i}", [64, 4, 8], BF16) for i in range(2)]
        vsa = [S.sb(f"vsa{i}", [8, 256], BF16) for i in range(2)]
        vsb = [S.sb(f"vsb{i}", [8, 256], BF16) for i in range(2)]
        for s_ in range(NS1):
            j = s_ % 2
            self.LD(ksa[j].ap, self.KTs[s_, 0:64, :, MAXW:MAXW + 8], [self.KTs], [ksa[j]])
            self.LD(ksb[j].ap, self.KTs[NS1 + s_, 0:64, :, MAXW:MAXW + 8], [self.KTs], [ksb[j]])
            self.TS("pool", ksa[j].ap, ksa[j].ap, sel[0:64, 0:1], None, ALU.mult, None, [ksa[j], sel], [ksa[j]])
            self.STT(ksa[j].ap, ksb[j].ap, sel[0:64, 1:2], ksa[j].ap, ALU.mult, ALU.add, [ksb[j], sel, ksa[j]], [ksa[j]])
            self.LD(self.KTsm[s_, 0:64, :, MAXW:MAXW + 8], ksa[j].ap, [ksa[j]], [self.KTsm])
            self.LD(vsa[j].ap, self.VDs[s_, MAXW:MAXW + 8, :], [self.VDs], [vsa[j]])
            self.LD(vsb[j].ap, self.VDs[NS1 + s_, MAXW:MAXW + 8, :], [self.VDs], [vsb[j]])
            self.TS("pool", vsa[j].ap, vsa[j].ap, sel[0:8, 0:1], None, ALU.mult, None, [vsa[j], sel], [vsa[j]])
            self.STT(vsa[j].ap, vsb[j].ap, sel[0:8, 1:2], vsa[j].ap, ALU.mult, ALU.add, [vsb[j], sel, vsa[j]], [vsa[j]])
            self.LD(self.VDsm[s_, MAXW:MAXW + 8, :], vsa[j].ap, [vsa[j]], [self.VDsm])

    def attn_cfgs(self):
        B = self.BLK
        c = {"p0": (1, 128, 1, 0), "p1": (4, 32, 4, 1), "p2": (16, B // 16, 16, 2),
             "s0": (1, 8, 1, 0), "s1": (4, 2, 4, 1), "s2": (16, 1, 8, 2)}
        return c

    def attn_group(self, name, QT, qcol0, KT, kpos0, Vsrc, vrow0, ACC, acol0, first, bufs):
        d, nq, nres, g = self.cfgs[name]
        bias = self.biasT[name]
        nk = nq + 128
        ktl = [(0, 128), (128, nk)]
        vt, pt_, tmpf, pS, pO = bufs
        vres = self.vres
        merged = 8 * nq <= 512
        W4 = 4 * nq
        units = []
        for rho in range(nres):
            shared = {}
            for kvh in range(4):
                st = {}

                def A(rho=rho, kvh=kvh, st=st, shared=shared):
                    if kvh == 0:
                        vts = []
                        for ki, (j0, j1) in enumerate(ktl):
                            vtile = vt[self.vcnt % len(vt)]
                            self.vcnt += 1
                            r0 = vrow0 + rho + d * (j0 - 128)
                            src = Vsrc[r0:r0 + d * (j1 - j0 - 1) + 1:d, :] if d > 1 else Vsrc[r0:r0 + (j1 - j0), :]
                            self.LD(vtile[0:j1 - j0, :], src, [vres], [vtile])
                            vts.append(vtile)
                        shared["vts"] = vts
                    q0 = qcol0 + rho
                    qs = QT[:, g, kvh * 4:(kvh + 1) * 4, q0:q0 + d * (nq - 1) + 1:d] if d > 1 else QT[:, g, kvh * 4:(kvh + 1) * 4, q0:q0 + nq]
                    pts = []
                    if merged:
                        c = self.acnt % len(pS)
                        self.acnt += 1
                        ps_, tf = pS[c], tmpf[c % len(tmpf)]
                        p_ = pt_[self.pcnt % len(pt_)]
                        self.pcnt += 1
                    for ki, (j0, j1) in enumerate(ktl):
                        nkk = j1 - j0
                        if not merged:
                            c = self.acnt % len(pS)
                            self.acnt += 1
                            ps_, tf = pS[c], tmpf[c % len(tmpf)]
                            p_ = pt_[self.pcnt % len(pt_)]
                            self.pcnt += 1
                        co = ki * W4 if merged else 0
                        k0 = kpos0 + rho + d * (j0 - 128)
                        kslice = KT[:, kvh, k0:k0 + d * (nkk - 1) + 1:d] if d > 1 else KT[:, kvh, k0:k0 + nkk]
                        out = ps_[0:nkk, co:co + W4].rearrange("p (h q) -> p h q", h=4)
                        self.MM(out, kslice, qs, True, True, [KT, QT], [ps_])
                        if not merged:
                            self.TT_("dve", tf[0:nkk, 0:W4], ps_[0:nkk, 0:W4], bias[0:nkk, kvh, ki, :], ALU.add, [ps_, bias], [tf])
                            self.ACT(p_[0:nkk, 0:W4], tf[0:nkk, 0:W4], AF.Exp, [tf], [p_])
                        pts.append((p_, nkk, co))
                    if merged:
                        self.TT_("dve", tf[:, 0:2 * W4], ps_[:, 0:2 * W4], bias[:, kvh, :, :].rearrange("p a c -> p (a c)"), ALU.add, [ps_, bias], [tf])
                        self.ACT(p_[:, 0:2 * W4], tf[:, 0:2 * W4], AF.Exp, [tf], [p_])
                    st["pts"] = pts

                def B(rho=rho, kvh=kvh, st=st, shared=shared):
                    pts, vts = st["pts"], shared["vts"]
                    if merged:
                        hf = self.ocnt % 2
                        self.ocnt += 1
                        po_ = pO[0][0:64, hf * 512:(hf + 1) * 512]
                        pres = [pO[0].part(hf)]
                    else:
                        po_ = pO[0][0:64, :]
                        pres = [pO[0].part(0), pO[0].part(1)]
                    for ki, (p_, nkk, co) in enumerate(pts):
                        self.MM(po_[:, 0:W4], vts[ki][0:nkk, kvh * 64:(kvh + 1) * 64], p_[0:nkk, co:co + W4], ki == 0, ki == 1, [vts[ki], p_], pres)
                    for ki, (p_, nkk, co) in enumerate(pts):
                        self.MM(po_[:, W4:2 * W4], self.ones_b[0:nkk, :], p_[0:nkk, co:co + W4], ki == 0, ki == 1, [self.ones_b, p_], pres)
                    a0 = acol0 + rho
                    dst = ACC[:, :, kvh * 4:(kvh + 1) * 4, a0:a0 + d * (nq - 1) + 1:d] if d > 1 else ACC[:, :, kvh * 4:(kvh + 1) * 4, a0:a0 + nq]
                    srcp = po_[:, 0:2 * W4].rearrange("p (a h q) -> p a h q", a=2, h=4)
                    if first:
                        self.CP("dve", dst, srcp, pres, [ACC])
                    else:
                        self.TT_("dve", dst, srcp, dst, ALU.add, pres + [ACC], [ACC])
                units.append((A, B))
        return units

    def run_units(self, units, L=3):
        n = len(units)
        for u in range(min(L, n)):
            units[u][0]()
        for u in range(n):
            if u + L < n:
                units[u + L][0]()
            units[u][1]()

    def ph_attn(self):
        S, di = self.S, self.di
        BLK = self.BLK
        nw = self.bcast_row("nw", di["norm_w"].ap[1, 1], D)
        wo = S.sb("wo", [64, H, D], BF16)
        S.dma("pool", wo.ap, di["attn_wo"].ap.rearrange("(h p) c -> p h c", p=64), [self.din_res], [wo])
        self.biasT = {}
        for name, (d, nq, nres, g) in self.cfgs.items():
            bt = S.sb(f"bias_{name}", [128, 4, 2, 4 * nq], F32)
            self.MSET("pool", bt.ap, 0.0, [bt])
            src = di[f"bias_{name}"].ap
            self.LD(bt[:, :, 0, :], src[:, 0:128, :].rearrange("k j c -> j k c"), [self.din_res], [bt])
            self.LD(bt[0:nq, :, 1, :], src[:, 128:128 + nq, :].rearrange("k j c -> j k c"), [self.din_res], [bt])
            self.biasT[name] = bt
        KT = S.sb("KT", [65, 4, MAXW + BLK], BF16)
        QT = S.sb("QT", [65, 3, H, BLK], BF16)
        self.MSET("pool", QT[64:65, :, :, :], 1.0, [QT])
        ACC = S.sb("ACC", [64, 2, H, BLK], F32)
        fin = S.sb("fin", [64, H, BLK], BF16)
        h = [S.sb(f"h{i}", [128, D], F32) for i in range(2)]
        xn = [S.sb(f"xn{i}", [128, D], BF16) for i in range(2)]
        xT = S.sb("xT", [128, 8, BLK], BF16)
        scr = S.sb("scr", [128, D], F32)
        st = [S.sb(f"st{i}", [128, 4], F32) for i in range(2)]
        vt = [S.sb(f"vt{i}", [128, 256], BF16) for i in range(8)]
        pt_ = [S.sb(f"pt{i}", [128, 512], BF16) for i in range(8)]
        tmpf = [S.sb(f"tf{i}", [128, 512], F32) for i in range(4)]
        psT1 = S.ps("psT", [128, D], BF16)
        psT = [psT1, psT1]
        pq1 = S.ps("pq", [64, 512])
        pq = [pq1, pq1]
        pS = [S.ps(f"pS{i}", [128, 512]) for i in range(4)]
        pO = [S.ps("pO0", [64, 1024])]
        bufs = (vt, pt_, tmpf, pS, pO)
        self.vcnt = self.acnt = self.pcnt = self.ocnt = 0
        wq = di["attn_wq"].ap
        wqb = [S.sb(f"wqb{i}", [128, 8, 512], BF16) for i in range(2)]

        def qproj(tiles, ncols):
            for i, t in enumerate(tiles):
                j = i % 2
                self.LD(h[j].ap, self.cH[t * 128:(t + 1) * 128, :], [self.cH], [h[j]])
                self.norm_tile(h[j], nw, xn[j], scr, st[j])
                self.transpose_cols(xn[j], D, xT[:, :, i * 128:(i + 1) * 128], xT, psT[j])
            qc = 0
            for cb_ in range(6):
                w = wqb[cb_ % 2]
                S.dma("pool", w.ap, wq[:, cb_ * 512:(cb_ + 1) * 512].rearrange("(k p) c -> p k c", p=128), [self.din_res], [w])
                for hh in range(8):
                    gh = cb_ * 8 + hh
                    g, hd = gh // 16, gh % 16
                    pz = pq[qc % 2]
                    qc += 1
                    for k in range(8):
                        self.MM(pz[:, 0:ncols], w[:, k, hh * 64:(hh + 1) * 64], xT[:, k, 0:ncols], k == 0, k == 7, [w, xT], [pz])
                    self.ACT(QT[0:64, g, hd, 0:ncols], pz[:, 0:ncols], AF.Copy, [pz], [QT], scale=0.125)

        def finish(tiles, ncols_valid):
            self.RCP(ACC[:, 1, :, 0:ncols_valid], ACC[:, 1, :, 0:ncols_valid], [ACC], [ACC])
            self.TT_("dve", fin[:, :, 0:ncols_valid], ACC[:, 0, :, 0:ncols_valid], ACC[:, 1, :, 0:ncols_valid], ALU.mult, [ACC], [fin])
            for i, t in enumerate(tiles):
                j = i % 2
                nv = min(128, ncols_valid - i * 128)
                self.LD(h[j].ap, self.cH[t * 128:(t + 1) * 128, :], [self.cH], [h[j]])
                for hh in range(2):
                    cs = slice(hh * 512, (hh + 1) * 512)
                    for hd in range(H):
                        self.MM(pS[hh][0:nv, :], fin[:, hd, i * 128:i * 128 + nv], wo[:, hd, cs], hd == 0, hd == H - 1, [fin, wo], [pS[hh]])
                    self.TT_("dve", h[j][0:nv, cs], pS[hh][0:nv, :], h[j][0:nv, cs], ALU.add, [pS[hh], h[j]], [h[j]])
                self.LD(self.cH[t * 128:(t + 1) * 128, :], h[j].ap, [h[j]], [self.cH])

        self.vres = self.cVD
        nblk = (self.cNPT * 128) // BLK
        tpb = BLK // 128
        for bi in range(nblk):
            tiles = list(range(bi * tpb, (bi + 1) * tpb))
            qproj(tiles, BLK)
            base = bi * BLK
            for hd in range(4):
                self.LD(KT[:, hd, :], self.cKT[:, hd, base:base + MAXW + BLK], [self.cKT], [KT])
            units = []
            for si in range(tpb):
                units += self.attn_group("p0", QT, si * 128, KT, MAXW + si * 128, self.cVD.ap, MAXW + base + si * 128, ACC, si * 128, True, bufs)
            for si in range(tpb):
                units += self.attn_group("p1", QT, si * 128, KT, MAXW + si * 128, self.cVD.ap, MAXW + base + si * 128, ACC, si * 128, False, bufs)
            units += self.attn_group("p2", QT, 0, KT, MAXW, self.cVD.ap, MAXW + base, ACC, 0, False, bufs)
            self.run_units(units)
            finish(tiles, BLK)
        for b in range(self.NS1):
            t = self.cNPT + b
            kts = KT
            for hd in range(4):
                self.LD(kts[:, hd, 0:MAXW + 8], self.KTsm[b, :, hd, :], [self.KTsm], [kts])
            self.vres = self.VDsm
            qproj([t], 128)
            units = []
            for gi, nm in enumerate(("s0", "s1", "s2")):
                units += self.attn_group(nm, QT, 0, kts, MAXW, self.VDsm.ap[b], MAXW, ACC, 0, gi == 0, bufs)
            self.run_units(units)
            finish([t], 8)


def t5_buckets(dist):
    d = np.asarray(dist, dtype=np.int64)
    max_exact = 16
    large = max_exact + (np.log(np.maximum(d, 1) / max_exact) / np.log(2048 / max_exact) * (32 - max_exact)).astype(np.int32)
    large = np.minimum(large, 31)
    return np.where(d < max_exact, d, large).astype(np.int32)


def make_bias_tables(rel_bias, cfgs):
    out = {}
    for name, (d, nq, nres, g) in cfgs.items():
        nk = nq + 128
        j = np.arange(nk)[:, None]
        i = np.arange(nq)[None, :]
        m = i - j + 128
        valid = (m >= 0) & (m <= 128)
        bk = t5_buckets(d * np.clip(m, 0, 128))
        tab = np.empty((4, nk, 4, nq), np.float32)
        for kvh in range(4):
            for hq in range(4):
                col = g * 16 + kvh * 4 + hq
                tab[kvh, :, hq, :] = np.where(valid, rel_bias[bk, col], np.float32(NEG))
        out[name] = np.ascontiguousarray(tab.reshape(4, nk, 4 * nq))
    return out


def make_consts():
    s = np.arange(128)[:, None]
    t = np.arange(128)[None, :]
    c = {}
    c["c_ident"] = np.eye(128, dtype=np.float32)
    c["c_tri"] = np.concatenate([(s <= t), (s > t)], 1).astype(np.float32)
    c["c_m1"] = np.concatenate([(t < s), (s < t), (s <= t)], 1).astype(np.float32)
    c["c_m2"] = np.concatenate([(s < t), (s <= t)], 1).astype(np.float32)
    rm = np.zeros((128, 2), np.float32)
    rm[:, 0] = 1.0
    rm[:8, 1] = 1.0
    c["c_rowmask"] = rm
    return c


_CACHE = {}


def get_builder(NPT, dbg=False):
    key = (NPT, dbg)
    if key not in _CACHE:
        b = Builder(NPT, dbg)
        b.build()
        _CACHE[key] = b
    return _CACHE[key]


def core_inputs(b, c, half, x_prompt_seq, x_sample, state_wkv, state_shift, cache_kv, p_prompt_seq, p_sample, weights, consts, bias_tabs):
    NTOK, NP = b.NTOK, b.NP
    xin = np.zeros((NTOK, D), np.float32)
    xin[:NP] = x_prompt_seq
    pin = np.zeros((2, NTOK, 256), np.float32)
    pin[:, :NP] = p_prompt_seq
    for s in range(NSB):
        r0 = NP + s * 128
        xin[r0:r0 + 8] = x_sample[s]
        pin[:, r0:r0 + 8] = p_sample[:, s]
    NHT = b.NH * 128
    pin1 = np.zeros((b.MTOK, 256), np.float32)
    pin1[:NHT] = p_prompt_seq[1, half * NHT:(half + 1) * NHT]
    for s in range(b.NS1):
        pin1[NHT + s * 128:NHT + s * 128 + 8] = p_sample[1, half * b.NS1 + s]
    sel = np.zeros((128, 2), np.float32)
    sel[:, half] = 1.0
    m = {"sel": sel, "pin1": pin1, "xin": xin, "pin": pin, "wkv0": np.ascontiguousarray(state_wkv), "shift0": np.ascontiguousarray(state_shift),
         "cache": np.ascontiguousarray(cache_kv.reshape(NSB, MAXW, 512)[half * b.NS1:(half + 1) * b.NS1])}
    m.update(weights)
    m.update(consts)
    for name, tab in bias_tabs.items():
        m[f"bias_{name}"] = tab
    return m


def kernel(x_prompt, x_sample, state_wkv, state_shift, cache_kv, p_prompt, p_sample,
           norm_w, ffn1_wi, ffn1_wo, ffn2_wi, ffn2_wo, pe_proj, pe_gate,
           rwkv_mix, rwkv_wrkv, rwkv_wo, rwkv_w0, rwkv_w1, rwkv_w2, rwkv_a0, rwkv_a1, rwkv_a2,
           rwkv_g1, rwkv_g2, rwkv_kk, rwkv_ka, rwkv_rk, rwkv_lnx_w, rwkv_lnx_b,
           attn_wq, attn_wo, kv_norm, w_kv, rel_bias, final_norm, _dbg=False):
    f = lambda a: np.ascontiguousarray(np.asarray(a, dtype=np.float32))
    x_prompt, x_sample, p_prompt, p_sample = f(x_prompt), f(x_sample), f(p_prompt), f(p_sample)
    state_wkv, state_shift, cache_kv = f(state_wkv), f(state_shift), f(cache_kv)
    B, T, _ = x_prompt.shape
    NPT = T // 128
    b = get_builder(NPT, _dbg)
    weights = {"norm_w": f(norm_w), "ffn1_wi": f(ffn1_wi), "ffn1_wo": f(ffn1_wo), "ffn2_wi": f(ffn2_wi), "ffn2_wo": f(ffn2_wo),
               "pe_proj": f(pe_proj), "pe_gate": f(pe_gate), "rwkv_mix": f(rwkv_mix)[0], "rwkv_wrkv": f(rwkv_wrkv)[0],
               "rwkv_wo": f(rwkv_wo)[0], "rwkv_w0": f(rwkv_w0)[0], "rwkv_w1": f(rwkv_w1)[0], "rwkv_w2": f(rwkv_w2)[0],
               "rwkv_a0": f(rwkv_a0)[0], "rwkv_a1": f(rwkv_a1)[0], "rwkv_a2": f(rwkv_a2)[0], "rwkv_g1": f(rwkv_g1)[0],
               "rwkv_g2": f(rwkv_g2)[0], "rwkv_kk": f(rwkv_kk)[0], "rwkv_ka": f(rwkv_ka)[0], "rwkv_rk": f(rwkv_rk)[0].reshape(-1),
               "rwkv_lnx_w": f(rwkv_lnx_w)[0], "rwkv_lnx_b": f(rwkv_lnx_b)[0], "attn_wq": f(attn_wq)[0], "attn_wo": f(attn_wo)[0],
               "kv_norm": f(kv_norm), "w_kv": f(w_kv), "final_norm": f(final_norm)}
    consts = make_consts()
    bias_tabs = make_bias_tables(f(rel_bias), b.cfgs)
    nsb_total = x_sample.shape[0]
    ngrp = nsb_total // NSB
    in_maps = []
    for c in range(8):
        pb = c % B
        sg = c % ngrp
        sl = slice(sg * NSB, (sg + 1) * NSB)
        in_maps.append(core_inputs(b, c, c // B, x_prompt[pb], x_sample[sl], state_wkv[0, sl], state_shift[0, sl], cache_kv[sl],
                                   p_prompt[:, pb], p_sample[:, sl], weights, consts, bias_tabs))
    res = run_bass_kernel_spmd(b.nc, in_maps, core_ids=list(range(8)))
    R = res.results
    NP = b.NP
    NHT = b.NH * 128
    y_prompt = np.stack([np.concatenate([R[c]["yout"][:NHT], R[c + B]["yout"][:NHT]], 0) for c in range(B)])
    kvw = min(MAXW, T)
    kv_prompt = np.stack([R[c]["kvout"][NP - kvw:NP].reshape(kvw, 2, 4, 64) for c in range(B)])
    wkv_prompt = np.stack([R[c]["wkvp"] for c in range(B)])[None]
    shift_prompt = np.stack([R[c]["shiftall"][NP - 1] for c in range(B)])[None]
    ys, kvs, wks, shs = [], [], [], []
    for sg in range(ngrp):
        r = R[sg]
        for s in range(NSB):
            r0 = NP + s * 128
            rr = R[sg + (s // b.NS1) * B]
            ys.append(rr["yout"][NHT + (s % b.NS1) * 128:NHT + (s % b.NS1) * 128 + 8])
            kvs.append(r["kvout"][r0:r0 + 8].reshape(8, 2, 4, 64))
            shs.append(r["shiftall"][r0 + 7])
        wks.append(r["wkvs"])
    y_sample = np.stack(ys)
    kv_sample = np.stack(kvs)
    wkv_sample = np.concatenate(wks, 0)[None]
    shift_sample = np.stack(shs)[None]
    outs = (y_prompt, y_sample, wkv_prompt, shift_prompt, kv_prompt, wkv_sample, shift_sample, kv_sample)
    outs = tuple(np.ascontiguousarray(o, dtype=np.float32) for o in outs)
    if _dbg:
        return outs, R
    return outs
```

```python
from contextlib import ExitStack
import numpy as np
import ml_dtypes
import concourse.bass as bass
import concourse.mybir as mybir
from concourse.bass_utils import run_bass_kernel_spmd

F32 = mybir.dt.float32
BF16 = mybir.dt.bfloat16
ALU = mybir.AluOpType
AF = mybir.ActivationFunctionType
AX = mybir.AxisListType

D = 1024
DFF = 2816
NFC = DFF // 128
H = 16
NSB = 8
MAXW = 2048
NEG = -1e30
GROUPS = ((128, 1), (512, 4), (2048, 16))


class Res:
    __slots__ = ("name", "w", "r", "psum")

    def __init__(self, name, psum=False):
        self.name = name
        self.w = {}
        self.r = {}
        self.psum = psum


class T:
    def __init__(self, name, ap):
        self.name = name
        self.ap = ap
        self.res = Res(name)
        self._parts = {}

    def part(self, key):
        if key not in self._parts:
            self._parts[key] = Res(f"{self.name}.{key}", self.res.psum)
        return self._parts[key]

    def __getitem__(self, k):
        return self.ap[k]


def _res(x):
    return x.res if isinstance(x, T) else x


class Sched:
    COMPUTE = ("pe", "dve", "act", "pool")
    NDMASEM = 8

    def __init__(self, nc):
        self.nc = nc
        self.gstack = ExitStack()
        self.stack = self.gstack
        self.streams = {k: [] for k in ("pe", "dve", "act", "pool", "sp")}
        self.sems = {}
        self.cnt = {}
        for k in self.COMPUTE:
            self.sems[k] = self.gstack.enter_context(nc.semaphore(f"s_{k}"))
            self.cnt[k] = 0
        self.dq = {}
        for q in ("sp", "pool"):
            sl = []
            for i in range(self.NDMASEM):
                key = f"d_{q}{i}"
                self.sems[key] = self.gstack.enter_context(nc.semaphore(key))
                self.cnt[key] = 0
                sl.append(key)
            self.dq[q] = [sl, 0]
        self.known = {k: {} for k in self.streams}
        self.n_ops = 0
        self.uid = 0

    def sb(self, name, shape, dtype):
        self.uid += 1
        h = self.stack.enter_context(self.nc.sbuf_tensor(f"{name}_{self.uid}", list(shape), dtype))
        return T(name, h[:])

    def ps(self, name, shape, dtype=F32):
        self.uid += 1
        h = self.stack.enter_context(self.nc.psum_tensor(f"{name}_{self.uid}", list(shape), dtype))
        t = T(name, h[:])
        t.res.psum = True
        return t

    def dram(self, name, shape, dtype, kind="Internal"):
        h = self.nc.dram_tensor(name, list(shape), dtype, kind=kind)
        return T(name, h.ap())

    def _collect(self, ekey, reads, writes, is_dma=False):
        deps = {}

        def add(d, same_ok):
            for s, v in d.items():
                if s == ekey and not (same_ok or is_dma):
                    continue
                if deps.get(s, 0) < v:
                    deps[s] = v

        for r in reads:
            add(_res(r).w, same_ok=(ekey != "pe"))
            if _res(r).psum:
                add(_res(r).r, same_ok=False)
        for w in writes:
            add(_res(w).w, same_ok=False)
            add(_res(w).r, same_ok=False)
        kn = self.known[ekey]
        out = []
        for s, v in deps.items():
            if kn.get(s, 0) < v:
                kn[s] = v
                out.append((s, v))
        return out

    def op(self, ekey, fn, reads=(), writes=()):
        waits = self._collect(ekey, reads, writes)
        self.cnt[ekey] += 1
        v = self.cnt[ekey]
        self.streams[ekey].append((waits, fn, (ekey, 1)))
        for r in reads:
            _res(r).r[ekey] = v
        for w in writes:
            _res(w).w[ekey] = v
        self.n_ops += 1

    def dma(self, q, out_ap, in_ap, reads=(), writes=(), **kw):
        sl, i = self.dq[q]
        key = sl[i % self.NDMASEM]
        self.dq[q][1] = i + 1
        waits = self._collect(q, reads, writes, is_dma=True)
        prev = self.cnt[key]
        kn = self.known[q]
        if prev > 0 and kn.get(key, 0) < prev:
            kn[key] = prev
            waits.append((key, prev))
        val = prev + 16
        self.cnt[key] = val

        def fn(e, out_ap=out_ap, in_ap=in_ap, kw=kw):
            return e.dma_start(out=out_ap, in_=in_ap, **kw)

        self.streams[q].append((waits, fn, (key, 16)))
        for r in reads:
            _res(r).r[key] = val
        for w in writes:
            _res(w).w[key] = val
        self.n_ops += 1

    def barrier(self):
        for ekey in self.streams:
            kn = self.known[ekey]
            waits = []
            for s, v in self.cnt.items():
                if s == ekey or v == 0:
                    continue
                if kn.get(s, 0) < v:
                    kn[s] = v
                    waits.append((s, v))
            self.streams[ekey].append((waits, None, None))

    def emit(self):
        nc = self.nc
        sems = self.sems
        streams = self.streams
        with nc.Block() as block:
            def mk(ekey):
                def body(e):
                    for waits, fn, inc in streams[ekey]:
                        for s, v in waits:
                            e.wait_ge(sems[s], v)
                        if fn is not None:
                            fn(e).then_inc(sems[inc[0]], inc[1])
                return body
            block.tensor(mk("pe"))
            block.vector(mk("dve"))
            block.scalar(mk("act"))
            block.gpsimd(mk("pool"))
            block.sync(mk("sp"))
        self.streams = {k: [] for k in streams}


class Builder:
    def __init__(self, NPT, dbg=False):
        self.NPT = NPT
        self.NP = NPT * 128
        self.TT = NPT + NSB
        self.NTOK = self.TT * 128
        self.NH = NPT // 2
        self.NS1 = NSB // 2
        self.MT = self.NH + self.NS1
        self.MTOK = self.MT * 128
        self.BLK = min(256, self.NH * 128)
        self.dbg = dbg
        self.nc = bass.Bass("TRN2", target_bir_lowering=False)
        self.S = Sched(self.nc)
        self.outs = []

    def MM(self, out, lhsT, rhs, start, stop, R, W):
        self.S.op("pe", lambda e: e.matmul(out, lhsT, rhs, start=start, stop=stop), R, W)

    def TR(self, out, in_, ident, R, W):
        self.S.op("pe", lambda e: e.transpose(out, in_, ident), R, W)

    def ACT(self, out, in_, func, R, W, **kw):
        self.S.op("act", lambda e: e.activation(out, in_, func, **kw), R, W)

    def TT_(self, eng, out, a, b, op, R, W):
        self.S.op(eng, lambda e: e.tensor_tensor(out, a, b, op), R, W)

    def TS(self, eng, out, a, s1, s2, op0, op1, R, W):
        if op1 is None:
            self.S.op(eng, lambda e: e.tensor_scalar(out, a, s1, None, op0), R, W)
        else:
            self.S.op(eng, lambda e: e.tensor_scalar(out, a, s1, s2, op0, op1), R, W)

    def STT(self, out, a, s, b, op0, op1, R, W):
        self.S.op("dve", lambda e: e.scalar_tensor_tensor(out, a, s, b, op0, op1), R, W)

    def CP(self, eng, out, in_, R, W):
        if eng == "act":
            self.S.op("act", lambda e: e.activation(out, in_, AF.Copy), R, W)
        else:
            self.S.op(eng, lambda e: e.tensor_copy(out, in_), R, W)

    def RED(self, out, in_, op, R, W):
        self.S.op("dve", lambda e: e.tensor_reduce(out, in_, AX.X, op), R, W)

    def RCP(self, out, in_, R, W):
        self.S.op("dve", lambda e: e.reciprocal(out, in_), R, W)

    def MSET(self, eng, ap, val, W):
        self.S.op(eng, lambda e: e.memset(ap, val), (), W)

    def LD(self, out, in_, R, W, q="sp"):
        self.S.dma(q, out, in_, R, W)

    def phase(self, fn, *a):
        S = self.S
        import os
        self._phi = getattr(self, "_phi", 0) + 1
        lim = int(os.environ.get("K_MAXPH", "999"))
        if self._phi > lim:
            return
        print("PHASE", self._phi, getattr(fn, "__name__", "?"), a, flush=True)
        with ExitStack() as st:
            S.stack = st
            fn(*a)
            S.barrier()
            S.emit()
        S.stack = S.gstack

    def bcast_row(self, name, src_ap_1d, n):
        t = self.S.sb(name, [128, n], F32)
        self.LD(t.ap, src_ap_1d.partition_broadcast(128), [self.din_res], [t])
        return t

    def norm_tile(self, h, nw, out, scr, st):
        self.MSET("dve", st[:, 0:1], 0.0, [st])
        self.ACT(scr.ap, h.ap, AF.Square, [h, st], [scr, st], accum_out=st[:, 0:1])
        self.TS("dve", st[:, 1:2], st[:, 0:1], 1.0 / D, 1e-6, ALU.mult, ALU.add, [st], [st])
        self.ACT(st[:, 1:2], st[:, 1:2], AF.Sqrt, [st], [st])
        self.RCP(st[:, 2:3], st[:, 1:2], [st], [st])
        self.STT(out.ap, h.ap, st[:, 2:3], nw.ap, ALU.mult, ALU.mult, [h, st, nw], [out])

    def transpose_cols(self, src, ncol, dst_ap3, dst_res, psT, eng="dve"):
        kc = ncol // 128
        for k in range(kc):
            self.TR(psT[:, k * 128:(k + 1) * 128], src[:, k * 128:(k + 1) * 128], self.identb.ap, [src, self.identb], [psT])
        self.CP(eng, dst_ap3, psT[:, 0:ncol].rearrange("p (k t) -> p k t", k=kc), [psT], [dst_res])

    def load_w(self, name, w2d, rows, cols, c0=0, c1=None):
        c1 = cols if c1 is None else c1
        kc = rows // 128
        t = self.S.sb(name, [128, kc, c1 - c0], BF16)
        self.LD(t.ap, w2d[:, c0:c1].rearrange("(k p) c -> p k c", p=128), [self.din_res], [t], q="pool")
        return t

    def build(self):
        S, nc = self.S, self.nc
        NTOK, TT, NP, NPT = self.NTOK, self.TT, self.NP, self.NPT
        di = {}
        self.din_res = Res("inputs")

        def inp(name, shape):
            di[name] = S.dram(name, shape, F32, kind="ExternalInput")
            return di[name]

        inp("xin", [NTOK, D]); inp("pin", [2, NTOK, 256]); inp("wkv0", [NSB, H, 64, 64]); inp("shift0", [NSB, D])
        inp("cache", [NSB // 2, MAXW, 512]); inp("sel", [128, 2]); inp("pin1", [self.MTOK, 256])
        inp("norm_w", [2, 4, D]); inp("ffn1_wi", [2, D, 2 * DFF]); inp("ffn1_wo", [2, DFF, D])
        inp("ffn2_wi", [2, D, 2 * DFF]); inp("ffn2_wo", [2, DFF, D]); inp("pe_proj", [2, 256, D]); inp("pe_gate", [2, D, D])
        inp("rwkv_mix", [6, D]); inp("rwkv_wrkv", [3, D, D]); inp("rwkv_wo", [D, D]); inp("rwkv_w0", [D])
        inp("rwkv_w1", [D, 64]); inp("rwkv_w2", [64, D]); inp("rwkv_a0", [D]); inp("rwkv_a1", [D, 64]); inp("rwkv_a2", [64, D])
        inp("rwkv_g1", [D, 160]); inp("rwkv_g2", [160, D]); inp("rwkv_kk", [D]); inp("rwkv_ka", [D]); inp("rwkv_rk", [D])
        inp("rwkv_lnx_w", [D]); inp("rwkv_lnx_b", [D]); inp("attn_wq", [D, 3 * D]); inp("attn_wo", [D, D])
        inp("kv_norm", [D]); inp("w_kv", [D, 512]); inp("final_norm", [D])
        inp("c_ident", [128, 128]); inp("c_tri", [128, 256]); inp("c_m1", [128, 384]); inp("c_m2", [128, 256])
        inp("c_rowmask", [128, 2])
        self.cfgs = self.attn_cfgs()
        for name, (d, nq, nres, g) in self.cfgs.items():
            inp(f"bias_{name}", [4, nq + 128, 4 * nq])
        self.di = di
        def outp(name, shape):
            t = S.dram(name, shape, F32, kind="ExternalOutput")
            self.outs.append(t)
            return t
        self.yout = outp("yout", [self.MTOK, D]); self.kvout = outp("kvout", [NTOK, 512])
        self.wkvp = outp("wkvp", [H, 64, 64]); self.wkvs = outp("wkvs", [NSB, H, 64, 64])
        self.shiftall = outp("shiftall", [NTOK, D])
        self.Hd = S.dram("Hd", [NTOK, D], F32)
        self.Hm = S.dram("Hm", [self.MTOK, D], F32)
        WM = MAXW + self.NH * 128
        self.KTm = S.dram("KTm", [65, 4, WM], BF16); self.VDm = S.dram("VDm", [WM, 256], BF16)
        self.KTsm = S.dram("KTsm", [self.NS1, 65, 4, MAXW + 8], BF16); self.VDsm = S.dram("VDsm", [self.NS1, MAXW + 8, 256], BF16)
        self.HN = S.dram("HN", [NTOK + 1, D], F32)
        self.Rd = S.dram("Rd", [NTOK, D], F32); self.Kd = S.dram("Kd", [NTOK, D], F32); self.Vd = S.dram("Vd", [NTOK, D], F32)
        self.KKd = S.dram("KKd", [NTOK, D], F32); self.Bd = S.dram("Bd", [NTOK, D], F32); self.LWd = S.dram("LWd", [NTOK, D], F32)
        self.Gd = S.dram("Gd", [NTOK, D], F32); self.BONd = S.dram("BONd", [NTOK, H], F32); self.Yd = S.dram("Yd", [NTOK, D], F32)
        self.KTp = S.dram("KTp", [65, 4, MAXW + NP], BF16); self.VDp = S.dram("VDp", [MAXW + NP, 256], BF16)
        self.KTs = S.dram("KTs", [NSB, 65, 4, MAXW + 8], BF16); self.VDs = S.dram("VDs", [NSB, MAXW + 8, 256], BF16)
        if self.dbg:
            self.dbgH = [outp(f"dbgH{i}", [NTOK if i < 4 else self.MTOK, D]) for i in range(8)]
            self.dbgY = outp("dbgY", [NTOK, D])
        self.identb = S.sb("identb", [128, 128], BF16)
        self.identf = S.sb("identf", [128, 128], F32)
        S.dma("pool", self.identb.ap, di["c_ident"].ap, [self.din_res], [self.identb])
        S.dma("sp", self.identf.ap, di["c_ident"].ap, [self.din_res], [self.identf])
        self.zero = S.sb("zero", [128, D], F32)
        self.MSET("pool", self.zero.ap, 0.0, [self.zero])
        self.ones_b = S.sb("ones_b", [128, 64], BF16)
        self.MSET("pool", self.ones_b.ap, 1.0, [self.ones_b])
        self.ones_f = S.sb("ones_f", [128, 1], F32)
        self.MSET("pool", self.ones_f.ap, 1.0, [self.ones_f])
        self.rowmask = S.sb("rowmask", [128, 2], F32)
        S.dma("sp", self.rowmask.ap, di["c_rowmask"].ap, [self.din_res], [self.rowmask])

        self.ndbg = 0
        self.cH, self.cTT, self.cNPT, self.cKT, self.cVD = self.Hd, self.TT, self.NPT, self.KTp, self.VDp
        self.cPin = lambda layer: di["pin"].ap[layer]
        self.phase(self.ph_init)
        for layer in range(2):
            if layer == 1:
                self.phase(self.ph_kv)
                self.phase(self.ph_select)
                self.phase(self.ph_select_s)
                self.cH, self.cTT, self.cNPT, self.cKT, self.cVD = self.Hm, self.MT, self.NH, self.KTm, self.VDm
                self.cPin = lambda layer: di["pin1"].ap
            self.phase(self.ph_ffn, layer, 0)
            self.snap()
            if layer == 0:
                self.phase(self.ph_rwkv_proj)
                self.phase(self.ph_rwkv_scan)
                self.phase(self.ph_rwkv_post)
            else:
                self.phase(self.ph_attn)
            self.snap()
            self.phase(self.ph_ffn, layer, 2)
            self.snap()
            self.phase(self.ph_pe, layer)
            self.snap()
        return nc

    def snap(self):
        if not self.dbg:
            return
        i = self.ndbg
        self.ndbg += 1

        def f():
            self.S.dma("sp", self.dbgH[i].ap, self.cH.ap, [self.cH], [self.dbgH[i]])
        self.phase(f)

    def ph_init(self):
        S = self.S
        S.dma("sp", self.Hd.ap, self.di["xin"].ap, [self.din_res], [self.Hd])
        S.dma("sp", self.HN[0:1, :], self.zero[0:1, :], [self.zero], [self.HN])
        zb = S.sb("zb", [128, 2048], BF16)
        self.MSET("dve", zb.ap, 0.0, [zb])
        nb = S.sb("nb", [128, 4 * MAXW], BF16)
        self.MSET("dve", nb.ap, NEG, [nb])
        for h in range(4):
            S.dma("sp", self.KTp[0:64, h, 0:MAXW], zb[0:64, :], [zb], [self.KTp])
        S.dma("sp", self.KTp[64:65, :, 0:MAXW], nb[0:1, :].rearrange("p (h n) -> p h n", h=4), [nb], [self.KTp])
        for h in range(4):
            for c0 in range(0, self.NP, 2048):
                c1 = min(self.NP, c0 + 2048)
                S.dma("sp", self.KTp[64:65, h, MAXW + c0:MAXW + c1], zb[0:1, 0:c1 - c0], [zb], [self.KTp])
        for r0 in range(0, MAXW, 128):
            S.dma("sp", self.VDp[r0:r0 + 128, :], zb[:, 0:256], [zb], [self.VDp])
        for b in range(self.NS1):
            for h in range(4):
                S.dma("sp", self.KTsm[b, 64:65, h, 0:2048], zb[0:1, 0:2048], [zb], [self.KTsm])
                S.dma("sp", self.KTsm[b, 64:65, h, 2048:2056], zb[0:1, 0:8], [zb], [self.KTsm])

    def ph_ffn(self, layer, which):
        S = self.S
        wi = self.di["ffn1_wi" if which == 0 else "ffn2_wi"].ap[layer]
        wo = self.di["ffn1_wo" if which == 0 else "ffn2_wo"].ap[layer]
        nw = self.bcast_row("nw", self.di["norm_w"].ap[layer, which], D)
        GT = 8
        groups = [list(range(s, min(self.cTT, s + GT))) for s in range(0, self.cTT, GT)]
        maxg = max(len(g) for g in groups)
        hg = [S.sb(f"hg{i}", [128, D], F32) for i in range(maxg)]
        xT = S.sb("xT", [128, 8, maxg * 128], BF16)
        hid = S.sb("hid", [128, NFC, maxg * 128], BF16)
        wob = S.sb("wob", [128, NFC, D], BF16)
        BW = 512
        blocks = [(c0, min(DFF, c0 + BW)) for c0 in range(0, DFF, BW)]
        wib = [S.sb(f"wib{i}", [128, 8, 2 * BW], BF16) for i in range(2)]
        xn = [S.sb(f"xn{i}", [128, D], BF16) for i in range(2)]
        scr = S.sb("scr", [128, D], F32)
        st = [S.sb(f"st{i}", [128, 4], F32) for i in range(2)]
        sg = [S.sb(f"sg{i}", [128, 512], BF16) for i in range(2)]
        psT = [S.ps(f"psT{i}", [128, D], BF16) for i in range(2)]
        pg = [S.ps(f"pg{i}", [128, 512]) for i in range(2)]
        pu = [S.ps(f"pu{i}", [128, 512]) for i in range(2)]
        po = S.ps("po", [128, D])
        cnt = 0
        nload = [0]
        fuse_norm = (layer == 0 and which == 0)
        if fuse_norm:
            nw1 = self.bcast_row("nw1", self.di["norm_w"].ap[0, 1], D)
            hno = [S.sb(f"hno{i}", [128, D], F32) for i in range(2)]

        def load_block(bi):
            c0, c1 = blocks[bi]
            w = wib[nload[0] % 2]
            nload[0] += 1
            bw = c1 - c0
            S.dma("pool", w[:, :, 0:bw], wi[:, c0:c1].rearrange("(k p) c -> p k c", p=128), [self.din_res], [w])
            S.dma("pool", w[:, :, BW:BW + bw], wi[:, DFF + c0:DFF + c1].rearrange("(k p) c -> p k c", p=128), [self.din_res], [w])
            return w

        pending = load_block(0)
        for gi, g in enumerate(groups):
            ng = len(g)
            ntok = ng * 128
            for i, t in enumerate(g):
                self.LD(hg[i].ap, self.cH[t * 128:(t + 1) * 128, :], [self.cH], [hg[i]])
                self.norm_tile(hg[i], nw, xn[i % 2], scr, st[i % 2])
                self.transpose_cols(xn[i % 2], D, xT[:, :, i * 128:(i + 1) * 128], xT, psT[i % 2])
            for bi, (c0, c1) in enumerate(blocks):
                w = pending
                if bi + 1 < len(blocks):
                    pending = load_block(bi + 1)
                elif gi + 1 < len(groups):
                    pending = load_block(0)
                if bi == 1:
                    S.dma("pool", wob.ap, wo.rearrange("(k p) c -> p k c", p=128), [self.din_res], [wob])
                for cc in range((c1 - c0) // 128):
                    c = c0 // 128 + cc
                    for n0 in range(0, ntok, 512):
                        n1 = min(ntok, n0 + 512)
                        nn = n1 - n0
                        j = cnt % 2
                        cnt += 1
                        for k in range(8):
                            self.MM(pg[j][:, 0:nn], w[:, k, cc * 128:(cc + 1) * 128], xT[:, k, n0:n1], k == 0, k == 7, [w, xT], [pg[j]])
                        for k in range(8):
                            self.MM(pu[j][:, 0:nn], w[:, k, BW + cc * 128:BW + (cc + 1) * 128], xT[:, k, n0:n1], k == 0, k == 7, [w, xT], [pu[j]])
                        self.ACT(sg[j][:, 0:nn], pg[j][:, 0:nn], AF.Silu, [pg[j]], [sg[j]])
                        self.TT_("dve", hid[:, c, n0:n1], pu[j][:, 0:nn], sg[j][:, 0:nn], ALU.mult, [pu[j], sg[j]], [hid])
            for i, t in enumerate(g):
                for hh in range(2):
                    for c in range(NFC):
                        self.MM(po[:, hh * 512:(hh + 1) * 512], hid[:, c, i * 128:(i + 1) * 128], wob[:, c, hh * 512:(hh + 1) * 512],
                                c == 0, c == NFC - 1, [hid, wob], [po])
                self.STT(hg[i].ap, po.ap, 0.5, hg[i].ap, ALU.mult, ALU.add, [po, hg[i]], [hg[i]])
                self.LD(self.cH[t * 128:(t + 1) * 128, :], hg[i].ap, [hg[i]], [self.cH])
                if fuse_norm:
                    o_ = hno[i % 2]
                    self.norm_tile(hg[i], nw1, o_, scr, st[i % 2])
                    self.LD(self.HN[1 + t * 128:1 + (t + 1) * 128, :], o_.ap, [o_], [self.HN])
                    self.LD(self.shiftall[t * 128:(t + 1) * 128, :], o_.ap, [o_], [self.shiftall])

    def ph_pe(self, layer):
        S = self.S
        nw = self.bcast_row("nw", self.di["norm_w"].ap[layer, 3], D)
        wg = self.load_w("wg", self.di["pe_gate"].ap[layer], D, D)
        wp = self.load_w("wp", self.di["pe_proj"].ap[layer], 256, D)
        h = [S.sb(f"h{i}", [128, D], F32) for i in range(2)]
        pb = [S.sb(f"pb{i}", [128, 256], BF16) for i in range(2)]
        xn = [S.sb(f"xn{i}", [128, D], BF16) for i in range(2)]
        xT = [S.sb(f"xT{i}", [128, 8, 128], BF16) for i in range(2)]
        pT = [S.sb(f"pT{i}", [128, 2, 128], BF16) for i in range(2)]
        sgm = [S.sb(f"sgm{i}", [128, D], F32) for i in range(2)]
        scr = S.sb("scr", [128, D], F32)
        st = [S.sb(f"st{i}", [128, 4], F32) for i in range(2)]
        psT = [S.ps(f"psT{i}", [128, D], BF16) for i in range(2)]
        pg = S.ps("pg", [128, D])
        pe = S.ps("pe", [128, D])
        fuse_final = (layer == 1)
        if fuse_final:
            nwf = self.bcast_row("nwf", self.di["final_norm"].ap, D)
            fo = [S.sb(f"fo{i}", [128, D], F32) for i in range(2)]
        for t in range(self.cTT):
            j = t % 2
            rows = slice(t * 128, (t + 1) * 128)
            self.LD(h[j].ap, self.cH[rows, :], [self.cH], [h[j]])
            S.dma("pool", pb[j].ap, self.cPin(layer)[rows, :], [self.din_res], [pb[j]])
            self.norm_tile(h[j], nw, xn[j], scr, st[j])
            self.transpose_cols(xn[j], D, xT[j].ap, xT[j], psT[j])
            self.transpose_cols(pb[j], 256, pT[j].ap, pT[j], psT[j], eng="act")
            for hh in range(2):
                cs = slice(hh * 512, (hh + 1) * 512)
                for k in range(8):
                    self.MM(pg[:, cs], xT[j][:, k, :], wg[:, k, cs], k == 0, k == 7, [xT[j], wg], [pg])
                for k in range(2):
                    self.MM(pe[:, cs], pT[j][:, k, :], wp[:, k, cs], k == 0, k == 1, [pT[j], wp], [pe])
            self.ACT(sgm[j].ap, pg.ap, AF.Sigmoid, [pg], [sgm[j]])
            self.TT_("dve", sgm[j].ap, pe.ap, sgm[j].ap, ALU.mult, [pe, sgm[j]], [sgm[j]])
            self.TT_("pool", h[j].ap, h[j].ap, sgm[j].ap, ALU.add, [h[j], sgm[j]], [h[j]])
            if fuse_final:
                self.norm_tile(h[j], nwf, fo[j], scr, st[j])
                self.LD(self.yout[rows, :], fo[j].ap, [fo[j]], [self.yout], q="pool")
            else:
                self.LD(self.cH[rows, :], h[j].ap, [h[j]], [self.cH], q="pool")

    def ph_final(self):
        S = self.S
        nw = self.bcast_row("nw", self.di["final_norm"].ap, D)
        h = [S.sb(f"h{i}", [128, D], F32) for i in range(2)]
        o = [S.sb(f"o{i}", [128, D], F32) for i in range(2)]
        scr = S.sb("scr", [128, D], F32)
        st = [S.sb(f"st{i}", [128, 4], F32) for i in range(2)]
        for t in range(self.cTT):
            j = t % 2
            rows = slice(t * 128, (t + 1) * 128)
            self.LD(h[j].ap, self.cH[rows, :], [self.cH], [h[j]])
            self.norm_tile(h[j], nw, o[j], scr, st[j])
            self.LD(self.yout[rows, :], o[j].ap, [o[j]], [self.yout])

    def ph_rwkv_norm(self):
        S = self.S
        nw = self.bcast_row("nw", self.di["norm_w"].ap[0, 1], D)
        h = [S.sb(f"h{i}", [128, D], F32) for i in range(2)]
        o = [S.sb(f"o{i}", [128, D], F32) for i in range(2)]
        scr = S.sb("scr", [128, D], F32)
        st = [S.sb(f"st{i}", [128, 4], F32) for i in range(2)]
        for t in range(self.TT):
            j = t % 2
            rows = slice(t * 128, (t + 1) * 128)
            self.LD(h[j].ap, self.Hd[rows, :], [self.Hd], [h[j]])
            self.norm_tile(h[j], nw, o[j], scr, st[j])
            self.LD(self.HN[1 + t * 128:1 + (t + 1) * 128, :], o[j].ap, [o[j]], [self.HN])
            self.LD(self.shiftall[rows, :], o[j].ap, [o[j]], [self.shiftall])

    def ph_rwkv_proj(self):
        S, di = self.S, self.di
        mixn = S.sb("mixn", [48, 128], F32)
        self.LD(mixn.ap, di["rwkv_mix"].ap.rearrange("a (k p) -> (a k) p", p=128), [self.din_res], [mixn])
        mixT = S.sb("mixT", [128, 48], F32)
        w0b = self.bcast_row("w0b", di["rwkv_w0"].ap, D)
        a0b = self.bcast_row("a0b", di["rwkv_a0"].ap, D)
        kkb = self.bcast_row("kkb", di["rwkv_kk"].ap, D)
        kab = self.bcast_row("kab", di["rwkv_ka"].ap, D)
        rkb = self.bcast_row("rkb", di["rwkv_rk"].ap, D)
        wr = [self.load_w(f"wrkv{i}", di["rwkv_wrkv"].ap[i], D, D) for i in range(3)]
        w1 = self.load_w("w1", di["rwkv_w1"].ap, D, 64)
        a1 = self.load_w("a1", di["rwkv_a1"].ap, D, 64)
        g1 = self.load_w("g1", di["rwkv_g1"].ap, D, 160)
        w2 = S.sb("w2", [64, D], BF16); S.dma("pool", w2.ap, di["rwkv_w2"].ap, [self.din_res], [w2])
        a2 = S.sb("a2", [64, D], BF16); S.dma("pool", a2.ap, di["rwkv_a2"].ap, [self.din_res], [a2])
        g2a = S.sb("g2a", [128, D], BF16); S.dma("pool", g2a.ap, di["rwkv_g2"].ap[0:128, :], [self.din_res], [g2a])
        g2b = S.sb("g2b", [32, D], BF16); S.dma("pool", g2b.ap, di["rwkv_g2"].ap[128:160, :], [self.din_res], [g2b])
        psm = S.ps("psm", [128, 512])
        self.S.op("pe", lambda e: e.transpose(psm[:, 0:48], mixn.ap, self.identf[0:48, 0:48]), [mixn, self.identf], [psm])
        self.CP("dve", mixT.ap, psm[:, 0:48], [psm], [mixT])
        def scaled(name, w, j, cols):
            t = S.sb(name, [128, 8, cols], BF16)
            for k in range(8):
                self.TS("dve" if k % 2 else "pool", t[:, k, :], w[:, k, :], mixT[:, j * 8 + k:j * 8 + k + 1], None, ALU.mult, None, [w, mixT], [t])
            return t
        wrs = [scaled("wrs0", wr[0], 0, D), scaled("wrs1", wr[1], 2, D), scaled("wrs2", wr[2], 3, D)]
        w1s = scaled("w1s", w1, 1, 64)
        a1s = scaled("a1s", a1, 4, 64)
        g1s = scaled("g1s", g1, 5, 160)
        hn = [S.sb(f"hn{i}", [128, D], F32) for i in range(2)]
        hp = [S.sb(f"hp{i}", [128, D], F32) for i in range(2)]
        xx = S.sb("xx", [128, D], F32)
        tmp = S.sb("tmp", [128, D], F32)
        xm = [S.sb(f"xm{i}", [128, D], BF16) for i in range(2)]
        xTh = [S.sb(f"xTh{i}", [128, 8, 128], BF16) for i in range(2)]
        xTx = [S.sb(f"xTx{i}", [128, 8, 128], BF16) for i in range(2)]
        lo = S.sb("lo", [128, 128], BF16)
        lo2 = S.sb("lo2", [32, 128], BF16)
        o_r = S.sb("o_r", [128, D], F32); o_k = S.sb("o_k", [128, D], F32); o_v = S.sb("o_v", [128, D], F32)
        o_a = S.sb("o_a", [128, D], F32); o_kk = S.sb("o_kk", [128, D], F32); o_b = S.sb("o_b", [128, D], F32)
        o_w = S.sb("o_w", [128, D], F32); o_g = S.sb("o_g", [128, D], F32)
        sm = S.sb("sm", [128, 3, H], F32)
        psT = [S.ps(f"psT{i}", [128, D], BF16) for i in range(2)]
        pp = [S.ps(f"pp{i}", [128, D]) for i in range(2)]
        pl = S.ps("pl", [128, 512])
        v3 = lambda t_: t_.ap.rearrange("p (h n) -> p h n", h=H)
        for t in range(self.TT):
            j = t % 2
            rows = slice(t * 128, (t + 1) * 128)
            self.LD(hn[j].ap, self.HN[1 + t * 128:1 + (t + 1) * 128, :], [self.HN], [hn[j]])
            self.LD(hp[j].ap, self.HN[t * 128:(t + 1) * 128, :], [self.HN], [hp[j]])
            if t >= self.NPT:
                self.LD(hp[j][0:1, :], di["shift0"].ap[t - self.NPT:t - self.NPT + 1, :], [self.din_res], [hp[j]])
            self.CP("pool", xm[0].ap, hn[j].ap, [hn[j]], [xm[0]])
            self.TT_("dve", xm[1].ap, hp[j].ap, hn[j].ap, ALU.subtract, [hp[j], hn[j]], [xm[1]])
            xh, xd = xTh[j], xTx[j]
            self.transpose_cols(xm[0], D, xh.ap, xh, psT[0], eng="act")
            self.transpose_cols(xm[1], D, xd.ap, xd, psT[1], eng="act")
            def proj(w, ws, pt):
                for hh in range(2):
                    cs = slice(hh * 512, (hh + 1) * 512)
                    for k in range(8):
                        self.MM(pt[:, cs], xh[:, k, :], w[:, k, cs], k == 0, False, [xh, w], [pt])
                    for k in range(8):
                        self.MM(pt[:, cs], xd[:, k, :], ws[:, k, cs], False, k == 7, [xd, ws], [pt])
            def lora1(out, w, ws, c0, c1):
                for k in range(8):
                    self.MM(out, w[:, k, c0:c1], xh[:, k, :], k == 0, False, [w, xh], [pl])
                for k in range(8):
                    self.MM(out, ws[:, k, c0:c1], xd[:, k, :], False, k == 7, [ws, xd], [pl])
            proj(wr[0], wrs[0], pp[0]); self.CP("act", o_r.ap, pp[0].ap, [pp[0]], [o_r])
            proj(wr[1], wrs[1], pp[1]); self.CP("act", o_k.ap, pp[1].ap, [pp[1]], [o_k])
            proj(wr[2], wrs[2], pp[0]); self.CP("act", o_v.ap, pp[0].ap, [pp[0]], [o_v])
            lora1(pl[0:64, 0:128], w1, w1s, 0, 64)
            self.ACT(lo[0:64, :], pl[0:64, 0:128], AF.Tanh, [pl], [lo])
            for hh in range(2):
                cs = slice(hh * 512, (hh + 1) * 512)
                self.MM(pp[1][:, cs], lo[0:64, :], w2[:, cs], True, True, [lo, w2], [pp[1]])
            self.TT_("dve", o_w.ap, pp[1].ap, w0b.ap, ALU.add, [pp[1], w0b], [o_w])
            self.ACT(o_w.ap, o_w.ap, AF.Sigmoid, [o_w], [o_w])
            self.TS("dve", o_w.ap, o_w.ap, -float(np.exp(-0.5)), None, ALU.mult, None, [o_w], [o_w])
            if t >= self.NPT:
                self.TS("dve", o_w.ap, o_w.ap, self.rowmask[:, 1:2], None, ALU.mult, None, [o_w, self.rowmask], [o_w])
            lora1(pl[0:64, 0:128], a1, a1s, 0, 64)
            self.CP("act", lo[0:64, :], pl[0:64, 0:128], [pl], [lo])
            for hh in range(2):
                cs = slice(hh * 512, (hh + 1) * 512)
                self.MM(pp[0][:, cs], lo[0:64, :], a2[:, cs], True, True, [lo, a2], [pp[0]])
            self.TT_("dve", o_a.ap, pp[0].ap, a0b.ap, ALU.add, [pp[0], a0b], [o_a])
            self.ACT(o_a.ap, o_a.ap, AF.Sigmoid, [o_a], [o_a])
            lora1(pl[:, 0:128], g1, g1s, 0, 128)
            lora1(pl[0:32, 128:256], g1, g1s, 128, 160)
            self.ACT(lo.ap, pl[:, 0:128], AF.Sigmoid, [pl], [lo])
            self.ACT(lo2.ap, pl[0:32, 128:256], AF.Sigmoid, [pl], [lo2])
            for hh in range(2):
                cs = slice(hh * 512, (hh + 1) * 512)
                self.MM(pp[1][:, cs], lo.ap, g2a[:, cs], True, False, [lo, g2a], [pp[1]])
                self.MM(pp[1][:, cs], lo2.ap, g2b[:, cs], False, True, [lo2, g2b], [pp[1]])
            self.CP("act", o_g.ap, pp[1].ap, [pp[1]], [o_g])
            self.TT_("dve", o_kk.ap, o_k.ap, kkb.ap, ALU.mult, [o_k, kkb], [o_kk])
            self.TT_("pool", tmp.ap, o_kk.ap, o_kk.ap, ALU.mult, [o_kk], [tmp])
            self.RED(sm[:, 0, :], v3(tmp), ALU.add, [tmp], [sm])
            self.ACT(sm[:, 0, :], sm[:, 0, :], AF.Sqrt, [sm], [sm])
            self.TS("dve", sm[:, 0, :], sm[:, 0, :], 1e-12, None, ALU.max, None, [sm], [sm])
            self.RCP(sm[:, 1, :], sm[:, 0, :], [sm], [sm])
            self.TT_("dve", v3(o_kk), v3(o_kk), sm[:, 1, :].unsqueeze(2).broadcast_to([128, H, 64]), ALU.mult, [o_kk, sm], [o_kk])
            self.TT_("pool", o_b.ap, o_kk.ap, o_a.ap, ALU.mult, [o_kk, o_a], [o_b])
            self.TS("dve", tmp.ap, o_a.ap, -1.0, None, ALU.add, None, [o_a], [tmp])
            self.TT_("dve", tmp.ap, tmp.ap, kab.ap, ALU.mult, [tmp, kab], [tmp])
            self.STT(o_k.ap, tmp.ap, 1.0, o_k.ap, ALU.add, ALU.mult, [tmp, o_k], [o_k])
            self.TT_("pool", tmp.ap, o_r.ap, o_k.ap, ALU.mult, [o_r, o_k], [tmp])
            self.TT_("dve", tmp.ap, tmp.ap, rkb.ap, ALU.mult, [tmp, rkb], [tmp])
            self.RED(sm[:, 2, :], v3(tmp), ALU.add, [tmp], [sm])
            if t >= self.NPT:
                self.TS("dve", o_b.ap, o_b.ap, self.rowmask[:, 1:2], None, ALU.mult, None, [o_b, self.rowmask], [o_b])
                self.TS("dve", o_k.ap, o_k.ap, self.rowmask[:, 1:2], None, ALU.mult, None, [o_k, self.rowmask], [o_k])
            for src, dst in ((o_r, self.Rd), (o_k, self.Kd), (o_v, self.Vd), (o_kk, self.KKd), (o_b, self.Bd), (o_w, self.LWd), (o_g, self.Gd)):
                self.LD(dst[rows, :], src.ap, [src], [dst], q="pool")
            self.LD(self.BONd[rows, :], sm[:, 2, :], [sm], [self.BONd], q="pool")

    def ph_rwkv_scan(self):
        S, di = self.S, self.di
        import os
        LV = float(os.environ.get("K_SCAN", "9"))
        tri = S.sb("tri", [128, 256], F32); self.LD(tri.ap, di["c_tri"].ap, [self.din_res], [tri])
        m1 = S.sb("m1", [128, 384], BF16); S.dma("pool", m1.ap, di["c_m1"].ap, [self.din_res], [m1])
        m2 = S.sb("m2", [128, 256], BF16); S.dma("pool", m2.ap, di["c_m2"].ap, [self.din_res], [m2])
        names = ("r", "k", "v", "kk", "b", "lw")
        srcs = (self.Rd, self.Kd, self.Vd, self.KKd, self.Bd, self.LWd)
        inb = {n: S.sb(f"in_{n}", [128, D], F32) for n in names}
        ecum = S.sb("ecum", [128, D], F32); encum = S.sb("encum", [128, D], F32)
        eex = S.sb("eex", [128, D], F32); erc = S.sb("erc", [128, D], F32)
        PB = [dict(tA=S.sb(f"tA{p}", [128, D], BF16), tR=S.sb(f"tR{p}", [128, D], BF16), tB=S.sb(f"tB{p}", [128, D], BF16),
                   tK=S.sb(f"tK{p}", [128, D], BF16), hB=S.sb(f"hB{p}", [128, D], BF16), hK=S.sb(f"hK{p}", [128, D], BF16),
                   vb=S.sb(f"vb{p}", [128, D], BF16), wc=S.sb(f"wc{p}", [64, H], F32)) for p in range(2)]
        yt = S.sb("yt", [128, D], F32)
        Mf = [S.sb(f"Mf{h}", [64, 64], F32) for h in range(H)]
        Mb = [S.sb(f"Mb{h}", [64, 64], BF16) for h in range(H)]
        G = 6
        NB = G
        TTh = [S.sb(f"TTh{i}", [64, 512], BF16) for i in range(NB)]
        SC1 = [S.sb(f"SC1{i}", [128, 384], BF16) for i in range(NB)]
        SC2 = [S.sb(f"SC2{i}", [128, 256], BF16) for i in range(NB)]
        XX = [[S.sb(f"XX{i}_{p}", [128, 256], BF16) for p in range(2)] for i in range(NB)]
        PP = [[S.sb(f"PP{i}_{p}", [128, 128], BF16) for p in range(2)] for i in range(NB)]
        Zb = [S.sb(f"Zb{i}", [128, 64], BF16) for i in range(NB)]
        AhT = [S.sb(f"AhT{i}", [64, 128], BF16) for i in range(NB)]
        Ub = [S.sb(f"Ub{i}", [128, 64], BF16) for i in range(NB)]
        st0 = S.sb("st0", [64, 64], F32)
        pcum = S.ps("pcum", [128, D]); prc = pcum
        bank = [S.ps(f"bk{i}", [128, 512]) for i in range(G)]
        pw = bank
        def prep(t, p):
            tA, tR, tB, tK, hB, hK, vb, wc = (PB[p][k] for k in ('tA', 'tR', 'tB', 'tK', 'hB', 'hK', 'vb', 'wc'))
            if LV < 2:
                return
            rows = slice(t * 128, (t + 1) * 128)
            for n, s_ in zip(names, srcs):
                self.LD(inb[n].ap, s_[rows, :], [s_], [inb[n]])
            lw = inb["lw"]
            if LV < 2.2:
                return
            for hh in range(2):
                cs = slice(hh * 512, (hh + 1) * 512)
                self.MM(pcum[:, cs], tri[:, 0:128], lw[:, cs], True, True, [tri, lw], [pcum])
            if LV < 2.12:
                return
            self.ACT(ecum.ap, pcum.ap, AF.Exp, [pcum], [ecum])
            if LV < 2.13:
                return
            self.ACT(encum.ap, pcum.ap, AF.Exp, [pcum], [encum], scale=-1.0)
            if LV < 2.14:
                return
            self.TT_("dve", eex.ap, pcum.ap, lw.ap, ALU.subtract, [pcum, lw], [eex])
            self.ACT(eex.ap, eex.ap, AF.Exp, [eex], [eex])
            if LV < 2.15:
                return
            for hh in range(2):
                cs = slice(hh * 512, (hh + 1) * 512)
                self.MM(prc[:, cs], tri[:, 128:256], lw[:, cs], True, True, [tri, lw], [prc])
            self.ACT(erc.ap, prc.ap, AF.Exp, [prc], [erc])
            if LV < 2.3:
                return
            self.STT(tA.ap, inb["kk"].ap, -1.0, eex.ap, ALU.mult, ALU.mult, [inb["kk"], eex], [tA])
            self.TT_("pool", tR.ap, inb["r"].ap, ecum.ap, ALU.mult, [inb["r"], ecum], [tR])
            self.TT_("dve", tB.ap, inb["b"].ap, encum.ap, ALU.mult, [inb["b"], encum], [tB])
            self.TT_("pool", tK.ap, inb["k"].ap, encum.ap, ALU.mult, [inb["k"], encum], [tK])
            self.TT_("dve", hB.ap, inb["b"].ap, erc.ap, ALU.mult, [inb["b"], erc], [hB])
            self.TT_("pool", hK.ap, inb["k"].ap, erc.ap, ALU.mult, [inb["k"], erc], [hK])
            self.CP("pool", vb.ap, inb["v"].ap, [inb["v"]], [vb])
            if LV < 2.4:
                return
            pz = pw[0]
            for h in range(H):
                self.MM(pz[0:64, h:h + 1], lw[:, h * 64:(h + 1) * 64], self.ones_f.ap, True, True, [lw, self.ones_f], [pz])
            self.ACT(wc.ap, pz[0:64, 0:H], AF.Exp, [pz], [wc])

        def heads_group(t, p, g0):
            tA, tR, tB, tK, hB, hK, vb, wc = (PB[p][k] for k in ('tA', 'tR', 'tB', 'tK', 'hB', 'hK', 'vb', 'wc'))
            heads = list(range(g0, min(H, g0 + G)))
            for i, h in enumerate(heads):
                hs = slice(h * 64, (h + 1) * 64)
                bk, tt = bank[i], TTh[i]
                pv = bk[0:64, 0:256].bitcast(BF16)
                for q, src in enumerate((tA, tR, tB, tK)):
                    self.TR(pv[:, q * 128:(q + 1) * 128], src[:, hs], self.identb.ap, [src, self.identb], [bk])
                self.CP("act", tt.ap, pv, [bk], [tt])
            for i, h in enumerate(heads):
                bk, tt, s1 = bank[i], TTh[i], SC1[i]
                self.MM(bk[:, 0:128], tt[:, 0:128], tt[:, 256:384], True, True, [tt], [bk])
                self.MM(bk[:, 128:384], tt[:, 256:384], tt[:, 0:256], True, True, [tt], [bk])
                self.TT_("dve", s1.ap, bk[:, 0:384], m1.ap, ALU.mult, [bk, m1], [s1])
            for i, h in enumerate(heads):
                bk, tt, s2 = bank[i], TTh[i], SC2[i]
                self.MM(bk[:, 0:256], tt[:, 384:512], tt[:, 0:256], True, True, [tt], [bk])
                self.TT_("dve", s2.ap, bk[:, 0:256], m2.ap, ALU.mult, [bk, m2], [s2])
            stt = {}
            for i, h in enumerate(heads):
                stt[i] = [SC1[i][:, 0:256], SC1[i], self.identb.ap, self.identb]
            for lv in range(7):
                for i, h in enumerate(heads):
                    bk = bank[i]
                    xcur, xres, pcur, pres = stt[i]
                    X, XT_ = xcur[:, 0:128], xcur[:, 128:256]
                    self.MM(bk[:, 0:128], X, pcur, True, True, [xres, pres], [bk])
                    if lv < 6:
                        self.MM(bk[:, 128:256], XT_, X, True, True, [xres], [bk])
                        self.MM(bk[:, 256:384], X, XT_, True, True, [xres], [bk])
                    pn = PP[i][lv % 2]
                    self.TT_("dve", pn.ap, bk[:, 0:128], pcur, ALU.add, [bk, pres], [pn])
                    stt[i][2], stt[i][3] = pn.ap, pn
                    if lv < 6:
                        xn_ = XX[i][lv % 2]
                        self.CP("act", xn_.ap, bk[:, 128:384], [bk], [xn_])
                        stt[i][0], stt[i][1] = xn_.ap, xn_
            for i, h in enumerate(heads):
                hs = slice(h * 64, (h + 1) * 64)
                bk = bank[i]
                P, pres = stt[i][2], stt[i][3]
                self.MM(bk[:, 0:64], SC2[i][:, 0:128], vb[:, hs], True, True, [SC2[i], vb], [bk])
                self.MM(bk[0:64, 64:192], tA[:, hs], P, True, True, [tA, pres], [bk])
                self.CP("act", Zb[i].ap, bk[:, 0:64], [bk], [Zb[i]])
                self.CP("dve", AhT[i].ap, bk[0:64, 64:192], [bk], [AhT[i]])
            for i, h in enumerate(heads):
                bk = bank[i]
                P, pres = stt[i][2], stt[i][3]
                self.MM(bk[:, 256:320], P, Zb[i].ap, True, False, [pres, Zb[i]], [bk])
                self.MM(bk[:, 256:320], AhT[i].ap, Mb[h].ap, False, True, [AhT[i], Mb[h]], [bk])
                self.CP("dve", Ub[i].ap, bk[:, 256:320], [bk], [Ub[i]])
            for i, h in enumerate(heads):
                hs = slice(h * 64, (h + 1) * 64)
                bk, tt = bank[i], TTh[i]
                self.MM(bk[:, 320:384], SC2[i][:, 128:256], vb[:, hs], True, False, [SC2[i], vb], [bk])
                self.MM(bk[:, 320:384], tt[:, 128:256], Mb[h].ap, False, False, [tt, Mb[h]], [bk])
                self.MM(bk[:, 320:384], SC1[i][:, 256:384], Ub[i].ap, False, True, [SC1[i], Ub[i]], [bk])
                self.CP("act", yt[:, hs], bk[:, 320:384], [bk], [yt])
            for i, h in enumerate(heads):
                hs = slice(h * 64, (h + 1) * 64)
                bk = bank[i]
                self.MM(bk[0:64, 384:448], hK[:, hs], vb[:, hs], True, False, [hK, vb], [bk])
                self.MM(bk[0:64, 384:448], hB[:, hs], Ub[i].ap, False, True, [hB, Ub[i]], [bk])
                self.STT(Mf[h].ap, Mf[h].ap, wc[:, h:h + 1], bk[0:64, 384:448], ALU.mult, ALU.add, [Mf[h], wc, bk], [Mf[h]])
                self.CP("pool", Mb[h].ap, Mf[h].ap, [Mf[h]], [Mb[h]])

        seqs = [(list(range(self.NPT)), None, self.wkvp.ap)]
        for b in range(NSB):
            seqs.append(([self.NPT + b], b, self.wkvs.ap[b]))
        hcnt = 0
        for tiles, b0, dst in seqs:
            for h in range(H):
                if b0 is None:
                    self.MSET("pool", Mf[h].ap, 0.0, [Mf[h]])
                else:
                    self.LD(st0.ap, di["wkv0"].ap[b0, h], [self.din_res], [st0])
                    pz = pw[h % 4]
                    self.S.op("pe", lambda e, o=pz[0:64, 0:64], i_=st0.ap, idn=self.identf[0:64, 0:64]: e.transpose(o, i_, idn), [st0, self.identf], [pz])
                    self.CP("dve", Mf[h].ap, pz[0:64, 0:64], [pz], [Mf[h]])
                self.CP("pool", Mb[h].ap, Mf[h].ap, [Mf[h]], [Mb[h]])
            par = 0
            prep(tiles[0], par)
            for idx, t in enumerate(tiles):
                rows = slice(t * 128, (t + 1) * 128)
                gl = list(range(0, H, G))
                for gi, g0 in enumerate(gl):
                    if gi == len(gl) - 1 and idx + 1 < len(tiles):
                        prep(tiles[idx + 1], 1 - par)
                    heads_group(t, par, g0)
                par = 1 - par
                self.LD(self.Yd[rows, :], yt.ap, [yt], [self.Yd])
            for h in range(H):
                pz = pw[h % 4]
                self.S.op("pe", lambda e, o=pz[0:64, 0:64], i_=Mf[h].ap, idn=self.identf[0:64, 0:64]: e.transpose(o, i_, idn), [Mf[h], self.identf], [pz])
                self.CP("dve", st0.ap, pz[0:64, 0:64], [pz], [st0])
                self.LD(dst[h], st0.ap, [st0], [self.wkvp if b0 is None else self.wkvs])

    def ph_rwkv_post(self):
        S, di = self.S, self.di
        lwb = self.bcast_row("lwb", di["rwkv_lnx_w"].ap, D)
        lbb = self.bcast_row("lbb", di["rwkv_lnx_b"].ap, D)
        wo = self.load_w("wo", di["rwkv_wo"].ap, D, D)
        y = [S.sb(f"y{i}", [128, D], F32) for i in range(2)]
        g = [S.sb(f"g{i}", [128, D], F32) for i in range(2)]
        v = [S.sb(f"v{i}", [128, D], F32) for i in range(2)]
        h = [S.sb(f"h{i}", [128, D], F32) for i in range(2)]
        bon = [S.sb(f"bon{i}", [128, H], F32) for i in range(2)]
        tmpl = [S.sb(f"tmp{i}", [128, D], F32) for i in range(2)]
        sml = [S.sb(f"sm{i}", [128, 4, H], F32) for i in range(2)]
        ob = [S.sb(f"ob{i}", [128, D], BF16) for i in range(2)]
        xT = [S.sb(f"xT{i}", [128, 8, 128], BF16) for i in range(2)]
        psT = [S.ps(f"psT{i}", [128, D], BF16) for i in range(2)]
        pol = [S.ps(f"po{i}", [128, D]) for i in range(2)]
        v3 = lambda a: a.rearrange("p (h n) -> p h n", h=H)
        bc = lambda a: a.unsqueeze(2).broadcast_to([128, H, 64])

        def tile_ops(t):
            j = t % 2
            rows = slice(t * 128, (t + 1) * 128)
            yy, tmp, sm, po = y[j], tmpl[j], sml[j], pol[j]
            ops = []

            def loads():
                self.LD(y[j].ap, self.Yd[rows, :], [self.Yd], [y[j]])
                self.LD(g[j].ap, self.Gd[rows, :], [self.Gd], [g[j]])
                self.LD(v[j].ap, self.Vd[rows, :], [self.Vd], [v[j]])
                self.LD(h[j].ap, self.Hd[rows, :], [self.Hd], [h[j]])
                self.LD(bon[j].ap, self.BONd[rows, :], [self.BONd], [bon[j]])
                if self.dbg:
                    self.LD(self.dbgY[rows, :], y[j].ap, [y[j]], [self.dbgY])
            ops.append(loads)
            ops.append(lambda: self.RED(sm[:, 0, :], v3(yy.ap), ALU.add, [yy], [sm]))
            ops.append(lambda: self.TS("dve", sm[:, 0, :], sm[:, 0, :], 1.0 / 64, None, ALU.mult, None, [sm], [sm]))
            ops.append(lambda: self.TT_("dve", v3(yy.ap), v3(yy.ap), bc(sm[:, 0, :]), ALU.subtract, [yy, sm], [yy]))
            ops.append(lambda: self.TT_("pool", tmp.ap, yy.ap, yy.ap, ALU.mult, [yy], [tmp]))
            ops.append(lambda: self.RED(sm[:, 1, :], v3(tmp.ap), ALU.add, [tmp], [sm]))
            ops.append(lambda: self.TS("dve", sm[:, 1, :], sm[:, 1, :], 1.0 / 64, 64e-5, ALU.mult, ALU.add, [sm], [sm]))
            ops.append(lambda: self.ACT(sm[:, 1, :], sm[:, 1, :], AF.Sqrt, [sm], [sm]))
            ops.append(lambda: self.RCP(sm[:, 2, :], sm[:, 1, :], [sm], [sm]))
            ops.append(lambda: self.TT_("pool", v3(tmp.ap), v3(v[j].ap), bc(bon[j].ap), ALU.mult, [v[j], bon[j]], [tmp]))
            ops.append(lambda: self.TT_("dve", v3(yy.ap), v3(yy.ap), bc(sm[:, 2, :]), ALU.mult, [yy, sm], [yy]))
            ops.append(lambda: self.TT_("pool", yy.ap, yy.ap, lwb.ap, ALU.mult, [yy, lwb], [yy]))
            ops.append(lambda: self.TT_("dve", yy.ap, yy.ap, lbb.ap, ALU.add, [yy, lbb], [yy]))
            ops.append(lambda: self.TT_("pool", yy.ap, yy.ap, tmp.ap, ALU.add, [yy, tmp], [yy]))
            ops.append(lambda: self.TT_("dve", ob[j].ap, yy.ap, g[j].ap, ALU.mult, [yy, g[j]], [ob[j]]))
            ops.append(lambda: self.transpose_cols(ob[j], D, xT[j].ap, xT[j], psT[j], eng="act"))

            def mm():
                for hh in range(2):
                    cs = slice(hh * 512, (hh + 1) * 512)
                    for k in range(8):
                        self.MM(po[:, cs], xT[j][:, k, :], wo[:, k, cs], k == 0, k == 7, [xT[j], wo], [po])
            ops.append(mm)
            ops.append(lambda: self.TT_("dve", h[j].ap, po.ap, h[j].ap, ALU.add, [po, h[j]], [h[j]]))
            ops.append(lambda: self.LD(self.Hd[rows, :], h[j].ap, [h[j]], [self.Hd]))
            return ops

        for t0_ in range(0, self.TT, 2):
            lists = [tile_ops(t) for t in range(t0_, min(self.TT, t0_ + 2))]
            for k in range(len(lists[0])):
                for l in lists:
                    l[k]()

    def ph_kv(self):
        S, di = self.S, self.di
        nw = self.bcast_row("nw", di["kv_norm"].ap, D)
        wkv = self.load_w("wkv", di["w_kv"].ap, D, 512)
        h = [S.sb(f"h{i}", [128, D], F32) for i in range(2)]
        xn = [S.sb(f"xn{i}", [128, D], BF16) for i in range(2)]
        xT = [S.sb(f"xT{i}", [128, 8, 128], BF16) for i in range(2)]
        kv = [S.sb(f"kv{i}", [128, 512], F32) for i in range(2)]
        kvb = [S.sb(f"kvb{i}", [128, 512], BF16) for i in range(2)]
        ktb = [S.sb(f"ktb{i}", [64, 4, 128], BF16) for i in range(2)]
        scr = S.sb("scr", [128, D], F32)
        st = [S.sb(f"st{i}", [128, 4], F32) for i in range(2)]
        cb = [S.sb(f"cb{i}", [128, 256], BF16) for i in range(2)]
        psT = [S.ps(f"psT{i}", [128, D], BF16) for i in range(2)]
        pk = S.ps("pk", [128, 512])
        pkt = [S.ps(f"pkt{i}", [64, 512], BF16) for i in range(2)]
        for t in range(self.TT):
            j = t % 2
            rows = slice(t * 128, (t + 1) * 128)
            self.LD(h[j].ap, self.Hd[rows, :], [self.Hd], [h[j]])
            self.norm_tile(h[j], nw, xn[j], scr, st[j])
            self.transpose_cols(xn[j], D, xT[j].ap, xT[j], psT[j])
            for k in range(8):
                self.MM(pk.ap, xT[j][:, k, :], wkv[:, k, :], k == 0, k == 7, [xT[j], wkv], [pk])
            self.CP("act", kv[j].ap, pk.ap, [pk], [kv[j]])
            self.CP("dve", kvb[j].ap, pk.ap, [pk], [kvb[j]])
            self.LD(self.kvout[rows, :], kv[j].ap, [kv[j]], [self.kvout])
            for hd in range(4):
                self.TR(pkt[j][:, hd * 128:(hd + 1) * 128], kvb[j][:, hd * 64:(hd + 1) * 64], self.identb.ap, [kvb[j], self.identb], [pkt[j]])
            self.CP("act", ktb[j].ap, pkt[j].ap.rearrange("p (h t) -> p h t", h=4), [pkt[j]], [ktb[j]])
            if t < self.NPT:
                self.LD(self.KTp[0:64, :, MAXW + t * 128:MAXW + (t + 1) * 128], ktb[j].ap, [ktb[j]], [self.KTp])
                self.LD(self.VDp[MAXW + t * 128:MAXW + (t + 1) * 128, :], kvb[j][:, 256:512], [kvb[j]], [self.VDp])
            else:
                b = t - self.NPT
                self.LD(self.KTs[b, 0:64, :, MAXW:MAXW + 8], ktb[j][:, :, 0:8], [ktb[j]], [self.KTs])
                self.LD(self.VDs[b, MAXW:MAXW + 8, :], kvb[j][0:8, 256:512], [kvb[j]], [self.VDs])
        cnt = 0
        for b in range(self.NS1):
            S.dma("pool", self.VDsm[b, 0:MAXW, :], di["cache"].ap[b, :, 256:512], [self.din_res], [self.VDsm])
            for r0 in range(0, MAXW, 128):
                j = cnt % 2
                cnt += 1
                S.dma("pool", cb[j].ap, di["cache"].ap[b, r0:r0 + 128, 0:256], [self.din_res], [cb[j]])
                for hd in range(4):
                    self.TR(pkt[j][:, hd * 128:(hd + 1) * 128], cb[j][:, hd * 64:(hd + 1) * 64], self.identb.ap, [cb[j], self.identb], [pkt[j]])
                self.CP("act" if j else "dve", ktb[j].ap, pkt[j].ap.rearrange("p (h t) -> p h t", h=4), [pkt[j]], [ktb[j]])
                self.LD(self.KTsm[b, 0:64, :, r0:r0 + 128], ktb[j].ap, [ktb[j]], [self.KTsm])

    def ph_select(self):
        S, di = self.S, self.di
        NH = self.NH
        sel = S.sb("sel", [128, 2], F32)
        self.LD(sel.ap, di["sel"].ap, [self.din_res], [sel])
        NB_ = 4 if NH % 4 == 0 else 1
        a = [S.sb(f"a{i}", [128, NB_, D], F32) for i in range(2)]
        b = [S.sb(f"b{i}", [128, NB_, D], F32) for i in range(2)]
        for ii, i in enumerate(range(0, NH, NB_)):
            j = ii % 2
            ra = self.Hd[i * 128:(i + NB_) * 128, :].rearrange("(n p) d -> p n d", p=128)
            rb = self.Hd[(NH + i) * 128:(NH + i + NB_) * 128, :].rearrange("(n p) d -> p n d", p=128)
            self.LD(a[j].ap, ra, [self.Hd], [a[j]])
            self.LD(b[j].ap, rb, [self.Hd], [b[j]])
            self.TS("pool", a[j].ap, a[j].ap, sel[:, 0:1], None, ALU.mult, None, [a[j], sel], [a[j]])
            self.STT(a[j].ap, b[j].ap, sel[:, 1:2], a[j].ap, ALU.mult, ALU.add, [b[j], sel, a[j]], [a[j]])
            self.LD(self.Hm[i * 128:(i + NB_) * 128, :].rearrange("(n p) d -> p n d", p=128), a[j].ap, [a[j]], [self.Hm])
        W = MAXW + NH * 128
        off = NH * 128
        ka = [S.sb(f"ka{i}", [65, W], BF16) for i in range(2)]
        kb = [S.sb(f"kb{i}", [65, W], BF16) for i in range(2)]
        for hd in range(4):
            j = hd % 2
            self.LD(ka[j].ap, self.KTp[:, hd, 0:W], [self.KTp], [ka[j]])
            self.LD(kb[j].ap, self.KTp[:, hd, off:off + W], [self.KTp], [kb[j]])
            self.TS("pool", ka[j].ap, ka[j].ap, sel[0:65, 0:1], None, ALU.mult, None, [ka[j], sel], [ka[j]])
            self.STT(ka[j].ap, kb[j].ap, sel[0:65, 1:2], ka[j].ap, ALU.mult, ALU.add, [kb[j], sel, ka[j]], [ka[j]])
            self.LD(self.KTm[:, hd, :], ka[j].ap, [ka[j]], [self.KTm])
        nvt = W // 128
        VB = 8 if nvt % 8 == 0 else 1
        va = [S.sb(f"va{i}", [128, VB, 256], BF16) for i in range(2)]
        vb_ = [S.sb(f"vb{i}", [128, VB, 256], BF16) for i in range(2)]
        for ii, i in enumerate(range(0, nvt, VB)):
            j = ii % 2
            self.LD(va[j].ap, self.VDp[i * 128:(i + VB) * 128, :].rearrange("(n p) d -> p n d", p=128), [self.VDp], [va[j]])
            self.LD(vb_[j].ap, self.VDp[off + i * 128:off + (i + VB) * 128, :].rearrange("(n p) d -> p n d", p=128), [self.VDp], [vb_[j]])
            self.TS("pool", va[j].ap, va[j].ap, sel[:, 0:1], None, ALU.mult, None, [va[j], sel], [va[j]])
            self.STT(va[j].ap, vb_[j].ap, sel[:, 1:2], va[j].ap, ALU.mult, ALU.add, [vb_[j], sel, va[j]], [va[j]])
            self.LD(self.VDm[i * 128:(i + VB) * 128, :].rearrange("(n p) d -> p n d", p=128), va[j].ap, [va[j]], [self.VDm])

    def ph_select_s(self):
        S, di = self.S, self.di
        NH = self.NH
        sel = S.sb("sel", [128, 2], F32)
        self.LD(sel.ap, di["sel"].ap, [self.din_res], [sel])
        a = [S.sb(f"a{i}", [128, D], F32) for i in range(2)]
        b = [S.sb(f"b{i}", [128, D], F32) for i in range(2)]
        NS1 = self.NS1
        for s_ in range(NS1):
            j = s_ % 2
            r0, r1 = (self.NPT + s_) * 128, (self.NPT + NS1 + s_) * 128
            self.LD(a[j].ap, self.Hd[r0:r0 + 128, :], [self.Hd], [a[j]])
            self.LD(b[j].ap, self.Hd[r1:r1 + 128, :], [self.Hd], [b[j]])
            self.TS("pool", a[j].ap, a[j].ap, sel[:, 0:1], None, ALU.mult, None, [a[j], sel], [a[j]])
            self.STT(a[j].ap, b[j].ap, sel[:, 1:2], a[j].ap, ALU.mult, ALU.add, [b[j], sel, a[j]], [a[j]])
            self.LD(self.Hm[(NH + s_) * 128:(NH + s_ + 1) * 128, :], a[j].ap, [a[j]], [self.Hm])
        ksa = [S.sb(f"ksa{i}", [64, 4, 8], BF16) for i in range(2)]
        ksb = [S.sb(f"ksb{i}", [64, 4, 8], BF16) for i in range(2)]
        vsa = [S.sb(f"vsa{i}", [8, 256], BF16) for i in range(2)]
        vsb = [S.sb(f"vsb{i}", [8, 256], BF16) for i in range(2)]
        for s_ in range(NS1):
            j = s_ % 2
            self.LD(ksa[j].ap, self.KTs[s_, 0:64, :, MAXW:MAXW + 8], [self.KTs], [ksa[j]])
            self.LD(ksb[j].ap, self.KTs[NS1 + s_, 0:64, :, MAXW:MAXW + 8], [self.KTs], [ksb[j]])
            self.TS("pool", ksa[j].ap, ksa[j].ap, sel[0:64, 0:1], None, ALU.mult, None, [ksa[j], sel], [ksa[j]])
            self.STT(ksa[j].ap, ksb[j].ap, sel[0:64, 1:2], ksa[j].ap, ALU.mult, ALU.add, [ksb[j], sel, ksa[j]], [ksa[j]])
            self.LD(self.KTsm[s_, 0:64, :, MAXW:MAXW + 8], ksa[j].ap, [ksa[j]], [self.KTsm])
            self.LD(vsa[j].ap, self.VDs[s_, MAXW:MAXW + 8, :], [self.VDs], [vsa[j]])
            self.LD(vsb[j].ap, self.VDs[NS1 + s_, MAXW:MAXW + 8, :], [self.VDs], [vsb[j]])
            self.TS("pool", vsa[j].ap, vsa[j].ap, sel[0:8, 0:1], None, ALU.mult, None, [vsa[j], sel], [vsa[j]])
            self.STT(vsa[j].ap, vsb[j].ap, sel[0:8, 1:2], vsa[j].ap, ALU.mult, ALU.add, [vsb[j], sel, vsa[j]], [vsa[j]])
            self.LD(self.VDsm[s_, MAXW:MAXW + 8, :], vsa[j].ap, [vsa[j]], [self.VDsm])

    def attn_cfgs(self):
        B = self.BLK
        c = {"p0": (1, 128, 1, 0), "p1": (4, 32, 4, 1), "p2": (16, B // 16, 16, 2),
             "s0": (1, 8, 1, 0), "s1": (4, 2, 4, 1), "s2": (16, 1, 8, 2)}
        return c

    def attn_group(self, name, QT, qcol0, KT, kpos0, Vsrc, vrow0, ACC, acol0, first, bufs):
        d, nq, nres, g = self.cfgs[name]
        bias = self.biasT[name]
        nk = nq + 128
        ktl = [(0, 128), (128, nk)]
        vt, pt_, tmpf, pS, pO = bufs
        vres = self.vres
        merged = 8 * nq <= 512
        W4 = 4 * nq
        units = []
        for rho in range(nres):
            shared = {}
            for kvh in range(4):
                st = {}

                def A(rho=rho, kvh=kvh, st=st, shared=shared):
                    if kvh == 0:
                        vts = []
                        for ki, (j0, j1) in enumerate(ktl):
                            vtile = vt[self.vcnt % len(vt)]
                            self.vcnt += 1
                            r0 = vrow0 + rho + d * (j0 - 128)
                            src = Vsrc[r0:r0 + d * (j1 - j0 - 1) + 1:d, :] if d > 1 else Vsrc[r0:r0 + (j1 - j0), :]
                            self.LD(vtile[0:j1 - j0, :], src, [vres], [vtile])
                            vts.append(vtile)
                        shared["vts"] = vts
                    q0 = qcol0 + rho
                    qs = QT[:, g, kvh * 4:(kvh + 1) * 4, q0:q0 + d * (nq - 1) + 1:d] if d > 1 else QT[:, g, kvh * 4:(kvh + 1) * 4, q0:q0 + nq]
                    pts = []
                    if merged:
                        c = self.acnt % len(pS)
                        self.acnt += 1
                        ps_, tf = pS[c], tmpf[c % len(tmpf)]
                        p_ = pt_[self.pcnt % len(pt_)]
                        self.pcnt += 1
                    for ki, (j0, j1) in enumerate(ktl):
                        nkk = j1 - j0
                        if not merged:
                            c = self.acnt % len(pS)
                            self.acnt += 1
                            ps_, tf = pS[c], tmpf[c % len(tmpf)]
                            p_ = pt_[self.pcnt % len(pt_)]
                            self.pcnt += 1
                        co = ki * W4 if merged else 0
                        k0 = kpos0 + rho + d * (j0 - 128)
                        kslice = KT[:, kvh, k0:k0 + d * (nkk - 1) + 1:d] if d > 1 else KT[:, kvh, k0:k0 + nkk]
                        out = ps_[0:nkk, co:co + W4].rearrange("p (h q) -> p h q", h=4)
                        self.MM(out, kslice, qs, True, True, [KT, QT], [ps_])
                        if not merged:
                            self.TT_("dve", tf[0:nkk, 0:W4], ps_[0:nkk, 0:W4], bias[0:nkk, kvh, ki, :], ALU.add, [ps_, bias], [tf])
                            self.ACT(p_[0:nkk, 0:W4], tf[0:nkk, 0:W4], AF.Exp, [tf], [p_])
                        pts.append((p_, nkk, co))
                    if merged:
                        self.TT_("dve", tf[:, 0:2 * W4], ps_[:, 0:2 * W4], bias[:, kvh, :, :].rearrange("p a c -> p (a c)"), ALU.add, [ps_, bias], [tf])
                        self.ACT(p_[:, 0:2 * W4], tf[:, 0:2 * W4], AF.Exp, [tf], [p_])
                    st["pts"] = pts

                def B(rho=rho, kvh=kvh, st=st, shared=shared):
                    pts, vts = st["pts"], shared["vts"]
                    if merged:
                        hf = self.ocnt % 2
                        self.ocnt += 1
                        po_ = pO[0][0:64, hf * 512:(hf + 1) * 512]
                        pres = [pO[0].part(hf)]
                    else:
                        po_ = pO[0][0:64, :]
                        pres = [pO[0].part(0), pO[0].part(1)]
                    for ki, (p_, nkk, co) in enumerate(pts):
                        self.MM(po_[:, 0:W4], vts[ki][0:nkk, kvh * 64:(kvh + 1) * 64], p_[0:nkk, co:co + W4], ki == 0, ki == 1, [vts[ki], p_], pres)
                    for ki, (p_, nkk, co) in enumerate(pts):
                        self.MM(po_[:, W4:2 * W4], self.ones_b[0:nkk, :], p_[0:nkk, co:co + W4], ki == 0, ki == 1, [self.ones_b, p_], pres)
                    a0 = acol0 + rho
                    dst = ACC[:, :, kvh * 4:(kvh + 1) * 4, a0:a0 + d * (nq - 1) + 1:d] if d > 1 else ACC[:, :, kvh * 4:(kvh + 1) * 4, a0:a0 + nq]
                    srcp = po_[:, 0:2 * W4].rearrange("p (a h q) -> p a h q", a=2, h=4)
                    if first:
                        self.CP("dve", dst, srcp, pres, [ACC])
                    else:
                        self.TT_("dve", dst, srcp, dst, ALU.add, pres + [ACC], [ACC])
                units.append((A, B))
        return units

    def run_units(self, units, L=3):
        n = len(units)
        for u in range(min(L, n)):
            units[u][0]()
        for u in range(n):
            if u + L < n:
                units[u + L][0]()
            units[u][1]()

    def ph_attn(self):
        S, di = self.S, self.di
        BLK = self.BLK
        nw = self.bcast_row("nw", di["norm_w"].ap[1, 1], D)
        wo = S.sb("wo", [64, H, D], BF16)
        S.dma("pool", wo.ap, di["attn_wo"].ap.rearrange("(h p) c -> p h c", p=64), [self.din_res], [wo])
        self.biasT = {}
        for name, (d, nq, nres, g) in self.cfgs.items():
            bt = S.sb(f"bias_{name}", [128, 4, 2, 4 * nq], F32)
            self.MSET("pool", bt.ap, 0.0, [bt])
            src = di[f"bias_{name}"].ap
            self.LD(bt[:, :, 0, :], src[:, 0:128, :].rearrange("k j c -> j k c"), [self.din_res], [bt])
            self.LD(bt[0:nq, :, 1, :], src[:, 128:128 + nq, :].rearrange("k j c -> j k c"), [self.din_res], [bt])
            self.biasT[name] = bt
        KT = S.sb("KT", [65, 4, MAXW + BLK], BF16)
        QT = S.sb("QT", [65, 3, H, BLK], BF16)
        self.MSET("pool", QT[64:65, :, :, :], 1.0, [QT])
        ACC = S.sb("ACC", [64, 2, H, BLK], F32)
        fin = S.sb("fin", [64, H, BLK], BF16)
        h = [S.sb(f"h{i}", [128, D], F32) for i in range(2)]
        xn = [S.sb(f"xn{i}", [128, D], BF16) for i in range(2)]
        xT = S.sb("xT", [128, 8, BLK], BF16)
        scr = S.sb("scr", [128, D], F32)
        st = [S.sb(f"st{i}", [128, 4], F32) for i in range(2)]
        vt = [S.sb(f"vt{i}", [128, 256], BF16) for i in range(8)]
        pt_ = [S.sb(f"pt{i}", [128, 512], BF16) for i in range(8)]
        tmpf = [S.sb(f"tf{i}", [128, 512], F32) for i in range(4)]
        psT1 = S.ps("psT", [128, D], BF16)
        psT = [psT1, psT1]
        pq1 = S.ps("pq", [64, 512])
        pq = [pq1, pq1]
        pS = [S.ps(f"pS{i}", [128, 512]) for i in range(4)]
        pO = [S.ps("pO0", [64, 1024])]
        bufs = (vt, pt_, tmpf, pS, pO)
        self.vcnt = self.acnt = self.pcnt = self.ocnt = 0
        wq = di["attn_wq"].ap
        wqb = [S.sb(f"wqb{i}", [128, 8, 512], BF16) for i in range(2)]

        def qproj(tiles, ncols):
            for i, t in enumerate(tiles):
                j = i % 2
                self.LD(h[j].ap, self.cH[t * 128:(t + 1) * 128, :], [self.cH], [h[j]])
                self.norm_tile(h[j], nw, xn[j], scr, st[j])
                self.transpose_cols(xn[j], D, xT[:, :, i * 128:(i + 1) * 128], xT, psT[j])
            qc = 0
            for cb_ in range(6):
                w = wqb[cb_ % 2]
                S.dma("pool", w.ap, wq[:, cb_ * 512:(cb_ + 1) * 512].rearrange("(k p) c -> p k c", p=128), [self.din_res], [w])
                for hh in range(8):
                    gh = cb_ * 8 + hh
                    g, hd = gh // 16, gh % 16
                    pz = pq[qc % 2]
                    qc += 1
                    for k in range(8):
                        self.MM(pz[:, 0:ncols], w[:, k, hh * 64:(hh + 1) * 64], xT[:, k, 0:ncols], k == 0, k == 7, [w, xT], [pz])
                    self.ACT(QT[0:64, g, hd, 0:ncols], pz[:, 0:ncols], AF.Copy, [pz], [QT], scale=0.125)

        def finish(tiles, ncols_valid):
            self.RCP(ACC[:, 1, :, 0:ncols_valid], ACC[:, 1, :, 0:ncols_valid], [ACC], [ACC])
            self.TT_("dve", fin[:, :, 0:ncols_valid], ACC[:, 0, :, 0:ncols_valid], ACC[:, 1, :, 0:ncols_valid], ALU.mult, [ACC], [fin])
            for i, t in enumerate(tiles):
                j = i % 2
                nv = min(128, ncols_valid - i * 128)
                self.LD(h[j].ap, self.cH[t * 128:(t + 1) * 128, :], [self.cH], [h[j]])
                for hh in range(2):
                    cs = slice(hh * 512, (hh + 1) * 512)
                    for hd in range(H):
                        self.MM(pS[hh][0:nv, :], fin[:, hd, i * 128:i * 128 + nv], wo[:, hd, cs], hd == 0, hd == H - 1, [fin, wo], [pS[hh]])
                    self.TT_("dve", h[j][0:nv, cs], pS[hh][0:nv, :], h[j][0:nv, cs], ALU.add, [pS[hh], h[j]], [h[j]])
                self.LD(self.cH[t * 128:(t + 1) * 128, :], h[j].ap, [h[j]], [self.cH])

        self.vres = self.cVD
        nblk = (self.cNPT * 128) // BLK
        tpb = BLK // 128
        for bi in range(nblk):
            tiles = list(range(bi * tpb, (bi + 1) * tpb))
            qproj(tiles, BLK)
            base = bi * BLK
            for hd in range(4):
                self.LD(KT[:, hd, :], self.cKT[:, hd, base:base + MAXW + BLK], [self.cKT], [KT])
            units = []
            for si in range(tpb):
                units += self.attn_group("p0", QT, si * 128, KT, MAXW + si * 128, self.cVD.ap, MAXW + base + si * 128, ACC, si * 128, True, bufs)
            for si in range(tpb):
                units += self.attn_group("p1", QT, si * 128, KT, MAXW + si * 128, self.cVD.ap, MAXW + base + si * 128, ACC, si * 128, False, bufs)
            units += self.attn_group("p2", QT, 0, KT, MAXW, self.cVD.ap, MAXW + base, ACC, 0, False, bufs)
            self.run_units(units)
            finish(tiles, BLK)
        for b in range(self.NS1):
            t = self.cNPT + b
            kts = KT
            for hd in range(4):
                self.LD(kts[:, hd, 0:MAXW + 8], self.KTsm[b, :, hd, :], [self.KTsm], [kts])
            self.vres = self.VDsm
            qproj([t], 128)
            units = []
            for gi, nm in enumerate(("s0", "s1", "s2")):
                units += self.attn_group(nm, QT, 0, kts, MAXW, self.VDsm.ap[b], MAXW, ACC, 0, gi == 0, bufs)
            self.run_units(units)
            finish([t], 8)


def t5_buckets(dist):
    d = np.asarray(dist, dtype=np.int64)
    max_exact = 16
    large = max_exact + (np.log(np.maximum(d, 1) / max_exact) / np.log(2048 / max_exact) * (32 - max_exact)).astype(np.int32)
    large = np.minimum(large, 31)
    return np.where(d < max_exact, d, large).astype(np.int32)


def make_bias_tables(rel_bias, cfgs):
    out = {}
    for name, (d, nq, nres, g) in cfgs.items():
        nk = nq + 128
        j = np.arange(nk)[:, None]
        i = np.arange(nq)[None, :]
        m = i - j + 128
        valid = (m >= 0) & (m <= 128)
        bk = t5_buckets(d * np.clip(m, 0, 128))
        tab = np.empty((4, nk, 4, nq), np.float32)
        for kvh in range(4):
            for hq in range(4):
                col = g * 16 + kvh * 4 + hq
                tab[kvh, :, hq, :] = np.where(valid, rel_bias[bk, col], np.float32(NEG))
        out[name] = np.ascontiguousarray(tab.reshape(4, nk, 4 * nq))
    return out


def make_consts():
    s = np.arange(128)[:, None]
    t = np.arange(128)[None, :]
    c = {}
    c["c_ident"] = np.eye(128, dtype=np.float32)
    c["c_tri"] = np.concatenate([(s <= t), (s > t)], 1).astype(np.float32)
    c["c_m1"] = np.concatenate([(t < s), (s < t), (s <= t)], 1).astype(np.float32)
    c["c_m2"] = np.concatenate([(s < t), (s <= t)], 1).astype(np.float32)
    rm = np.zeros((128, 2), np.float32)
    rm[:, 0] = 1.0
    rm[:8, 1] = 1.0
    c["c_rowmask"] = rm
    return c


_CACHE = {}


def get_builder(NPT, dbg=False):
    key = (NPT, dbg)
    if key not in _CACHE:
        b = Builder(NPT, dbg)
        b.build()
        _CACHE[key] = b
    return _CACHE[key]


def core_inputs(b, c, half, x_prompt_seq, x_sample, state_wkv, state_shift, cache_kv, p_prompt_seq, p_sample, weights, consts, bias_tabs):
    NTOK, NP = b.NTOK, b.NP
    xin = np.zeros((NTOK, D), np.float32)
    xin[:NP] = x_prompt_seq
    pin = np.zeros((2, NTOK, 256), np.float32)
    pin[:, :NP] = p_prompt_seq
    for s in range(NSB):
        r0 = NP + s * 128
        xin[r0:r0 + 8] = x_sample[s]
        pin[:, r0:r0 + 8] = p_sample[:, s]
    NHT = b.NH * 128
    pin1 = np.zeros((b.MTOK, 256), np.float32)
    pin1[:NHT] = p_prompt_seq[1, half * NHT:(half + 1) * NHT]
    for s in range(b.NS1):
        pin1[NHT + s * 128:NHT + s * 128 + 8] = p_sample[1, half * b.NS1 + s]
    sel = np.zeros((128, 2), np.float32)
    sel[:, half] = 1.0
    m = {"sel": sel, "pin1": pin1, "xin": xin, "pin": pin, "wkv0": np.ascontiguousarray(state_wkv), "shift0": np.ascontiguousarray(state_shift),
         "cache": np.ascontiguousarray(cache_kv.reshape(NSB, MAXW, 512)[half * b.NS1:(half + 1) * b.NS1])}
    m.update(weights)
    m.update(consts)
    for name, tab in bias_tabs.items():
        m[f"bias_{name}"] = tab
    return m


def kernel(x_prompt, x_sample, state_wkv, state_shift, cache_kv, p_prompt, p_sample,
           norm_w, ffn1_wi, ffn1_wo, ffn2_wi, ffn2_wo, pe_proj, pe_gate,
           rwkv_mix, rwkv_wrkv, rwkv_wo, rwkv_w0, rwkv_w1, rwkv_w2, rwkv_a0, rwkv_a1, rwkv_a2,
           rwkv_g1, rwkv_g2, rwkv_kk, rwkv_ka, rwkv_rk, rwkv_lnx_w, rwkv_lnx_b,
           attn_wq, attn_wo, kv_norm, w_kv, rel_bias, final_norm, _dbg=False):
    f = lambda a: np.ascontiguousarray(np.asarray(a, dtype=np.float32))
    x_prompt, x_sample, p_prompt, p_sample = f(x_prompt), f(x_sample), f(p_prompt), f(p_sample)
    state_wkv, state_shift, cache_kv = f(state_wkv), f(state_shift), f(cache_kv)
    B, T, _ = x_prompt.shape
    NPT = T // 128
    b = get_builder(NPT, _dbg)
    weights = {"norm_w": f(norm_w), "ffn1_wi": f(ffn1_wi), "ffn1_wo": f(ffn1_wo), "ffn2_wi": f(ffn2_wi), "ffn2_wo": f(ffn2_wo),
               "pe_proj": f(pe_proj), "pe_gate": f(pe_gate), "rwkv_mix": f(rwkv_mix)[0], "rwkv_wrkv": f(rwkv_wrkv)[0],
               "rwkv_wo": f(rwkv_wo)[0], "rwkv_w0": f(rwkv_w0)[0], "rwkv_w1": f(rwkv_w1)[0], "rwkv_w2": f(rwkv_w2)[0],
               "rwkv_a0": f(rwkv_a0)[0], "rwkv_a1": f(rwkv_a1)[0], "rwkv_a2": f(rwkv_a2)[0], "rwkv_g1": f(rwkv_g1)[0],
               "rwkv_g2": f(rwkv_g2)[0], "rwkv_kk": f(rwkv_kk)[0], "rwkv_ka": f(rwkv_ka)[0], "rwkv_rk": f(rwkv_rk)[0].reshape(-1),
               "rwkv_lnx_w": f(rwkv_lnx_w)[0], "rwkv_lnx_b": f(rwkv_lnx_b)[0], "attn_wq": f(attn_wq)[0], "attn_wo": f(attn_wo)[0],
               "kv_norm": f(kv_norm), "w_kv": f(w_kv), "final_norm": f(final_norm)}
    consts = make_consts()
    bias_tabs = make_bias_tables(f(rel_bias), b.cfgs)
    nsb_total = x_sample.shape[0]
    ngrp = nsb_total // NSB
    in_maps = []
    for c in range(8):
        pb = c % B
        sg = c % ngrp
        sl = slice(sg * NSB, (sg + 1) * NSB)
        in_maps.append(core_inputs(b, c, c // B, x_prompt[pb], x_sample[sl], state_wkv[0, sl], state_shift[0, sl], cache_kv[sl],
                                   p_prompt[:, pb], p_sample[:, sl], weights, consts, bias_tabs))
    res = run_bass_kernel_spmd(b.nc, in_maps, core_ids=list(range(8)))
    R = res.results
    NP = b.NP
    NHT = b.NH * 128
    y_prompt = np.stack([np.concatenate([R[c]["yout"][:NHT], R[c + B]["yout"][:NHT]], 0) for c in range(B)])
    kvw = min(MAXW, T)
    kv_prompt = np.stack([R[c]["kvout"][NP - kvw:NP].reshape(kvw, 2, 4, 64) for c in range(B)])
    wkv_prompt = np.stack([R[c]["wkvp"] for c in range(B)])[None]
    shift_prompt = np.stack([R[c]["shiftall"][NP - 1] for c in range(B)])[None]
    ys, kvs, wks, shs = [], [], [], []
    for sg in range(ngrp):
        r = R[sg]
        for s in range(NSB):
            r0 = NP + s * 128
            rr = R[sg + (s // b.NS1) * B]
            ys.append(rr["yout"][NHT + (s % b.NS1) * 128:NHT + (s % b.NS1) * 128 + 8])
            kvs.append(r["kvout"][r0:r0 + 8].reshape(8, 2, 4, 64))
            shs.append(r["shiftall"][r0 + 7])
        wks.append(r["wkvs"])
    y_sample = np.stack(ys)
    kv_sample = np.stack(kvs)
    wkv_sample = np.concatenate(wks, 0)[None]
    shift_sample = np.stack(shs)[None]
    outs = (y_prompt, y_sample, wkv_prompt, shift_prompt, kv_prompt, wkv_sample, shift_sample, kv_sample)
    outs = tuple(np.ascontiguousarray(o, dtype=np.float32) for o in outs)
    if _dbg:
        return outs, R
    return outs
```

```python
from contextlib import ExitStack
import numpy as np
import ml_dtypes
import concourse.bass as bass
import concourse.mybir as mybir
from concourse.bass_utils import run_bass_kernel_spmd

F32 = mybir.dt.float32
BF16 = mybir.dt.bfloat16
ALU = mybir.AluOpType
AF = mybir.ActivationFunctionType
AX = mybir.AxisListType

D = 1024
DFF = 2816
NFC = DFF // 128
H = 16
NSB = 8
MAXW = 2048
NEG = -1e30
GROUPS = ((128, 1), (512, 4), (2048, 16))


class Res:
    __slots__ = ("name", "w", "r", "psum")

    def __init__(self, name, psum=False):
        self.name = name
        self.w = {}
        self.r = {}
        self.psum = psum


class T:
    def __init__(self, name, ap):
        self.name = name
        self.ap = ap
        self.res = Res(name)
        self._parts = {}

    def part(self, key):
        if key not in self._parts:
            self._parts[key] = Res(f"{self.name}.{key}", self.res.psum)
        return self._parts[key]

    def __getitem__(self, k):
        return self.ap[k]


def _res(x):
    return x.res if isinstance(x, T) else x


class Sched:
    COMPUTE = ("pe", "dve", "act", "pool")
    NDMASEM = 8

    def __init__(self, nc):
        self.nc = nc
        self.gstack = ExitStack()
        self.stack = self.gstack
        self.streams = {k: [] for k in ("pe", "dve", "act", "pool", "sp")}
        self.sems = {}
        self.cnt = {}
        for k in self.COMPUTE:
            self.sems[k] = self.gstack.enter_context(nc.semaphore(f"s_{k}"))
            self.cnt[k] = 0
        self.dq = {}
        for q in ("sp", "pool"):
            sl = []
            for i in range(self.NDMASEM):
                key = f"d_{q}{i}"
                self.sems[key] = self.gstack.enter_context(nc.semaphore(key))
                self.cnt[key] = 0
                sl.append(key)
            self.dq[q] = [sl, 0]
        self.known = {k: {} for k in self.streams}
        self.n_ops = 0
        self.uid = 0

    def sb(self, name, shape, dtype):
        self.uid += 1
        h = self.stack.enter_context(self.nc.sbuf_tensor(f"{name}_{self.uid}", list(shape), dtype))
        return T(name, h[:])

    def ps(self, name, shape, dtype=F32):
        self.uid += 1
        h = self.stack.enter_context(self.nc.psum_tensor(f"{name}_{self.uid}", list(shape), dtype))
        t = T(name, h[:])
        t.res.psum = True
        return t

    def dram(self, name, shape, dtype, kind="Internal"):
        h = self.nc.dram_tensor(name, list(shape), dtype, kind=kind)
        return T(name, h.ap())

    def _collect(self, ekey, reads, writes, is_dma=False):
        deps = {}

        def add(d, same_ok):
            for s, v in d.items():
                if s == ekey and not (same_ok or is_dma):
                    continue
                if deps.get(s, 0) < v:
                    deps[s] = v

        for r in reads:
            add(_res(r).w, same_ok=(ekey != "pe"))
            if _res(r).psum:
                add(_res(r).r, same_ok=False)
        for w in writes:
            add(_res(w).w, same_ok=False)
            add(_res(w).r, same_ok=False)
        kn = self.known[ekey]
        out = []
        for s, v in deps.items():
            if kn.get(s, 0) < v:
                kn[s] = v
                out.append((s, v))
        return out

    def op(self, ekey, fn, reads=(), writes=()):
        waits = self._collect(ekey, reads, writes)
        self.cnt[ekey] += 1
        v = self.cnt[ekey]
        self.streams[ekey].append((waits, fn, (ekey, 1)))
        for r in reads:
            _res(r).r[ekey] = v
        for w in writes:
            _res(w).w[ekey] = v
        self.n_ops += 1

    def dma(self, q, out_ap, in_ap, reads=(), writes=(), **kw):
        sl, i = self.dq[q]
        key = sl[i % self.NDMASEM]
        self.dq[q][1] = i + 1
        waits = self._collect(q, reads, writes, is_dma=True)
        prev = self.cnt[key]
        kn = self.known[q]
        if prev > 0 and kn.get(key, 0) < prev:
            kn[key] = prev
            waits.append((key, prev))
        val = prev + 16
        self.cnt[key] = val

        def fn(e, out_ap=out_ap, in_ap=in_ap, kw=kw):
            return e.dma_start(out=out_ap, in_=in_ap, **kw)

        self.streams[q].append((waits, fn, (key, 16)))
        for r in reads:
            _res(r).r[key] = val
        for w in writes:
            _res(w).w[key] = val
        self.n_ops += 1

    def barrier(self):
        for ekey in self.streams:
            kn = self.known[ekey]
            waits = []
            for s, v in self.cnt.items():
                if s == ekey or v == 0:
                    continue
                if kn.get(s, 0) < v:
                    kn[s] = v
                    waits.append((s, v))
            self.streams[ekey].append((waits, None, None))

    def emit(self):
        nc = self.nc
        sems = self.sems
        streams = self.streams
        with nc.Block() as block:
            def mk(ekey):
                def body(e):
                    for waits, fn, inc in streams[ekey]:
                        for s, v in waits:
                            e.wait_ge(sems[s], v)
                        if fn is not None:
                            fn(e).then_inc(sems[inc[0]], inc[1])
                return body
            block.tensor(mk("pe"))
            block.vector(mk("dve"))
            block.scalar(mk("act"))
            block.gpsimd(mk("pool"))
            block.sync(mk("sp"))
        self.streams = {k: [] for k in streams}


class Builder:
    def __init__(self, NPT, dbg=False):
        self.NPT = NPT
        self.NP = NPT * 128
        self.TT = NPT + NSB
        self.NTOK = self.TT * 128
        self.NH = NPT // 2
        self.NS1 = NSB // 2
        self.MT = self.NH + self.NS1
        self.MTOK = self.MT * 128
        self.BLK = min(256, self.NH * 128)
        self.dbg = dbg
        self.nc = bass.Bass("TRN2", target_bir_lowering=False)
        self.S = Sched(self.nc)
        self.outs = []

    def MM(self, out, lhsT, rhs, start, stop, R, W):
        self.S.op("pe", lambda e: e.matmul(out, lhsT, rhs, start=start, stop=stop), R, W)

    def TR(self, out, in_, ident, R, W):
        self.S.op("pe", lambda e: e.transpose(out, in_, ident), R, W)

    def ACT(self, out, in_, func, R, W, **kw):
        self.S.op("act", lambda e: e.activation(out, in_, func, **kw), R, W)

    def TT_(self, eng, out, a, b, op, R, W):
        self.S.op(eng, lambda e: e.tensor_tensor(out, a, b, op), R, W)

    def TS(self, eng, out, a, s1, s2, op0, op1, R, W):
        if op1 is None:
            self.S.op(eng, lambda e: e.tensor_scalar(out, a, s1, None, op0), R, W)
        else:
            self.S.op(eng, lambda e: e.tensor_scalar(out, a, s1, s2, op0, op1), R, W)

    def STT(self, out, a, s, b, op0, op1, R, W):
        self.S.op("dve", lambda e: e.scalar_tensor_tensor(out, a, s, b, op0, op1), R, W)

    def CP(self, eng, out, in_, R, W):
        if eng == "act":
            self.S.op("act", lambda e: e.activation(out, in_, AF.Copy), R, W)
        else:
            self.S.op(eng, lambda e: e.tensor_copy(out, in_), R, W)

    def RED(self, out, in_, op, R, W):
        self.S.op("dve", lambda e: e.tensor_reduce(out, in_, AX.X, op), R, W)

    def RCP(self, out, in_, R, W):
        self.S.op("dve", lambda e: e.reciprocal(out, in_), R, W)

    def MSET(self, eng, ap, val, W):
        self.S.op(eng, lambda e: e.memset(ap, val), (), W)

    def LD(self, out, in_, R, W, q="sp"):
        self.S.dma(q, out, in_, R, W)

    def phase(self, fn, *a):
        S = self.S
        import os
        self._phi = getattr(self, "_phi", 0) + 1
        lim = int(os.environ.get("K_MAXPH", "999"))
        if self._phi > lim:
            return
        print("PHASE", self._phi, getattr(fn, "__name__", "?"), a, flush=True)
        with ExitStack() as st:
            S.stack = st
            fn(*a)
            S.barrier()
            S.emit()
        S.stack = S.gstack

    def bcast_row(self, name, src_ap_1d, n):
        t = self.S.sb(name, [128, n], F32)
        self.LD(t.ap, src_ap_1d.partition_broadcast(128), [self.din_res], [t])
        return t

    def norm_tile(self, h, nw, out, scr, st):
        self.MSET("dve", st[:, 0:1], 0.0, [st])
        self.ACT(scr.ap, h.ap, AF.Square, [h, st], [scr, st], accum_out=st[:, 0:1])
        self.TS("dve", st[:, 1:2], st[:, 0:1], 1.0 / D, 1e-6, ALU.mult, ALU.add, [st], [st])
        self.ACT(st[:, 1:2], st[:, 1:2], AF.Sqrt, [st], [st])
        self.RCP(st[:, 2:3], st[:, 1:2], [st], [st])
        self.STT(out.ap, h.ap, st[:, 2:3], nw.ap, ALU.mult, ALU.mult, [h, st, nw], [out])

    def transpose_cols(self, src, ncol, dst_ap3, dst_res, psT, eng="dve"):
        kc = ncol // 128
        for k in range(kc):
            self.TR(psT[:, k * 128:(k + 1) * 128], src[:, k * 128:(k + 1) * 128], self.identb.ap, [src, self.identb], [psT])
        self.CP(eng, dst_ap3, psT[:, 0:ncol].rearrange("p (k t) -> p k t", k=kc), [psT], [dst_res])

    def load_w(self, name, w2d, rows, cols, c0=0, c1=None):
        c1 = cols if c1 is None else c1
        kc = rows // 128
        t = self.S.sb(name, [128, kc, c1 - c0], BF16)
        self.LD(t.ap, w2d[:, c0:c1].rearrange("(k p) c -> p k c", p=128), [self.din_res], [t], q="pool")
        return t

    def build(self):
        S, nc = self.S, self.nc
        NTOK, TT, NP, NPT = self.NTOK, self.TT, self.NP, self.NPT
        di = {}
        self.din_res = Res("inputs")

        def inp(name, shape):
            di[name] = S.dram(name, shape, F32, kind="ExternalInput")
            return di[name]

        inp("xin", [NTOK, D]); inp("pin", [2, NTOK, 256]); inp("wkv0", [NSB, H, 64, 64]); inp("shift0", [NSB, D])
        inp("cache", [NSB // 2, MAXW, 512]); inp("sel", [128, 2]); inp("pin1", [self.MTOK, 256])
        inp("norm_w", [2, 4, D]); inp("ffn1_wi", [2, D, 2 * DFF]); inp("ffn1_wo", [2, DFF, D])
        inp("ffn2_wi", [2, D, 2 * DFF]); inp("ffn2_wo", [2, DFF, D]); inp("pe_proj", [2, 256, D]); inp("pe_gate", [2, D, D])
        inp("rwkv_mix", [6, D]); inp("rwkv_wrkv", [3, D, D]); inp("rwkv_wo", [D, D]); inp("rwkv_w0", [D])
        inp("rwkv_w1", [D, 64]); inp("rwkv_w2", [64, D]); inp("rwkv_a0", [D]); inp("rwkv_a1", [D, 64]); inp("rwkv_a2", [64, D])
        inp("rwkv_g1", [D, 160]); inp("rwkv_g2", [160, D]); inp("rwkv_kk", [D]); inp("rwkv_ka", [D]); inp("rwkv_rk", [D])
        inp("rwkv_lnx_w", [D]); inp("rwkv_lnx_b", [D]); inp("attn_wq", [D, 3 * D]); inp("attn_wo", [D, D])
        inp("kv_norm", [D]); inp("w_kv", [D, 512]); inp("final_norm", [D])
        inp("c_ident", [128, 128]); inp("c_tri", [128, 256]); inp("c_m1", [128, 384]); inp("c_m2", [128, 256])
        inp("c_rowmask", [128, 2])
        self.cfgs = self.attn_cfgs()
        for name, (d, nq, nres, g) in self.cfgs.items():
            inp(f"bias_{name}", [4, nq + 128, 4 * nq])
        self.di = di
        def outp(name, shape):
            t = S.dram(name, shape, F32, kind="ExternalOutput")
            self.outs.append(t)
            return t
        self.yout = outp("yout", [self.MTOK, D]); self.kvout = outp("kvout", [NTOK, 512])
        self.wkvp = outp("wkvp", [H, 64, 64]); self.wkvs = outp("wkvs", [NSB, H, 64, 64])
        self.shiftall = outp("shiftall", [NTOK, D])
        self.Hd = S.dram("Hd", [NTOK, D], F32)
        self.Hm = S.dram("Hm", [self.MTOK, D], F32)
        WM = MAXW + self.NH * 128
        self.KTm = S.dram("KTm", [65, 4, WM], BF16); self.VDm = S.dram("VDm", [WM, 256], BF16)
        self.KTsm = S.dram("KTsm", [self.NS1, 65, 4, MAXW + 8], BF16); self.VDsm = S.dram("VDsm", [self.NS1, MAXW + 8, 256], BF16)
        self.HN = S.dram("HN", [NTOK + 1, D], F32)
        self.Rd = S.dram("Rd", [NTOK, D], F32); self.Kd = S.dram("Kd", [NTOK, D], F32); self.Vd = S.dram("Vd", [NTOK, D], F32)
        self.KKd = S.dram("KKd", [NTOK, D], F32); self.Bd = S.dram("Bd", [NTOK, D], F32); self.LWd = S.dram("LWd", [NTOK, D], F32)
        self.Gd = S.dram("Gd", [NTOK, D], F32); self.BONd = S.dram("BONd", [NTOK, H], F32); self.Yd = S.dram("Yd", [NTOK, D], F32)
        self.KTp = S.dram("KTp", [65, 4, MAXW + NP], BF16); self.VDp = S.dram("VDp", [MAXW + NP, 256], BF16)
        self.KTs = S.dram("KTs", [NSB, 65, 4, MAXW + 8], BF16); self.VDs = S.dram("VDs", [NSB, MAXW + 8, 256], BF16)
        if self.dbg:
            self.dbgH = [outp(f"dbgH{i}", [NTOK if i < 4 else self.MTOK, D]) for i in range(8)]
            self.dbgY = outp("dbgY", [NTOK, D])
        self.identb = S.sb("identb", [128, 128], BF16)
        self.identf = S.sb("identf", [128, 128], F32)
        S.dma("pool", self.identb.ap, di["c_ident"].ap, [self.din_res], [self.identb])
        S.dma("sp", self.identf.ap, di["c_ident"].ap, [self.din_res], [self.identf])
        self.zero = S.sb("zero", [128, D], F32)
        self.MSET("pool", self.zero.ap, 0.0, [self.zero])
        self.ones_b = S.sb("ones_b", [128, 64], BF16)
        self.MSET("pool", self.ones_b.ap, 1.0, [self.ones_b])
        self.ones_f = S.sb("ones_f", [128, 1], F32)
        self.MSET("pool", self.ones_f.ap, 1.0, [self.ones_f])
        self.rowmask = S.sb("rowmask", [128, 2], F32)
        S.dma("sp", self.rowmask.ap, di["c_rowmask"].ap, [self.din_res], [self.rowmask])

        self.ndbg = 0
        self.cH, self.cTT, self.cNPT, self.cKT, self.cVD = self.Hd, self.TT, self.NPT, self.KTp, self.VDp
        self.cPin = lambda layer: di["pin"].ap[layer]
        self.phase(self.ph_init)
        for layer in range(2):
            if layer == 1:
                self.phase(self.ph_kv)
                self.phase(self.ph_select)
                self.phase(self.ph_select_s)
                self.cH, self.cTT, self.cNPT, self.cKT, self.cVD = self.Hm, self.MT, self.NH, self.KTm, self.VDm
                self.cPin = lambda layer: di["pin1"].ap
            self.phase(self.ph_ffn, layer, 0)
            self.snap()
            if layer == 0:
                self.phase(self.ph_rwkv_proj)
                self.phase(self.ph_rwkv_scan)
                self.phase(self.ph_rwkv_post)
            else:
                self.phase(self.ph_attn)
            self.snap()
            self.phase(self.ph_ffn, layer, 2)
            self.snap()
            self.phase(self.ph_pe, layer)
            self.snap()
        return nc

    def snap(self):
        if not self.dbg:
            return
        i = self.ndbg
        self.ndbg += 1

        def f():
            self.S.dma("sp", self.dbgH[i].ap, self.cH.ap, [self.cH], [self.dbgH[i]])
        self.phase(f)

    def ph_init(self):
        S = self.S
        S.dma("sp", self.Hd.ap, self.di["xin"].ap, [self.din_res], [self.Hd])
        S.dma("sp", self.HN[0:1, :], self.zero[0:1, :], [self.zero], [self.HN])
        zb = S.sb("zb", [128, 2048], BF16)
        self.MSET("dve", zb.ap, 0.0, [zb])
        nb = S.sb("nb", [128, 4 * MAXW], BF16)
        self.MSET("dve", nb.ap, NEG, [nb])
        for h in range(4):
            S.dma("sp", self.KTp[0:64, h, 0:MAXW], zb[0:64, :], [zb], [self.KTp])
        S.dma("sp", self.KTp[64:65, :, 0:MAXW], nb[0:1, :].rearrange("p (h n) -> p h n", h=4), [nb], [self.KTp])
        for h in range(4):
            for c0 in range(0, self.NP, 2048):
                c1 = min(self.NP, c0 + 2048)
                S.dma("sp", self.KTp[64:65, h, MAXW + c0:MAXW + c1], zb[0:1, 0:c1 - c0], [zb], [self.KTp])
        for r0 in range(0, MAXW, 128):
            S.dma("sp", self.VDp[r0:r0 + 128, :], zb[:, 0:256], [zb], [self.VDp])
        for b in range(self.NS1):
            for h in range(4):
                S.dma("sp", self.KTsm[b, 64:65, h, 0:2048], zb[0:1, 0:2048], [zb], [self.KTsm])
                S.dma("sp", self.KTsm[b, 64:65, h, 2048:2056], zb[0:1, 0:8], [zb], [self.KTsm])

    def ph_ffn(self, layer, which):
        S = self.S
        wi = self.di["ffn1_wi" if which == 0 else "ffn2_wi"].ap[layer]
        wo = self.di["ffn1_wo" if which == 0 else "ffn2_wo"].ap[layer]
        nw = self.bcast_row("nw", self.di["norm_w"].ap[layer, which], D)
        GT = 8
        groups = [list(range(s, min(self.cTT, s + GT))) for s in range(0, self.cTT, GT)]
        maxg = max(len(g) for g in groups)
        hg = [S.sb(f"hg{i}", [128, D], F32) for i in range(maxg)]
        xT = S.sb("xT", [128, 8, maxg * 128], BF16)
        hid = S.sb("hid", [128, NFC, maxg * 128], BF16)
        wob = S.sb("wob", [128, NFC, D], BF16)
        BW = 512
        blocks = [(c0, min(DFF, c0 + BW)) for c0 in range(0, DFF, BW)]
        wib = [S.sb(f"wib{i}", [128, 8, 2 * BW], BF16) for i in range(2)]
        xn = [S.sb(f"xn{i}", [128, D], BF16) for i in range(2)]
        scr = S.sb("scr", [128, D], F32)
        st = [S.sb(f"st{i}", [128, 4], F32) for i in range(2)]
        sg = [S.sb(f"sg{i}", [128, 512], BF16) for i in range(2)]
        psT = [S.ps(f"psT{i}", [128, D], BF16) for i in range(2)]
        pg = [S.ps(f"pg{i}", [128, 512]) for i in range(2)]
        pu = [S.ps(f"pu{i}", [128, 512]) for i in range(2)]
        po = S.ps("po", [128, D])
        cnt = 0
        nload = [0]
        fuse_norm = (layer == 0 and which == 0)
        if fuse_norm:
            nw1 = self.bcast_row("nw1", self.di["norm_w"].ap[0, 1], D)
            hno = [S.sb(f"hno{i}", [128, D], F32) for i in range(2)]

        def load_block(bi):
            c0, c1 = blocks[bi]
            w = wib[nload[0] % 2]
            nload[0] += 1
            bw = c1 - c0
            S.dma("pool", w[:, :, 0:bw], wi[:, c0:c1].rearrange("(k p) c -> p k c", p=128), [self.din_res], [w])
            S.dma("pool", w[:, :, BW:BW + bw], wi[:, DFF + c0:DFF + c1].rearrange("(k p) c -> p k c", p=128), [self.din_res], [w])
            return w

        pending = load_block(0)
        for gi, g in enumerate(groups):
            ng = len(g)
            ntok = ng * 128
            for i, t in enumerate(g):
                self.LD(hg[i].ap, self.cH[t * 128:(t + 1) * 128, :], [self.cH], [hg[i]])
                self.norm_tile(hg[i], nw, xn[i % 2], scr, st[i % 2])
                self.transpose_cols(xn[i % 2], D, xT[:, :, i * 128:(i + 1) * 128], xT, psT[i % 2])
            for bi, (c0, c1) in enumerate(blocks):
                w = pending
                if bi + 1 < len(blocks):
                    pending = load_block(bi + 1)
                elif gi + 1 < len(groups):
                    pending = load_block(0)
                if bi == 1:
                    S.dma("pool", wob.ap, wo.rearrange("(k p) c -> p k c", p=128), [self.din_res], [wob])
                for cc in range((c1 - c0) // 128):
                    c = c0 // 128 + cc
                    for n0 in range(0, ntok, 512):
                        n1 = min(ntok, n0 + 512)
                        nn = n1 - n0
                        j = cnt % 2
                        cnt += 1
                        for k in range(8):
                            self.MM(pg[j][:, 0:nn], w[:, k, cc * 128:(cc + 1) * 128], xT[:, k, n0:n1], k == 0, k == 7, [w, xT], [pg[j]])
                        for k in range(8):
                            self.MM(pu[j][:, 0:nn], w[:, k, BW + cc * 128:BW + (cc + 1) * 128], xT[:, k, n0:n1], k == 0, k == 7, [w, xT], [pu[j]])
                        self.ACT(sg[j][:, 0:nn], pg[j][:, 0:nn], AF.Silu, [pg[j]], [sg[j]])
                        self.TT_("dve", hid[:, c, n0:n1], pu[j][:, 0:nn], sg[j][:, 0:nn], ALU.mult, [pu[j], sg[j]], [hid])
            for i, t in enumerate(g):
                for hh in range(2):
                    for c in range(NFC):
                        self.MM(po[:, hh * 512:(hh + 1) * 512], hid[:, c, i * 128:(i + 1) * 128], wob[:, c, hh * 512:(hh + 1) * 512],
                                c == 0, c == NFC - 1, [hid, wob], [po])
                self.STT(hg[i].ap, po.ap, 0.5, hg[i].ap, ALU.mult, ALU.add, [po, hg[i]], [hg[i]])
                self.LD(self.cH[t * 128:(t + 1) * 128, :], hg[i].ap, [hg[i]], [self.cH])
                if fuse_norm:
                    o_ = hno[i % 2]
                    self.norm_tile(hg[i], nw1, o_, scr, st[i % 2])
                    self.LD(self.HN[1 + t * 128:1 + (t + 1) * 128, :], o_.ap, [o_], [self.HN])
                    self.LD(self.shiftall[t * 128:(t + 1) * 128, :], o_.ap, [o_], [self.shiftall])

    def ph_pe(self, layer):
        S = self.S
        nw = self.bcast_row("nw", self.di["norm_w"].ap[layer, 3], D)
        wg = self.load_w("wg", self.di["pe_gate"].ap[layer], D, D)
        wp = self.load_w("wp", self.di["pe_proj"].ap[layer], 256, D)
        h = [S.sb(f"h{i}", [128, D], F32) for i in range(2)]
        pb = [S.sb(f"pb{i}", [128, 256], BF16) for i in range(2)]
        xn = [S.sb(f"xn{i}", [128, D], BF16) for i in range(2)]
        xT = [S.sb(f"xT{i}", [128, 8, 128], BF16) for i in range(2)]
        pT = [S.sb(f"pT{i}", [128, 2, 128], BF16) for i in range(2)]
        sgm = [S.sb(f"sgm{i}", [128, D], F32) for i in range(2)]
        scr = S.sb("scr", [128, D], F32)
        st = [S.sb(f"st{i}", [128, 4], F32) for i in range(2)]
        psT = [S.ps(f"psT{i}", [128, D], BF16) for i in range(2)]
        pg = S.ps("pg", [128, D])
        pe = S.ps("pe", [128, D])
        fuse_final = (layer == 1)
        if fuse_final:
            nwf = self.bcast_row("nwf", self.di["final_norm"].ap, D)
            fo = [S.sb(f"fo{i}", [128, D], F32) for i in range(2)]
        for t in range(self.cTT):
            j = t % 2
            rows = slice(t * 128, (t + 1) * 128)
            self.LD(h[j].ap, self.cH[rows, :], [self.cH], [h[j]])
            S.dma("pool", pb[j].ap, self.cPin(layer)[rows, :], [self.din_res], [pb[j]])
            self.norm_tile(h[j], nw, xn[j], scr, st[j])
            self.transpose_cols(xn[j], D, xT[j].ap, xT[j], psT[j])
            self.transpose_cols(pb[j], 256, pT[j].ap, pT[j], psT[j], eng="act")
            for hh in range(2):
                cs = slice(hh * 512, (hh + 1) * 512)
                for k in range(8):
                    self.MM(pg[:, cs], xT[j][:, k, :], wg[:, k, cs], k == 0, k == 7, [xT[j], wg], [pg])
                for k in range(2):
                    self.MM(pe[:, cs], pT[j][:, k, :], wp[:, k, cs], k == 0, k == 1, [pT[j], wp], [pe])
            self.ACT(sgm[j].ap, pg.ap, AF.Sigmoid, [pg], [sgm[j]])
            self.TT_("dve", sgm[j].ap, pe.ap, sgm[j].ap, ALU.mult, [pe, sgm[j]], [sgm[j]])
            self.TT_("pool", h[j].ap, h[j].ap, sgm[j].ap, ALU.add, [h[j], sgm[j]], [h[j]])
            if fuse_final:
                self.norm_tile(h[j], nwf, fo[j], scr, st[j])
                self.LD(self.yout[rows, :], fo[j].ap, [fo[j]], [self.yout], q="pool")
            else:
                self.LD(self.cH[rows, :], h[j].ap, [h[j]], [self.cH], q="pool")

    def ph_final(self):
        S = self.S
        nw = self.bcast_row("nw", self.di["final_norm"].ap, D)
        h = [S.sb(f"h{i}", [128, D], F32) for i in range(2)]
        o = [S.sb(f"o{i}", [128, D], F32) for i in range(2)]
        scr = S.sb("scr", [128, D], F32)
        st = [S.sb(f"st{i}", [128, 4], F32) for i in range(2)]
        for t in range(self.cTT):
            j = t % 2
            rows = slice(t * 128, (t + 1) * 128)
            self.LD(h[j].ap, self.cH[rows, :], [self.cH], [h[j]])
            self.norm_tile(h[j], nw, o[j], scr, st[j])
            self.LD(self.yout[rows, :], o[j].ap, [o[j]], [self.yout])

    def ph_rwkv_norm(self):
        S = self.S
        nw = self.bcast_row("nw", self.di["norm_w"].ap[0, 1], D)
        h = [S.sb(f"h{i}", [128, D], F32) for i in range(2)]
        o = [S.sb(f"o{i}", [128, D], F32) for i in range(2)]
        scr = S.sb("scr", [128, D], F32)
        st = [S.sb(f"st{i}", [128, 4], F32) for i in range(2)]
        for t in range(self.TT):
            j = t % 2
            rows = slice(t * 128, (t + 1) * 128)
            self.LD(h[j].ap, self.Hd[rows, :], [self.Hd], [h[j]])
            self.norm_tile(h[j], nw, o[j], scr, st[j])
            self.LD(self.HN[1 + t * 128:1 + (t + 1) * 128, :], o[j].ap, [o[j]], [self.HN])
            self.LD(self.shiftall[rows, :], o[j].ap, [o[j]], [self.shiftall])

    def ph_rwkv_proj(self):
        S, di = self.S, self.di
        mixn = S.sb("mixn", [48, 128], F32)
        self.LD(mixn.ap, di["rwkv_mix"].ap.rearrange("a (k p) -> (a k) p", p=128), [self.din_res], [mixn])
        mixT = S.sb("mixT", [128, 48], F32)
        w0b = self.bcast_row("w0b", di["rwkv_w0"].ap, D)
        a0b = self.bcast_row("a0b", di["rwkv_a0"].ap, D)
        kkb = self.bcast_row("kkb", di["rwkv_kk"].ap, D)
        kab = self.bcast_row("kab", di["rwkv_ka"].ap, D)
        rkb = self.bcast_row("rkb", di["rwkv_rk"].ap, D)
        wr = [self.load_w(f"wrkv{i}", di["rwkv_wrkv"].ap[i], D, D) for i in range(3)]
        w1 = self.load_w("w1", di["rwkv_w1"].ap, D, 64)
        a1 = self.load_w("a1", di["rwkv_a1"].ap, D, 64)
        g1 = self.load_w("g1", di["rwkv_g1"].ap, D, 160)
        w2 = S.sb("w2", [64, D], BF16); S.dma("pool", w2.ap, di["rwkv_w2"].ap, [self.din_res], [w2])
        a2 = S.sb("a2", [64, D], BF16); S.dma("pool", a2.ap, di["rwkv_a2"].ap, [self.din_res], [a2])
        g2a = S.sb("g2a", [128, D], BF16); S.dma("pool", g2a.ap, di["rwkv_g2"].ap[0:128, :], [self.din_res], [g2a])
        g2b = S.sb("g2b", [32, D], BF16); S.dma("pool", g2b.ap, di["rwkv_g2"].ap[128:160, :], [self.din_res], [g2b])
        psm = S.ps("psm", [128, 512])
        self.S.op("pe", lambda e: e.transpose(psm[:, 0:48], mixn.ap, self.identf[0:48, 0:48]), [mixn, self.identf], [psm])
        self.CP("dve", mixT.ap, psm[:, 0:48], [psm], [mixT])
        def scaled(name, w, j, cols):
            t = S.sb(name, [128, 8, cols], BF16)
            for k in range(8):
                self.TS("dve" if k % 2 else "pool", t[:, k, :], w[:, k, :], mixT[:, j * 8 + k:j * 8 + k + 1], None, ALU.mult, None, [w, mixT], [t])
            return t
        wrs = [scaled("wrs0", wr[0], 0, D), scaled("wrs1", wr[1], 2, D), scaled("wrs2", wr[2], 3, D)]
        w1s = scaled("w1s", w1, 1, 64)
        a1s = scaled("a1s", a1, 4, 64)
        g1s = scaled("g1s", g1, 5, 160)
        hn = [S.sb(f"hn{i}", [128, D], F32) for i in range(2)]
        hp = [S.sb(f"hp{i}", [128, D], F32) for i in range(2)]
        xx = S.sb("xx", [128, D], F32)
        tmp = S.sb("tmp", [128, D], F32)
        xm = [S.sb(f"xm{i}", [128, D], BF16) for i in range(2)]
        xTh = [S.sb(f"xTh{i}", [128, 8, 128], BF16) for i in range(2)]
        xTx = [S.sb(f"xTx{i}", [128, 8, 128], BF16) for i in range(2)]
        lo = S.sb("lo", [128, 128], BF16)
        lo2 = S.sb("lo2", [32, 128], BF16)
        o_r = S.sb("o_r", [128, D], F32); o_k = S.sb("o_k", [128, D], F32); o_v = S.sb("o_v", [128, D], F32)
        o_a = S.sb("o_a", [128, D], F32); o_kk = S.sb("o_kk", [128, D], F32); o_b = S.sb("o_b", [128, D], F32)
        o_w = S.sb("o_w", [128, D], F32); o_g = S.sb("o_g", [128, D], F32)
        sm = S.sb("sm", [128, 3, H], F32)
        psT = [S.ps(f"psT{i}", [128, D], BF16) for i in range(2)]
        pp = [S.ps(f"pp{i}", [128, D]) for i in range(2)]
        pl = S.ps("pl", [128, 512])
        v3 = lambda t_: t_.ap.rearrange("p (h n) -> p h n", h=H)
        for t in range(self.TT):
            j = t % 2
            rows = slice(t * 128, (t + 1) * 128)
            self.LD(hn[j].ap, self.HN[1 + t * 128:1 + (t + 1) * 128, :], [self.HN], [hn[j]])
            self.LD(hp[j].ap, self.HN[t * 128:(t + 1) * 128, :], [self.HN], [hp[j]])
            if t >= self.NPT:
                self.LD(hp[j][0:1, :], di["shift0"].ap[t - self.NPT:t - self.NPT + 1, :], [self.din_res], [hp[j]])
            self.CP("pool", xm[0].ap, hn[j].ap, [hn[j]], [xm[0]])
            self.TT_("dve", xm[1].ap, hp[j].ap, hn[j].ap, ALU.subtract, [hp[j], hn[j]], [xm[1]])
            xh, xd = xTh[j], xTx[j]
            self.transpose_cols(xm[0], D, xh.ap, xh, psT[0], eng="act")
            self.transpose_cols(xm[1], D, xd.ap, xd, psT[1], eng="act")
            def proj(w, ws, pt):
                for hh in range(2):
                    cs = slice(hh * 512, (hh + 1) * 512)
                    for k in range(8):
                        self.MM(pt[:, cs], xh[:, k, :], w[:, k, cs], k == 0, False, [xh, w], [pt])
                    for k in range(8):
                        self.MM(pt[:, cs], xd[:, k, :], ws[:, k, cs], False, k == 7, [xd, ws], [pt])
            def lora1(out, w, ws, c0, c1):
                for k in range(8):
                    self.MM(out, w[:, k, c0:c1], xh[:, k, :], k == 0, False, [w, xh], [pl])
                for k in range(8):
                    self.MM(out, ws[:, k, c0:c1], xd[:, k, :], False, k == 7, [ws, xd], [pl])
            proj(wr[0], wrs[0], pp[0]); self.CP("act", o_r.ap, pp[0].ap, [pp[0]], [o_r])
            proj(wr[1], wrs[1], pp[1]); self.CP("act", o_k.ap, pp[1].ap, [pp[1]], [o_k])
            proj(wr[2], wrs[2], pp[0]); self.CP("act", o_v.ap, pp[0].ap, [pp[0]], [o_v])
            lora1(pl[0:64, 0:128], w1, w1s, 0, 64)
            self.ACT(lo[0:64, :], pl[0:64, 0:128], AF.Tanh, [pl], [lo])
            for hh in range(2):
                cs = slice(hh * 512, (hh + 1) * 512)
                self.MM(pp[1][:, cs], lo[0:64, :], w2[:, cs], True, True, [lo, w2], [pp[1]])
            self.TT_("dve", o_w.ap, pp[1].ap, w0b.ap, ALU.add, [pp[1], w0b], [o_w])
            self.ACT(o_w.ap, o_w.ap, AF.Sigmoid, [o_w], [o_w])
            self.TS("dve", o_w.ap, o_w.ap, -float(np.exp(-0.5)), None, ALU.mult, None, [o_w], [o_w])
            if t >= self.NPT:
                self.TS("dve", o_w.ap, o_w.ap, self.rowmask[:, 1:2], None, ALU.mult, None, [o_w, self.rowmask], [o_w])
            lora1(pl[0:64, 0:128], a1, a1s, 0, 64)
            self.CP("act", lo[0:64, :], pl[0:64, 0:128], [pl], [lo])
            for hh in range(2):
                cs = slice(hh * 512, (hh + 1) * 512)
                self.MM(pp[0][:, cs], lo[0:64, :], a2[:, cs], True, True, [lo, a2], [pp[0]])
            self.TT_("dve", o_a.ap, pp[0].ap, a0b.ap, ALU.add, [pp[0], a0b], [o_a])
            self.ACT(o_a.ap, o_a.ap, AF.Sigmoid, [o_a], [o_a])
            lora1(pl[:, 0:128], g1, g1s, 0, 128)
            lora1(pl[0:32, 128:256], g1, g1s, 128, 160)
            self.ACT(lo.ap, pl[:, 0:128], AF.Sigmoid, [pl], [lo])
            self.ACT(lo2.ap, pl[0:32, 128:256], AF.Sigmoid, [pl], [lo2])
            for hh in range(2):
                cs = slice(hh * 512, (hh + 1) * 512)
                self.MM(pp[1][:, cs], lo.ap, g2a[:, cs], True, False, [lo, g2a], [pp[1]])
                self.MM(pp[1][:, cs], lo2.ap, g2b[:, cs], False, True, [lo2, g2b], [pp[1]])
            self.CP("act", o_g.ap, pp[1].ap, [pp[1]], [o_g])
            self.TT_("dve", o_kk.ap, o_k.ap, kkb.ap, ALU.mult, [o_k, kkb], [o_kk])
            self.TT_("pool", tmp.ap, o_kk.ap, o_kk.ap, ALU.mult, [o_kk], [tmp])
            self.RED(sm[:, 0, :], v3(tmp), ALU.add, [tmp], [sm])
            self.ACT(sm[:, 0, :], sm[:, 0, :], AF.Sqrt, [sm], [sm])
            self.TS("dve", sm[:, 0, :], sm[:, 0, :], 1e-12, None, ALU.max, None, [sm], [sm])
            self.RCP(sm[:, 1, :], sm[:, 0, :], [sm], [sm])
            self.TT_("dve", v3(o_kk), v3(o_kk), sm[:, 1, :].unsqueeze(2).broadcast_to([128, H, 64]), ALU.mult, [o_kk, sm], [o_kk])
            self.TT_("pool", o_b.ap, o_kk.ap, o_a.ap, ALU.mult, [o_kk, o_a], [o_b])
            self.TS("dve", tmp.ap, o_a.ap, -1.0, None, ALU.add, None, [o_a], [tmp])
            self.TT_("dve", tmp.ap, tmp.ap, kab.ap, ALU.mult, [tmp, kab], [tmp])
            self.STT(o_k.ap, tmp.ap, 1.0, o_k.ap, ALU.add, ALU.mult, [tmp, o_k], [o_k])
            self.TT_("pool", tmp.ap, o_r.ap, o_k.ap, ALU.mult, [o_r, o_k], [tmp])
            self.TT_("dve", tmp.ap, tmp.ap, rkb.ap, ALU.mult, [tmp, rkb], [tmp])
            self.RED(sm[:, 2, :], v3(tmp), ALU.add, [tmp], [sm])
            if t >= self.NPT:
                self.TS("dve", o_b.ap, o_b.ap, self.rowmask[:, 1:2], None, ALU.mult, None, [o_b, self.rowmask], [o_b])
                self.TS("dve", o_k.ap, o_k.ap, self.rowmask[:, 1:2], None, ALU.mult, None, [o_k, self.rowmask], [o_k])
            for src, dst in ((o_r, self.Rd), (o_k, self.Kd), (o_v, self.Vd), (o_kk, self.KKd), (o_b, self.Bd), (o_w, self.LWd), (o_g, self.Gd)):
                self.LD(dst[rows, :], src.ap, [src], [dst], q="pool")
            self.LD(self.BONd[rows, :], sm[:, 2, :], [sm], [self.BONd], q="pool")

    def ph_rwkv_scan(self):
        S, di = self.S, self.di
        import os
        LV = float(os.environ.get("K_SCAN", "9"))
        tri = S.sb("tri", [128, 256], F32); self.LD(tri.ap, di["c_tri"].ap, [self.din_res], [tri])
        m1 = S.sb("m1", [128, 384], BF16); S.dma("pool", m1.ap, di["c_m1"].ap, [self.din_res], [m1])
        m2 = S.sb("m2", [128, 256], BF16); S.dma("pool", m2.ap, di["c_m2"].ap, [self.din_res], [m2])
        names = ("r", "k", "v", "kk", "b", "lw")
        srcs = (self.Rd, self.Kd, self.Vd, self.KKd, self.Bd, self.LWd)
        inb = {n: S.sb(f"in_{n}", [128, D], F32) for n in names}
        ecum = S.sb("ecum", [128, D], F32); encum = S.sb("encum", [128, D], F32)
        eex = S.sb("eex", [128, D], F32); erc = S.sb("erc", [128, D], F32)
        PB = [dict(tA=S.sb(f"tA{p}", [128, D], BF16), tR=S.sb(f"tR{p}", [128, D], BF16), tB=S.sb(f"tB{p}", [128, D], BF16),
                   tK=S.sb(f"tK{p}", [128, D], BF16), hB=S.sb(f"hB{p}", [128, D], BF16), hK=S.sb(f"hK{p}", [128, D], BF16),
                   vb=S.sb(f"vb{p}", [128, D], BF16), wc=S.sb(f"wc{p}", [64, H], F32)) for p in range(2)]
        yt = S.sb("yt", [128, D], F32)
        Mf = [S.sb(f"Mf{h}", [64, 64], F32) for h in range(H)]
        Mb = [S.sb(f"Mb{h}", [64, 64], BF16) for h in range(H)]
        G = 6
        NB = G
        TTh = [S.sb(f"TTh{i}", [64, 512], BF16) for i in range(NB)]
        SC1 = [S.sb(f"SC1{i}", [128, 384], BF16) for i in range(NB)]
        SC2 = [S.sb(f"SC2{i}", [128, 256], BF16) for i in range(NB)]
        XX = [[S.sb(f"XX{i}_{p}", [128, 256], BF16) for p in range(2)] for i in range(NB)]
        PP = [[S.sb(f"PP{i}_{p}", [128, 128], BF16) for p in range(2)] for i in range(NB)]
        Zb = [S.sb(f"Zb{i}", [128, 64], BF16) for i in range(NB)]
        AhT = [S.sb(f"AhT{i}", [64, 128], BF16) for i in range(NB)]
        Ub = [S.sb(f"Ub{i}", [128, 64], BF16) for i in range(NB)]
        st0 = S.sb("st0", [64, 64], F32)
        pcum = S.ps("pcum", [128, D]); prc = pcum
        bank = [S.ps(f"bk{i}", [128, 512]) for i in range(G)]
        pw = bank
        def prep(t, p):
            tA, tR, tB, tK, hB, hK, vb, wc = (PB[p][k] for k in ('tA', 'tR', 'tB', 'tK', 'hB', 'hK', 'vb', 'wc'))
            if LV < 2:
                return
            rows = slice(t * 128, (t + 1) * 128)
            for n, s_ in zip(names, srcs):
                self.LD(inb[n].ap, s_[rows, :], [s_], [inb[n]])
            lw = inb["lw"]
            if LV < 2.2:
                return
            for hh in range(2):
                cs = slice(hh * 512, (hh + 1) * 512)
                self.MM(pcum[:, cs], tri[:, 0:128], lw[:, cs], True, True, [tri, lw], [pcum])
            if LV < 2.12:
                return
            self.ACT(ecum.ap, pcum.ap, AF.Exp, [pcum], [ecum])
            if LV < 2.13:
                return
            self.ACT(encum.ap, pcum.ap, AF.Exp, [pcum], [encum], scale=-1.0)
            if LV < 2.14:
                return
            self.TT_("dve", eex.ap, pcum.ap, lw.ap, ALU.subtract, [pcum, lw], [eex])
            self.ACT(eex.ap, eex.ap, AF.Exp, [eex], [eex])
            if LV < 2.15:
                return
            for hh in range(2):
                cs = slice(hh * 512, (hh + 1) * 512)
                self.MM(prc[:, cs], tri[:, 128:256], lw[:, cs], True, True, [tri, lw], [prc])
            self.ACT(erc.ap, prc.ap, AF.Exp, [prc], [erc])
            if LV < 2.3:
                return
            self.STT(tA.ap, inb["kk"].ap, -1.0, eex.ap, ALU.mult, ALU.mult, [inb["kk"], eex], [tA])
            self.TT_("pool", tR.ap, inb["r"].ap, ecum.ap, ALU.mult, [inb["r"], ecum], [tR])
            self.TT_("dve", tB.ap, inb["b"].ap, encum.ap, ALU.mult, [inb["b"], encum], [tB])
            self.TT_("pool", tK.ap, inb["k"].ap, encum.ap, ALU.mult, [inb["k"], encum], [tK])
            self.TT_("dve", hB.ap, inb["b"].ap, erc.ap, ALU.mult, [inb["b"], erc], [hB])
            self.TT_("pool", hK.ap, inb["k"].ap, erc.ap, ALU.mult, [inb["k"], erc], [hK])
            self.CP("pool", vb.ap, inb["v"].ap, [inb["v"]], [vb])
            if LV < 2.4:
                return
            pz = pw[0]
            for h in range(H):
                self.MM(pz[0:64, h:h + 1], lw[:, h * 64:(h + 1) * 64], self.ones_f.ap, True, True, [lw, self.ones_f], [pz])
            self.ACT(wc.ap, pz[0:64, 0:H], AF.Exp, [pz], [wc])

        def heads_group(t, p, g0):
            tA, tR, tB, tK, hB, hK, vb, wc = (PB[p][k] for k in ('tA', 'tR', 'tB', 'tK', 'hB', 'hK', 'vb', 'wc'))
            heads = list(range(g0, min(H, g0 + G)))
            for i, h in enumerate(heads):
                hs = slice(h * 64, (h + 1) * 64)
                bk, tt = bank[i], TTh[i]
                pv = bk[0:64, 0:256].bitcast(BF16)
                for q, src in enumerate((tA, tR, tB, tK)):
                    self.TR(pv[:, q * 128:(q + 1) * 128], src[:, hs], self.identb.ap, [src, self.identb], [bk])
                self.CP("act", tt.ap, pv, [bk], [tt])
            for i, h in enumerate(heads):
                bk, tt, s1 = bank[i], TTh[i], SC1[i]
                self.MM(bk[:, 0:128], tt[:, 0:128], tt[:, 256:384], True, True, [tt], [bk])
                self.MM(bk[:, 128:384], tt[:, 256:384], tt[:, 0:256], True, True, [tt], [bk])
                self.TT_("dve", s1.ap, bk[:, 0:384], m1.ap, ALU.mult, [bk, m1], [s1])
            for i, h in enumerate(heads):
                bk, tt, s2 = bank[i], TTh[i], SC2[i]
                self.MM(bk[:, 0:256], tt[:, 384:512], tt[:, 0:256], True, True, [tt], [bk])
                self.TT_("dve", s2.ap, bk[:, 0:256], m2.ap, ALU.mult, [bk, m2], [s2])
            stt = {}
            for i, h in enumerate(heads):
                stt[i] = [SC1[i][:, 0:256], SC1[i], self.identb.ap, self.identb]
            for lv in range(7):
                for i, h in enumerate(heads):
                    bk = bank[i]
                    xcur, xres, pcur, pres = stt[i]
                    X, XT_ = xcur[:, 0:128], xcur[:, 128:256]
                    self.MM(bk[:, 0:128], X, pcur, True, True, [xres, pres], [bk])
                    if lv < 6:
                        self.MM(bk[:, 128:256], XT_, X, True, True, [xres], [bk])
                        self.MM(bk[:, 256:384], X, XT_, True, True, [xres], [bk])
                    pn = PP[i][lv % 2]
                    self.TT_("dve", pn.ap, bk[:, 0:128], pcur, ALU.add, [bk, pres], [pn])
                    stt[i][2], stt[i][3] = pn.ap, pn
                    if lv < 6:
                        xn_ = XX[i][lv % 2]
                        self.CP("act", xn_.ap, bk[:, 128:384], [bk], [xn_])
                        stt[i][0], stt[i][1] = xn_.ap, xn_
            for i, h in enumerate(heads):
                hs = slice(h * 64, (h + 1) * 64)
                bk = bank[i]
                P, pres = stt[i][2], stt[i][3]
                self.MM(bk[:, 0:64], SC2[i][:, 0:128], vb[:, hs], True, True, [SC2[i], vb], [bk])
                self.MM(bk[0:64, 64:192], tA[:, hs], P, True, True, [tA, pres], [bk])
                self.CP("act", Zb[i].ap, bk[:, 0:64], [bk], [Zb[i]])
                self.CP("dve", AhT[i].ap, bk[0:64, 64:192], [bk], [AhT[i]])
            for i, h in enumerate(heads):
                bk = bank[i]
                P, pres = stt[i][2], stt[i][3]
                self.MM(bk[:, 256:320], P, Zb[i].ap, True, False, [pres, Zb[i]], [bk])
                self.MM(bk[:, 256:320], AhT[i].ap, Mb[h].ap, False, True, [AhT[i], Mb[h]], [bk])
                self.CP("dve", Ub[i].ap, bk[:, 256:320], [bk], [Ub[i]])
            for i, h in enumerate(heads):
                hs = slice(h * 64, (h + 1) * 64)
                bk, tt = bank[i], TTh[i]
                self.MM(bk[:, 320:384], SC2[i][:, 128:256], vb[:, hs], True, False, [SC2[i], vb], [bk])
                self.MM(bk[:, 320:384], tt[:, 128:256], Mb[h].ap, False, False, [tt, Mb[h]], [bk])
                self.MM(bk[:, 320:384], SC1[i][:, 256:384], Ub[i].ap, False, True, [SC1[i], Ub[i]], [bk])
                self.CP("act", yt[:, hs], bk[:, 320:384], [bk], [yt])
            for i, h in enumerate(heads):
                hs = slice(h * 64, (h + 1) * 64)
                bk = bank[i]
                self.MM(bk[0:64, 384:448], hK[:, hs], vb[:, hs], True, False, [hK, vb], [bk])
                self.MM(bk[0:64, 384:448], hB[:, hs], Ub[i].ap, False, True, [hB, Ub[i]], [bk])
                self.STT(Mf[h].ap, Mf[h].ap, wc[:, h:h + 1], bk[0:64, 384:448], ALU.mult, ALU.add, [Mf[h], wc, bk], [Mf[h]])
                self.CP("pool", Mb[h].ap, Mf[h].ap, [Mf[h]], [Mb[h]])

        seqs = [(list(range(self.NPT)), None, self.wkvp.ap)]
        for b in range(NSB):
            seqs.append(([self.NPT + b], b, self.wkvs.ap[b]))
        hcnt = 0
        for tiles, b0, dst in seqs:
            for h in range(H):
                if b0 is None:
                    self.MSET("pool", Mf[h].ap, 0.0, [Mf[h]])
                else:
                    self.LD(st0.ap, di["wkv0"].ap[b0, h], [self.din_res], [st0])
                    pz = pw[h % 4]
                    self.S.op("pe", lambda e, o=pz[0:64, 0:64], i_=st0.ap, idn=self.identf[0:64, 0:64]: e.transpose(o, i_, idn), [st0, self.identf], [pz])
                    self.CP("dve", Mf[h].ap, pz[0:64, 0:64], [pz], [Mf[h]])
                self.CP("pool", Mb[h].ap, Mf[h].ap, [Mf[h]], [Mb[h]])
            par = 0
            prep(tiles[0], par)
            for idx, t in enumerate(tiles):
                rows = slice(t * 128, (t + 1) * 128)
                gl = list(range(0, H, G))
                for gi, g0 in enumerate(gl):
                    if gi == len(gl) - 1 and idx + 1 < len(tiles):
                        prep(tiles[idx + 1], 1 - par)
                    heads_group(t, par, g0)
                par = 1 - par
                self.LD(self.Yd[rows, :], yt.ap, [yt], [self.Yd])
            for h in range(H):
                pz = pw[h % 4]
                self.S.op("pe", lambda e, o=pz[0:64, 0:64], i_=Mf[h].ap, idn=self.identf[0:64, 0:64]: e.transpose(o, i_, idn), [Mf[h], self.identf], [pz])
                self.CP("dve", st0.ap, pz[0:64, 0:64], [pz], [st0])
                self.LD(dst[h], st0.ap, [st0], [self.wkvp if b0 is None else self.wkvs])

    def ph_rwkv_post(self):
        S, di = self.S, self.di
        lwb = self.bcast_row("lwb", di["rwkv_lnx_w"].ap, D)
        lbb = self.bcast_row("lbb", di["rwkv_lnx_b"].ap, D)
        wo = self.load_w("wo", di["rwkv_wo"].ap, D, D)
        y = [S.sb(f"y{i}", [128, D], F32) for i in range(2)]
        g = [S.sb(f"g{i}", [128, D], F32) for i in range(2)]
        v = [S.sb(f"v{i}", [128, D], F32) for i in range(2)]
        h = [S.sb(f"h{i}", [128, D], F32) for i in range(2)]
        bon = [S.sb(f"bon{i}", [128, H], F32) for i in range(2)]
        tmpl = [S.sb(f"tmp{i}", [128, D], F32) for i in range(2)]
        sml = [S.sb(f"sm{i}", [128, 4, H], F32) for i in range(2)]
        ob = [S.sb(f"ob{i}", [128, D], BF16) for i in range(2)]
        xT = [S.sb(f"xT{i}", [128, 8, 128], BF16) for i in range(2)]
        psT = [S.ps(f"psT{i}", [128, D], BF16) for i in range(2)]
        pol = [S.ps(f"po{i}", [128, D]) for i in range(2)]
        v3 = lambda a: a.rearrange("p (h n) -> p h n", h=H)
        bc = lambda a: a.unsqueeze(2).broadcast_to([128, H, 64])

        def tile_ops(t):
            j = t % 2
            rows = slice(t * 128, (t + 1) * 128)
            yy, tmp, sm, po = y[j], tmpl[j], sml[j], pol[j]
            ops = []

            def loads():
                self.LD(y[j].ap, self.Yd[rows, :], [self.Yd], [y[j]])
                self.LD(g[j].ap, self.Gd[rows, :], [self.Gd], [g[j]])
                self.LD(v[j].ap, self.Vd[rows, :], [self.Vd], [v[j]])
                self.LD(h[j].ap, self.Hd[rows, :], [self.Hd], [h[j]])
                self.LD(bon[j].ap, self.BONd[rows, :], [self.BONd], [bon[j]])
                if self.dbg:
                    self.LD(self.dbgY[rows, :], y[j].ap, [y[j]], [self.dbgY])
            ops.append(loads)
            ops.append(lambda: self.RED(sm[:, 0, :], v3(yy.ap), ALU.add, [yy], [sm]))
            ops.append(lambda: self.TS("dve", sm[:, 0, :], sm[:, 0, :], 1.0 / 64, None, ALU.mult, None, [sm], [sm]))
            ops.append(lambda: self.TT_("dve", v3(yy.ap), v3(yy.ap), bc(sm[:, 0, :]), ALU.subtract, [yy, sm], [yy]))
            ops.append(lambda: self.TT_("pool", tmp.ap, yy.ap, yy.ap, ALU.mult, [yy], [tmp]))
            ops.append(lambda: self.RED(sm[:, 1, :], v3(tmp.ap), ALU.add, [tmp], [sm]))
            ops.append(lambda: self.TS("dve", sm[:, 1, :], sm[:, 1, :], 1.0 / 64, 64e-5, ALU.mult, ALU.add, [sm], [sm]))
            ops.append(lambda: self.ACT(sm[:, 1, :], sm[:, 1, :], AF.Sqrt, [sm], [sm]))
            ops.append(lambda: self.RCP(sm[:, 2, :], sm[:, 1, :], [sm], [sm]))
            ops.append(lambda: self.TT_("pool", v3(tmp.ap), v3(v[j].ap), bc(bon[j].ap), ALU.mult, [v[j], bon[j]], [tmp]))
            ops.append(lambda: self.TT_("dve", v3(yy.ap), v3(yy.ap), bc(sm[:, 2, :]), ALU.mult, [yy, sm], [yy]))
            ops.append(lambda: self.TT_("pool", yy.ap, yy.ap, lwb.ap, ALU.mult, [yy, lwb], [yy]))
            ops.append(lambda: self.TT_("dve", yy.ap, yy.ap, lbb.ap, ALU.add, [yy, lbb], [yy]))
            ops.append(lambda: self.TT_("pool", yy.ap, yy.ap, tmp.ap, ALU.add, [yy, tmp], [yy]))
            ops.append(lambda: self.TT_("dve", ob[j].ap, yy.ap, g[j].ap, ALU.mult, [yy, g[j]], [ob[j]]))
            ops.append(lambda: self.transpose_cols(ob[j], D, xT[j].ap, xT[j], psT[j], eng="act"))

            def mm():
                for hh in range(2):
                    cs = slice(hh * 512, (hh + 1) * 512)
                    for k in range(8):
                        self.MM(po[:, cs], xT[j][:, k, :], wo[:, k, cs], k == 0, k == 7, [xT[j], wo], [po])
            ops.append(mm)
            ops.append(lambda: self.TT_("dve", h[j].ap, po.ap, h[j].ap, ALU.add, [po, h[j]], [h[j]]))
            ops.append(lambda: self.LD(self.Hd[rows, :], h[j].ap, [h[j]], [self.Hd]))
            return ops

        for t0_ in range(0, self.TT, 2):
            lists = [tile_ops(t) for t in range(t0_, min(self.TT, t0_ + 2))]
            for k in range(len(lists[0])):
                for l in lists:
                    l[k]()

    def ph_kv(self):
        S, di = self.S, self.di
        nw = self.bcast_row("nw", di["kv_norm"].ap, D)
        wkv = self.load_w("wkv", di["w_kv"].ap, D, 512)
        h = [S.sb(f"h{i}", [128, D], F32) for i in range(2)]
        xn = [S.sb(f"xn{i}", [128, D], BF16) for i in range(2)]
        xT = [S.sb(f"xT{i}", [128, 8, 128], BF16) for i in range(2)]
        kv = [S.sb(f"kv{i}", [128, 512], F32) for i in range(2)]
        kvb = [S.sb(f"kvb{i}", [128, 512], BF16) for i in range(2)]
        ktb = [S.sb(f"ktb{i}", [64, 4, 128], BF16) for i in range(2)]
        scr = S.sb("scr", [128, D], F32)
        st = [S.sb(f"st{i}", [128, 4], F32) for i in range(2)]
        cb = [S.sb(f"cb{i}", [128, 256], BF16) for i in range(2)]
        psT = [S.ps(f"psT{i}", [128, D], BF16) for i in range(2)]
        pk = S.ps("pk", [128, 512])
        pkt = [S.ps(f"pkt{i}", [64, 512], BF16) for i in range(2)]
        for t in range(self.TT):
            j = t % 2
            rows = slice(t * 128, (t + 1) * 128)
            self.LD(h[j].ap, self.Hd[rows, :], [self.Hd], [h[j]])
            self.norm_tile(h[j], nw, xn[j], scr, st[j])
            self.transpose_cols(xn[j], D, xT[j].ap, xT[j], psT[j])
            for k in range(8):
                self.MM(pk.ap, xT[j][:, k, :], wkv[:, k, :], k == 0, k == 7, [xT[j], wkv], [pk])
            self.CP("act", kv[j].ap, pk.ap, [pk], [kv[j]])
            self.CP("dve", kvb[j].ap, pk.ap, [pk], [kvb[j]])
            self.LD(self.kvout[rows, :], kv[j].ap, [kv[j]], [self.kvout], q="pool")
            for hd in range(4):
                self.TR(pkt[j][:, hd * 128:(hd + 1) * 128], kvb[j][:, hd * 64:(hd + 1) * 64], self.identb.ap, [kvb[j], self.identb], [pkt[j]])
            self.CP("act", ktb[j].ap, pkt[j].ap.rearrange("p (h t) -> p h t", h=4), [pkt[j]], [ktb[j]])
            if t < self.NPT:
                self.LD(self.KTp[0:64, :, MAXW + t * 128:MAXW + (t + 1) * 128], ktb[j].ap, [ktb[j]], [self.KTp], q="pool")
                self.LD(self.VDp[MAXW + t * 128:MAXW + (t + 1) * 128, :], kvb[j][:, 256:512], [kvb[j]], [self.VDp], q="pool")
            else:
                b = t - self.NPT
                self.LD(self.KTs[b, 0:64, :, MAXW:MAXW + 8], ktb[j][:, :, 0:8], [ktb[j]], [self.KTs], q="pool")
                self.LD(self.VDs[b, MAXW:MAXW + 8, :], kvb[j][0:8, 256:512], [kvb[j]], [self.VDs], q="pool")
        cnt = 0
        for b in range(self.NS1):
            S.dma("pool", self.VDsm[b, 0:MAXW, :], di["cache"].ap[b, :, 256:512], [self.din_res], [self.VDsm])
            for r0 in range(0, MAXW, 128):
                j = cnt % 2
                cnt += 1
                S.dma("pool", cb[j].ap, di["cache"].ap[b, r0:r0 + 128, 0:256], [self.din_res], [cb[j]])
                for hd in range(4):
                    self.TR(pkt[j][:, hd * 128:(hd + 1) * 128], cb[j][:, hd * 64:(hd + 1) * 64], self.identb.ap, [cb[j], self.identb], [pkt[j]])
                self.CP("act" if j else "dve", ktb[j].ap, pkt[j].ap.rearrange("p (h t) -> p h t", h=4), [pkt[j]], [ktb[j]])
                self.LD(self.KTsm[b, 0:64, :, r0:r0 + 128], ktb[j].ap, [ktb[j]], [self.KTsm])

    def ph_select(self):
        S, di = self.S, self.di
        NH = self.NH
        sel = S.sb("sel", [128, 2], F32)
        self.LD(sel.ap, di["sel"].ap, [self.din_res], [sel])
        NB_ = 4 if NH % 4 == 0 else 1
        a = [S.sb(f"a{i}", [128, NB_, D], F32) for i in range(2)]
        b = [S.sb(f"b{i}", [128, NB_, D], F32) for i in range(2)]
        for ii, i in enumerate(range(0, NH, NB_)):
            j = ii % 2
            ra = self.Hd[i * 128:(i + NB_) * 128, :].rearrange("(n p) d -> p n d", p=128)
            rb = self.Hd[(NH + i) * 128:(NH + i + NB_) * 128, :].rearrange("(n p) d -> p n d", p=128)
            self.LD(a[j].ap, ra, [self.Hd], [a[j]])
            self.LD(b[j].ap, rb, [self.Hd], [b[j]])
            self.TS("pool", a[j].ap, a[j].ap, sel[:, 0:1], None, ALU.mult, None, [a[j], sel], [a[j]])
            self.STT(a[j].ap, b[j].ap, sel[:, 1:2], a[j].ap, ALU.mult, ALU.add, [b[j], sel, a[j]], [a[j]])
            self.LD(self.Hm[i * 128:(i + NB_) * 128, :].rearrange("(n p) d -> p n d", p=128), a[j].ap, [a[j]], [self.Hm])
        W = MAXW + NH * 128
        off = NH * 128
        ka = [S.sb(f"ka{i}", [65, W], BF16) for i in range(2)]
        kb = [S.sb(f"kb{i}", [65, W], BF16) for i in range(2)]
        for hd in range(4):
            j = hd % 2
            self.LD(ka[j].ap, self.KTp[:, hd, 0:W], [self.KTp], [ka[j]])
            self.LD(kb[j].ap, self.KTp[:, hd, off:off + W], [self.KTp], [kb[j]])
            self.TS("pool", ka[j].ap, ka[j].ap, sel[0:65, 0:1], None, ALU.mult, None, [ka[j], sel], [ka[j]])
            self.STT(ka[j].ap, kb[j].ap, sel[0:65, 1:2], ka[j].ap, ALU.mult, ALU.add, [kb[j], sel, ka[j]], [ka[j]])
            self.LD(self.KTm[:, hd, :], ka[j].ap, [ka[j]], [self.KTm])
        nvt = W // 128
        VB = 8 if nvt % 8 == 0 else 1
        va = [S.sb(f"va{i}", [128, VB, 256], BF16) for i in range(2)]
        vb_ = [S.sb(f"vb{i}", [128, VB, 256], BF16) for i in range(2)]
        for ii, i in enumerate(range(0, nvt, VB)):
            j = ii % 2
            self.LD(va[j].ap, self.VDp[i * 128:(i + VB) * 128, :].rearrange("(n p) d -> p n d", p=128), [self.VDp], [va[j]])
            self.LD(vb_[j].ap, self.VDp[off + i * 128:off + (i + VB) * 128, :].rearrange("(n p) d -> p n d", p=128), [self.VDp], [vb_[j]])
            self.TS("pool", va[j].ap, va[j].ap, sel[:, 0:1], None, ALU.mult, None, [va[j], sel], [va[j]])
            self.STT(va[j].ap, vb_[j].ap, sel[:, 1:2], va[j].ap, ALU.mult, ALU.add, [vb_[j], sel, va[j]], [va[j]])
            self.LD(self.VDm[i * 128:(i + VB) * 128, :].rearrange("(n p) d -> p n d", p=128), va[j].ap, [va[j]], [self.VDm])

    def ph_select_s(self):
        S, di = self.S, self.di
        NH = self.NH
        sel = S.sb("sel", [128, 2], F32)
        self.LD(sel.ap, di["sel"].ap, [self.din_res], [sel])
        a = [S.sb(f"a{i}", [128, D], F32) for i in range(2)]
        b = [S.sb(f"b{i}", [128, D], F32) for i in range(2)]
        NS1 = self.NS1
        for s_ in range(NS1):
            j = s_ % 2
            r0, r1 = (self.NPT + s_) * 128, (self.NPT + NS1 + s_) * 128
            self.LD(a[j].ap, self.Hd[r0:r0 + 128, :], [self.Hd], [a[j]])
            self.LD(b[j].ap, self.Hd[r1:r1 + 128, :], [self.Hd], [b[j]])
            self.TS("pool", a[j].ap, a[j].ap, sel[:, 0:1], None, ALU.mult, None, [a[j], sel], [a[j]])
            self.STT(a[j].ap, b[j].ap, sel[:, 1:2], a[j].ap, ALU.mult, ALU.add, [b[j], sel, a[j]], [a[j]])
            self.LD(self.Hm[(NH + s_) * 128:(NH + s_ + 1) * 128, :], a[j].ap, [a[j]], [self.Hm])
        ksa = [S.sb(f"ksa{i}", [64, 4, 8], BF16) for i in range(2)]
        ksb = [S.sb(f"ksb{i}", [64, 4, 8], BF16) for i in range(2)]
        vsa = [S.sb(f"vsa{i}", [8, 256], BF16) for i in range(2)]
        vsb = [S.sb(f"vsb{i}", [8, 256], BF16) for i in range(2)]
        for s_ in range(NS1):
            j = s_ % 2
            self.LD(ksa[j].ap, self.KTs[s_, 0:64, :, MAXW:MAXW + 8], [self.KTs], [ksa[j]])
            self.LD(ksb[j].ap, self.KTs[NS1 + s_, 0:64, :, MAXW:MAXW + 8], [self.KTs], [ksb[j]])
            self.TS("pool", ksa[j].ap, ksa[j].ap, sel[0:64, 0:1], None, ALU.mult, None, [ksa[j], sel], [ksa[j]])
            self.STT(ksa[j].ap, ksb[j].ap, sel[0:64, 1:2], ksa[j].ap, ALU.mult, ALU.add, [ksb[j], sel, ksa[j]], [ksa[j]])
            self.LD(self.KTsm[s_, 0:64, :, MAXW:MAXW + 8], ksa[j].ap, [ksa[j]], [self.KTsm])
            self.LD(vsa[j].ap, self.VDs[s_, MAXW:MAXW + 8, :], [self.VDs], [vsa[j]])
            self.LD(vsb[j].ap, self.VDs[NS1 + s_, MAXW:MAXW + 8, :], [self.VDs], [vsb[j]])
            self.TS("pool", vsa[j].ap, vsa[j].ap, sel[0:8, 0:1], None, ALU.mult, None, [vsa[j], sel], [vsa[j]])
            self.STT(vsa[j].ap, vsb[j].ap, sel[0:8, 1:2], vsa[j].ap, ALU.mult, ALU.add, [vsb[j], sel, vsa[j]], [vsa[j]])
            self.LD(self.VDsm[s_, MAXW:MAXW + 8, :], vsa[j].ap, [vsa[j]], [self.VDsm])

    def attn_cfgs(self):
        B = self.BLK
        c = {"p0": (1, 128, 1, 0), "p1": (4, 32, 4, 1), "p2": (16, B // 16, 16, 2),
             "s0": (1, 8, 1, 0), "s1": (4, 2, 4, 1), "s2": (16, 1, 8, 2)}
        return c

    def attn_group(self, name, QT, qcol0, KT, kpos0, Vsrc, vrow0, ACC, acol0, first, bufs):
        d, nq, nres, g = self.cfgs[name]
        bias = self.biasT[name]
        nk = nq + 128
        ktl = [(0, 128), (128, nk)]
        vt, pt_, tmpf, pS, pO = bufs
        vres = self.vres
        merged = 8 * nq <= 512
        W4 = 4 * nq
        units = []
        for rho in range(nres):
            shared = {}
            for kvh in range(4):
                st = {}

                def A(rho=rho, kvh=kvh, st=st, shared=shared):
                    if kvh == 0:
                        vts = []
                        for ki, (j0, j1) in enumerate(ktl):
                            vtile = vt[self.vcnt % len(vt)]
                            self.vcnt += 1
                            r0 = vrow0 + rho + d * (j0 - 128)
                            src = Vsrc[r0:r0 + d * (j1 - j0 - 1) + 1:d, :] if d > 1 else Vsrc[r0:r0 + (j1 - j0), :]
                            self.LD(vtile[0:j1 - j0, :], src, [vres], [vtile])
                            vts.append(vtile)
                        shared["vts"] = vts
                    q0 = qcol0 + rho
                    qs = QT[:, g, kvh * 4:(kvh + 1) * 4, q0:q0 + d * (nq - 1) + 1:d] if d > 1 else QT[:, g, kvh * 4:(kvh + 1) * 4, q0:q0 + nq]
                    pts = []
                    if merged:
                        c = self.acnt % len(pS)
                        self.acnt += 1
                        ps_, tf = pS[c], tmpf[c % len(tmpf)]
                        p_ = pt_[self.pcnt % len(pt_)]
                        self.pcnt += 1
                    for ki, (j0, j1) in enumerate(ktl):
                        nkk = j1 - j0
                        if not merged:
                            c = self.acnt % len(pS)
                            self.acnt += 1
                            ps_, tf = pS[c], tmpf[c % len(tmpf)]
                            p_ = pt_[self.pcnt % len(pt_)]
                            self.pcnt += 1
                        co = ki * W4 if merged else 0
                        k0 = kpos0 + rho + d * (j0 - 128)
                        kslice = KT[:, kvh, k0:k0 + d * (nkk - 1) + 1:d] if d > 1 else KT[:, kvh, k0:k0 + nkk]
                        out = ps_[0:nkk, co:co + W4].rearrange("p (h q) -> p h q", h=4)
                        self.MM(out, kslice, qs, True, True, [KT, QT], [ps_])
                        if not merged:
                            self.TT_("dve", tf[0:nkk, 0:W4], ps_[0:nkk, 0:W4], bias[0:nkk, kvh, ki, :], ALU.add, [ps_, bias], [tf])
                            self.ACT(p_[0:nkk, 0:W4], tf[0:nkk, 0:W4], AF.Exp, [tf], [p_])
                        pts.append((p_, nkk, co))
                    if merged:
                        self.TT_("dve", tf[:, 0:2 * W4], ps_[:, 0:2 * W4], bias[:, kvh, :, :].rearrange("p a c -> p (a c)"), ALU.add, [ps_, bias], [tf])
                        self.ACT(p_[:, 0:2 * W4], tf[:, 0:2 * W4], AF.Exp, [tf], [p_])
                    st["pts"] = pts

                def B(rho=rho, kvh=kvh, st=st, shared=shared):
                    pts, vts = st["pts"], shared["vts"]
                    if merged:
                        hf = self.ocnt % 2
                        self.ocnt += 1
                        po_ = pO[0][0:64, hf * 512:(hf + 1) * 512]
                        pres = [pO[0].part(hf)]
                    else:
                        po_ = pO[0][0:64, :]
                        pres = [pO[0].part(0), pO[0].part(1)]
                    for ki, (p_, nkk, co) in enumerate(pts):
                        self.MM(po_[:, 0:W4], vts[ki][0:nkk, kvh * 64:(kvh + 1) * 64], p_[0:nkk, co:co + W4], ki == 0, ki == 1, [vts[ki], p_], pres)
                    for ki, (p_, nkk, co) in enumerate(pts):
                        self.MM(po_[:, W4:2 * W4], self.ones_b[0:nkk, :], p_[0:nkk, co:co + W4], ki == 0, ki == 1, [self.ones_b, p_], pres)
                    a0 = acol0 + rho
                    dst = ACC[:, :, kvh * 4:(kvh + 1) * 4, a0:a0 + d * (nq - 1) + 1:d] if d > 1 else ACC[:, :, kvh * 4:(kvh + 1) * 4, a0:a0 + nq]
                    srcp = po_[:, 0:2 * W4].rearrange("p (a h q) -> p a h q", a=2, h=4)
                    if first:
                        self.CP("dve", dst, srcp, pres, [ACC])
                    else:
                        self.TT_("dve", dst, srcp, dst, ALU.add, pres + [ACC], [ACC])
                units.append((A, B))
        return units

    def run_units(self, units, L=3):
        n = len(units)
        for u in range(min(L, n)):
            units[u][0]()
        for u in range(n):
            if u + L < n:
                units[u + L][0]()
            units[u][1]()

    def ph_attn(self):
        S, di = self.S, self.di
        BLK = self.BLK
        nw = self.bcast_row("nw", di["norm_w"].ap[1, 1], D)
        wo = S.sb("wo", [64, H, D], BF16)
        S.dma("pool", wo.ap, di["attn_wo"].ap.rearrange("(h p) c -> p h c", p=64), [self.din_res], [wo])
        self.biasT = {}
        for name, (d, nq, nres, g) in self.cfgs.items():
            bt = S.sb(f"bias_{name}", [128, 4, 2, 4 * nq], F32)
            self.MSET("pool", bt.ap, 0.0, [bt])
            src = di[f"bias_{name}"].ap
            self.LD(bt[:, :, 0, :], src[:, 0:128, :].rearrange("k j c -> j k c"), [self.din_res], [bt])
            self.LD(bt[0:nq, :, 1, :], src[:, 128:128 + nq, :].rearrange("k j c -> j k c"), [self.din_res], [bt])
            self.biasT[name] = bt
        KT = S.sb("KT", [65, 4, MAXW + BLK], BF16)
        QT = S.sb("QT", [65, 3, H, BLK], BF16)
        self.MSET("pool", QT[64:65, :, :, :], 1.0, [QT])
        ACC = S.sb("ACC", [64, 2, H, BLK], F32)
        fin = S.sb("fin", [64, H, BLK], BF16)
        h = [S.sb(f"h{i}", [128, D], F32) for i in range(2)]
        xn = [S.sb(f"xn{i}", [128, D], BF16) for i in range(2)]
        xT = S.sb("xT", [128, 8, BLK], BF16)
        scr = S.sb("scr", [128, D], F32)
        st = [S.sb(f"st{i}", [128, 4], F32) for i in range(2)]
        vt = [S.sb(f"vt{i}", [128, 256], BF16) for i in range(8)]
        pt_ = [S.sb(f"pt{i}", [128, 512], BF16) for i in range(8)]
        tmpf = [S.sb(f"tf{i}", [128, 512], F32) for i in range(4)]
        psT1 = S.ps("psT", [128, D], BF16)
        psT = [psT1, psT1]
        pq1 = S.ps("pq", [64, 512])
        pq = [pq1, pq1]
        pS = [S.ps(f"pS{i}", [128, 512]) for i in range(4)]
        pO = [S.ps("pO0", [64, 1024])]
        bufs = (vt, pt_, tmpf, pS, pO)
        self.vcnt = self.acnt = self.pcnt = self.ocnt = 0
        wq = di["attn_wq"].ap
        wqb = [S.sb(f"wqb{i}", [128, 8, 512], BF16) for i in range(2)]

        def qproj(tiles, ncols):
            for i, t in enumerate(tiles):
                j = i % 2
                self.LD(h[j].ap, self.cH[t * 128:(t + 1) * 128, :], [self.cH], [h[j]])
                self.norm_tile(h[j], nw, xn[j], scr, st[j])
                self.transpose_cols(xn[j], D, xT[:, :, i * 128:(i + 1) * 128], xT, psT[j])
            qc = 0
            for cb_ in range(6):
                w = wqb[cb_ % 2]
                S.dma("pool", w.ap, wq[:, cb_ * 512:(cb_ + 1) * 512].rearrange("(k p) c -> p k c", p=128), [self.din_res], [w])
                for hh in range(8):
                    gh = cb_ * 8 + hh
                    g, hd = gh // 16, gh % 16
                    pz = pq[qc % 2]
                    qc += 1
                    for k in range(8):
                        self.MM(pz[:, 0:ncols], w[:, k, hh * 64:(hh + 1) * 64], xT[:, k, 0:ncols], k == 0, k == 7, [w, xT], [pz])
                    self.ACT(QT[0:64, g, hd, 0:ncols], pz[:, 0:ncols], AF.Copy, [pz], [QT], scale=0.125)

        def finish(tiles, ncols_valid):
            self.RCP(ACC[:, 1, :, 0:ncols_valid], ACC[:, 1, :, 0:ncols_valid], [ACC], [ACC])
            self.TT_("dve", fin[:, :, 0:ncols_valid], ACC[:, 0, :, 0:ncols_valid], ACC[:, 1, :, 0:ncols_valid], ALU.mult, [ACC], [fin])
            for i, t in enumerate(tiles):
                j = i % 2
                nv = min(128, ncols_valid - i * 128)
                self.LD(h[j].ap, self.cH[t * 128:(t + 1) * 128, :], [self.cH], [h[j]])
                for hh in range(2):
                    cs = slice(hh * 512, (hh + 1) * 512)
                    for hd in range(H):
                        self.MM(pS[hh][0:nv, :], fin[:, hd, i * 128:i * 128 + nv], wo[:, hd, cs], hd == 0, hd == H - 1, [fin, wo], [pS[hh]])
                    self.TT_("dve", h[j][0:nv, cs], pS[hh][0:nv, :], h[j][0:nv, cs], ALU.add, [pS[hh], h[j]], [h[j]])
                self.LD(self.cH[t * 128:(t + 1) * 128, :], h[j].ap, [h[j]], [self.cH], q="pool")

        self.vres = self.cVD
        nblk = (self.cNPT * 128) // BLK
        tpb = BLK // 128
        for bi in range(nblk):
            tiles = list(range(bi * tpb, (bi + 1) * tpb))
            qproj(tiles, BLK)
            base = bi * BLK
            for hd in range(4):
                self.LD(KT[:, hd, :], self.cKT[:, hd, base:base + MAXW + BLK], [self.cKT], [KT])
            units = []
            for si in range(tpb):
                units += self.attn_group("p0", QT, si * 128, KT, MAXW + si * 128, self.cVD.ap, MAXW + base + si * 128, ACC, si * 128, True, bufs)
            for si in range(tpb):
                units += self.attn_group("p1", QT, si * 128, KT, MAXW + si * 128, self.cVD.ap, MAXW + base + si * 128, ACC, si * 128, False, bufs)
            units += self.attn_group("p2", QT, 0, KT, MAXW, self.cVD.ap, MAXW + base, ACC, 0, False, bufs)
            self.run_units(units)
            finish(tiles, BLK)
        for b in range(self.NS1):
            t = self.cNPT + b
            kts = KT
            for hd in range(4):
                self.LD(kts[:, hd, 0:MAXW + 8], self.KTsm[b, :, hd, :], [self.KTsm], [kts])
            self.vres = self.VDsm
            qproj([t], 128)
            units = []
            for gi, nm in enumerate(("s0", "s1", "s2")):
                units += self.attn_group(nm, QT, 0, kts, MAXW, self.VDsm.ap[b], MAXW, ACC, 0, gi == 0, bufs)
            self.run_units(units)
            finish([t], 8)


def t5_buckets(dist):
    d = np.asarray(dist, dtype=np.int64)
    max_exact = 16
    large = max_exact + (np.log(np.maximum(d, 1) / max_exact) / np.log(2048 / max_exact) * (32 - max_exact)).astype(np.int32)
    large = np.minimum(large, 31)
    return np.where(d < max_exact, d, large).astype(np.int32)


def make_bias_tables(rel_bias, cfgs):
    out = {}
    for name, (d, nq, nres, g) in cfgs.items():
        nk = nq + 128
        j = np.arange(nk)[:, None]
        i = np.arange(nq)[None, :]
        m = i - j + 128
        valid = (m >= 0) & (m <= 128)
        bk = t5_buckets(d * np.clip(m, 0, 128))
        tab = np.empty((4, nk, 4, nq), np.float32)
        for kvh in range(4):
            for hq in range(4):
                col = g * 16 + kvh * 4 + hq
                tab[kvh, :, hq, :] = np.where(valid, rel_bias[bk, col], np.float32(NEG))
        out[name] = np.ascontiguousarray(tab.reshape(4, nk, 4 * nq))
    return out


def make_consts():
    s = np.arange(128)[:, None]
    t = np.arange(128)[None, :]
    c = {}
    c["c_ident"] = np.eye(128, dtype=np.float32)
    c["c_tri"] = np.concatenate([(s <= t), (s > t)], 1).astype(np.float32)
    c["c_m1"] = np.concatenate([(t < s), (s < t), (s <= t)], 1).astype(np.float32)
    c["c_m2"] = np.concatenate([(s < t), (s <= t)], 1).astype(np.float32)
    rm = np.zeros((128, 2), np.float32)
    rm[:, 0] = 1.0
    rm[:8, 1] = 1.0
    c["c_rowmask"] = rm
    return c


_CACHE = {}


def get_builder(NPT, dbg=False):
    key = (NPT, dbg)
    if key not in _CACHE:
        b = Builder(NPT, dbg)
        b.build()
        _CACHE[key] = b
    return _CACHE[key]


def core_inputs(b, c, half, x_prompt_seq, x_sample, state_wkv, state_shift, cache_kv, p_prompt_seq, p_sample, weights, consts, bias_tabs):
    NTOK, NP = b.NTOK, b.NP
    xin = np.zeros((NTOK, D), np.float32)
    xin[:NP] = x_prompt_seq
    pin = np.zeros((2, NTOK, 256), np.float32)
    pin[:, :NP] = p_prompt_seq
    for s in range(NSB):
        r0 = NP + s * 128
        xin[r0:r0 + 8] = x_sample[s]
        pin[:, r0:r0 + 8] = p_sample[:, s]
    NHT = b.NH * 128
    pin1 = np.zeros((b.MTOK, 256), np.float32)
    pin1[:NHT] = p_prompt_seq[1, half * NHT:(half + 1) * NHT]
    for s in range(b.NS1):
        pin1[NHT + s * 128:NHT + s * 128 + 8] = p_sample[1, half * b.NS1 + s]
    sel = np.zeros((128, 2), np.float32)
    sel[:, half] = 1.0
    m = {"sel": sel, "pin1": pin1, "xin": xin, "pin": pin, "wkv0": np.ascontiguousarray(state_wkv), "shift0": np.ascontiguousarray(state_shift),
         "cache": np.ascontiguousarray(cache_kv.reshape(NSB, MAXW, 512)[half * b.NS1:(half + 1) * b.NS1])}
    m.update(weights)
    m.update(consts)
    for name, tab in bias_tabs.items():
        m[f"bias_{name}"] = tab
    return m


def kernel(x_prompt, x_sample, state_wkv, state_shift, cache_kv, p_prompt, p_sample,
           norm_w, ffn1_wi, ffn1_wo, ffn2_wi, ffn2_wo, pe_proj, pe_gate,
           rwkv_mix, rwkv_wrkv, rwkv_wo, rwkv_w0, rwkv_w1, rwkv_w2, rwkv_a0, rwkv_a1, rwkv_a2,
           rwkv_g1, rwkv_g2, rwkv_kk, rwkv_ka, rwkv_rk, rwkv_lnx_w, rwkv_lnx_b,
           attn_wq, attn_wo, kv_norm, w_kv, rel_bias, final_norm, _dbg=False):
    f = lambda a: np.ascontiguousarray(np.asarray(a, dtype=np.float32))
    x_prompt, x_sample, p_prompt, p_sample = f(x_prompt), f(x_sample), f(p_prompt), f(p_sample)
    state_wkv, state_shift, cache_kv = f(state_wkv), f(state_shift), f(cache_kv)
    B, T, _ = x_prompt.shape
    NPT = T // 128
    b = get_builder(NPT, _dbg)
    weights = {"norm_w": f(norm_w), "ffn1_wi": f(ffn1_wi), "ffn1_wo": f(ffn1_wo), "ffn2_wi": f(ffn2_wi), "ffn2_wo": f(ffn2_wo),
               "pe_proj": f(pe_proj), "pe_gate": f(pe_gate), "rwkv_mix": f(rwkv_mix)[0], "rwkv_wrkv": f(rwkv_wrkv)[0],
               "rwkv_wo": f(rwkv_wo)[0], "rwkv_w0": f(rwkv_w0)[0], "rwkv_w1": f(rwkv_w1)[0], "rwkv_w2": f(rwkv_w2)[0],
               "rwkv_a0": f(rwkv_a0)[0], "rwkv_a1": f(rwkv_a1)[0], "rwkv_a2": f(rwkv_a2)[0], "rwkv_g1": f(rwkv_g1)[0],
               "rwkv_g2": f(rwkv_g2)[0], "rwkv_kk": f(rwkv_kk)[0], "rwkv_ka": f(rwkv_ka)[0], "rwkv_rk": f(rwkv_rk)[0].reshape(-1),
               "rwkv_lnx_w": f(rwkv_lnx_w)[0], "rwkv_lnx_b": f(rwkv_lnx_b)[0], "attn_wq": f(attn_wq)[0], "attn_wo": f(attn_wo)[0],
               "kv_norm": f(kv_norm), "w_kv": f(w_kv), "final_norm": f(final_norm)}
    consts = make_consts()
    bias_tabs = make_bias_tables(f(rel_bias), b.cfgs)
    nsb_total = x_sample.shape[0]
    ngrp = nsb_total // NSB
    in_maps = []
    for c in range(8):
        pb = c % B
        sg = c % ngrp
        sl = slice(sg * NSB, (sg + 1) * NSB)
        in_maps.append(core_inputs(b, c, c // B, x_prompt[pb], x_sample[sl], state_wkv[0, sl], state_shift[0, sl], cache_kv[sl],
                                   p_prompt[:, pb], p_sample[:, sl], weights, consts, bias_tabs))
    res = run_bass_kernel_spmd(b.nc, in_maps, core_ids=list(range(8)))
    R = res.results
    NP = b.NP
    NHT = b.NH * 128
    y_prompt = np.stack([np.concatenate([R[c]["yout"][:NHT], R[c + B]["yout"][:NHT]], 0) for c in range(B)])
    kvw = min(MAXW, T)
    kv_prompt = np.stack([R[c]["kvout"][NP - kvw:NP].reshape(kvw, 2, 4, 64) for c in range(B)])
    wkv_prompt = np.stack([R[c]["wkvp"] for c in range(B)])[None]
    shift_prompt = np.stack([R[c]["shiftall"][NP - 1] for c in range(B)])[None]
    ys, kvs, wks, shs = [], [], [], []
    for sg in range(ngrp):
        r = R[sg]
        for s in range(NSB):
            r0 = NP + s * 128
            rr = R[sg + (s // b.NS1) * B]
            ys.append(rr["yout"][NHT + (s % b.NS1) * 128:NHT + (s % b.NS1) * 128 + 8])
            kvs.append(r["kvout"][r0:r0 + 8].reshape(8, 2, 4, 64))
            shs.append(r["shiftall"][r0 + 7])
        wks.append(r["wkvs"])
    y_sample = np.stack(ys)
    kv_sample = np.stack(kvs)
    wkv_sample = np.concatenate(wks, 0)[None]
    shift_sample = np.stack(shs)[None]
    outs = (y_prompt, y_sample, wkv_prompt, shift_prompt, kv_prompt, wkv_sample, shift_sample, kv_sample)
    outs = tuple(np.ascontiguousarray(o, dtype=np.float32) for o in outs)
    if _dbg:
        return outs, R
    return outs
```
